# Optimizing a Trainium2 kernel written in Bass

```python
import math
import jax
import jax.numpy as jnp
from jax import lax
import numpy as np

D_MODEL = 1024
BATCH = 16
SEQ = 2048
DEPTH = 2
DEC_BATCH = 8
DEC_SEQ = 4096
PAST_LEN = 128

BRANCH_W = D_MODEL // 2
N_BRANCH = 3
D_FF = 2816
RMS_EPS = 1e-6

S5_H = 16
S5_G = BRANCH_W // S5_H
S5_P = 64
S5_DT_MIN = 1e-3
S5_DT_MAX = 1e-1
S5_MAX_RE = -1e-4

HEAD_DIM = 64
ATT_HQ = BRANCH_W // HEAD_DIM
ATT_HKV = 2
ATT_GROUP = ATT_HQ // ATT_HKV
WINDOW = 128
ATT_BLOCK = 128
ROPE_DIM = HEAD_DIM // 4
ROPE_THETA = 500000.0
NEG_INF = -1e30

RW_N = 64
RW_H = BRANCH_W // RW_N
RW_RANK_W = 64
RW_RANK_A = 64
RW_RANK_G = 128
RW_LN_EPS = 64e-5
N_RW_IN = 3 * BRANCH_W + 2 * RW_RANK_W + 2 * RW_RANK_A + RW_RANK_G

N_IN = BRANCH_W + ATT_HQ * HEAD_DIM + 2 * ATT_HKV * HEAD_DIM + N_RW_IN + N_BRANCH * D_MODEL

kernel_name = 'hybrid_bidir_s5_swa_rwkv7_encoder'


def rms_norm(x, gain):
    xf = x.astype(jnp.float32)
    y = xf * lax.rsqrt(jnp.mean(xf * xf, axis=-1, keepdims=True) + RMS_EPS)
    return (y * gain.astype(jnp.float32)).astype(x.dtype)


def swiglu(h, w_gate, w_up, w_down):
    return (jax.nn.silu(h @ w_gate) * (h @ w_up)) @ w_down


def rope_tables(seq_len):
    inv_freq = ROPE_THETA ** (-jnp.arange(0, ROPE_DIM, 2, dtype=jnp.float32) / ROPE_DIM)
    ang = jnp.arange(seq_len, dtype=jnp.float32)[:, None] * inv_freq[None, :]
    return jnp.cos(ang), jnp.sin(ang)


def partial_rope(t, cos, sin):
    half = ROPE_DIM // 2
    tf = t.astype(jnp.float32)
    t1, t2 = tf[..., :half], tf[..., half:ROPE_DIM]
    c, s = cos[None, :, None, :], sin[None, :, None, :]
    rot = jnp.concatenate([t1 * c - t2 * s, t2 * c + t1 * s], axis=-1)
    return jnp.concatenate([rot, tf[..., ROPE_DIM:]], axis=-1).astype(t.dtype)


def _linear_recurrence(e1, e2):
    a1, b1 = e1
    a2, b2 = e2
    return a1 * a2, a2 * b1 + b2


def s5_branch(u, lam_re, lam_im, log_step, b_re, b_im, c_re, c_im, d_skip, w_glu):
    bsz, seq_len, _ = u.shape
    f32 = jnp.float32
    uf = u.astype(f32).reshape(bsz, seq_len, S5_G, S5_H)
    y = d_skip.astype(f32) * uf
    for d in range(2):
        lam = lax.complex(jnp.minimum(lam_re[d].astype(f32), S5_MAX_RE), lam_im[d].astype(f32))
        dt = jnp.exp(log_step[d].astype(f32))[:, None]
        a_bar = jnp.exp(lam * dt)
        b_bar = ((a_bar - 1.0) / lam)[..., None] * lax.complex(b_re[d].astype(f32), b_im[d].astype(f32))
        bu = jnp.einsum('blgh,gph->blgp', uf, b_bar)
        a_seq = jnp.broadcast_to(a_bar, (1, seq_len, S5_G, S5_P))
        _, states = lax.associative_scan(_linear_recurrence, (a_seq, bu), reverse=(d == 1), axis=1)
        c = lax.complex(c_re[d].astype(f32), c_im[d].astype(f32))
        y = y + jnp.real(jnp.einsum('blgp,ghp->blgh', states, c))
    y = jax.nn.gelu(y.reshape(bsz, seq_len, BRANCH_W)).astype(u.dtype)
    return y * jax.nn.sigmoid(y @ w_glu)


def band_blocks(t):
    bsz, seq_len, nh, hd = t.shape
    nb = seq_len // ATT_BLOCK
    tp = jnp.pad(t, ((0, 0), (ATT_BLOCK, ATT_BLOCK), (0, 0), (0, 0)))
    tp = tp.reshape(bsz, nb + 2, ATT_BLOCK, nh, hd)
    return jnp.concatenate([tp[:, :-2], tp[:, 1:-1], tp[:, 2:]], axis=2)


def window_attention(q, k, v, q_gain, k_gain, sink, cos, sin):
    bsz, seq_len = q.shape[:2]
    nb = seq_len // ATT_BLOCK
    q = partial_rope(rms_norm(q, q_gain), cos, sin)
    k = partial_rope(rms_norm(k, k_gain), cos, sin)
    qb = q.reshape(bsz, nb, ATT_BLOCK, ATT_HKV, ATT_GROUP, HEAD_DIM)
    kb, vb = band_blocks(k), band_blocks(v)
    s = jnp.einsum('bnqkgd,bnskd->bnkgqs', qb, kb).astype(jnp.float32) * (HEAD_DIM ** -0.5)
    blk = jnp.arange(nb)[:, None, None] * ATT_BLOCK
    qpos = blk + jnp.arange(ATT_BLOCK)[None, :, None]
    kpos = blk - ATT_BLOCK + jnp.arange(3 * ATT_BLOCK)[None, None, :]
    mask = (jnp.abs(kpos - qpos) <= WINDOW) & (kpos >= 0) & (kpos < seq_len)
    s = jnp.where(mask[None, :, None, None], s, NEG_INF)
    sink_col = jnp.broadcast_to(sink.astype(jnp.float32).reshape(1, 1, ATT_HKV, ATT_GROUP, 1, 1),
                                s.shape[:-1] + (1,))
    p = jax.nn.softmax(jnp.concatenate([s, sink_col], axis=-1), axis=-1)[..., :-1]
    o = jnp.einsum('bnkgqs,bnskd->bnqkgd', p.astype(v.dtype), vb)
    return o.reshape(bsz, seq_len, ATT_HQ * HEAD_DIM)


def token_shift_centred(z, mu):
    prev = jnp.pad(z[:, :-1], ((0, 0), (1, 0), (0, 0)))
    nxt = jnp.pad(z[:, 1:], ((0, 0), (0, 1), (0, 0)))
    return z + mu[0] * (prev - z) + mu[1] * (nxt - z)


def wkv7_scan(r, w, k, v, kk, a, reverse):
    bsz = r.shape[0]
    xs = tuple(jnp.moveaxis(t, 1, 0) for t in (r, w, k, v, kk, a))

    def step(state, inp):
        r_t, w_t, k_t, v_t, kk_t, a_t = inp
        sa = jnp.einsum('bhij,bhj->bhi', state, -kk_t)
        state = (state * w_t[:, :, None, :]
                 + sa[..., None] * (kk_t * a_t)[:, :, None, :]
                 + v_t[..., None] * k_t[:, :, None, :])
        return state, jnp.einsum('bhij,bhj->bhi', state, r_t)

    state0 = jnp.zeros((bsz, RW_H, RW_N, RW_N), jnp.float32)
    _, ys = lax.scan(step, state0, xs, reverse=reverse)
    return jnp.moveaxis(ys, 0, 1)


def rwkv7_branch(z, mu, w0, w2, a0, a2, g2, k_k, k_a, r_k, ln_w, ln_b):
    bsz, seq_len, _ = z.shape
    f32 = jnp.float32
    z = token_shift_centred(z, mu)
    idx = np.cumsum([BRANCH_W, BRANCH_W, BRANCH_W, RW_RANK_W, RW_RANK_W, RW_RANK_A, RW_RANK_A]).tolist()
    r, k, v, wl_f, wl_b, al_f, al_b, gl = jnp.split(z, idx, axis=-1)

    def heads(t):
        return t.astype(f32).reshape(bsz, seq_len, RW_H, RW_N)

    rh, kh, vh = heads(r), heads(k), heads(v)
    kk = kh * k_k.astype(f32).reshape(RW_H, RW_N)
    kk = kk / jnp.maximum(jnp.sqrt(jnp.sum(kk * kk, axis=-1, keepdims=True)), 1e-12)
    k_a_h = k_a.astype(f32).reshape(RW_H, RW_N)
    outs = []
    for d, (wl, al) in enumerate(((wl_f, al_f), (wl_b, al_b))):
        w_pre = (w0[d] + jnp.tanh(wl) @ w2[d]).astype(f32)
        decay = heads(jnp.exp(-jnp.exp(-jax.nn.softplus(-w_pre) - 0.5)))
        a = heads(jax.nn.sigmoid((a0[d] + al @ a2[d]).astype(f32)))
        kd = kh * (1.0 + (a - 1.0) * k_a_h)
        outs.append(wkv7_scan(rh, decay, kd, vh, kk, a, reverse=(d == 1)))
    y = outs[0] + outs[1]
    mean = jnp.mean(y, axis=-1, keepdims=True)
    var = jnp.mean(jnp.square(y - mean), axis=-1, keepdims=True)
    y = ((y - mean) * lax.rsqrt(var + RW_LN_EPS)).reshape(bsz, seq_len, BRANCH_W)
    y = y * ln_w.astype(f32) + ln_b.astype(f32)
    bonus = jnp.sum(rh * kh * r_k.astype(f32), axis=-1, keepdims=True) * vh
    y = y + bonus.reshape(bsz, seq_len, BRANCH_W)
    g = jax.nn.sigmoid(gl) @ g2
    return (y * g.astype(f32)).astype(z.dtype)


def encoder_layer(x, p, cos, sin):
    bsz, seq_len, _ = x.shape
    x = x + 0.5 * swiglu(rms_norm(x, p['ffn1_norm']), p['ffn1_w_gate'], p['ffn1_w_up'], p['ffn1_w_down'])
    h = rms_norm(x, p['mix_norm'])
    idx = np.cumsum([BRANCH_W, ATT_HQ * HEAD_DIM, ATT_HKV * HEAD_DIM, ATT_HKV * HEAD_DIM, N_RW_IN]).tolist()
    u_s5, q, k, v, z_rw, gate_logits = jnp.split(h @ p['w_in'], idx, axis=-1)
    y_a = s5_branch(u_s5, p['s5_lam_re'], p['s5_lam_im'], p['s5_log_step'], p['s5_b_re'], p['s5_b_im'],
                    p['s5_c_re'], p['s5_c_im'], p['s5_d'], p['s5_w_glu'])
    y_b = window_attention(q.reshape(bsz, seq_len, ATT_HQ, HEAD_DIM),
                           k.reshape(bsz, seq_len, ATT_HKV, HEAD_DIM),
                           v.reshape(bsz, seq_len, ATT_HKV, HEAD_DIM),
                           p['q_norm'], p['k_norm'], p['attn_sink'], cos, sin)
    y_c = rwkv7_branch(z_rw, p['rw_mu'], p['rw_w0'], p['rw_w2'], p['rw_a0'], p['rw_a2'], p['rw_g2'],
                       p['rw_k_k'], p['rw_k_a'], p['rw_r_k'], p['rw_ln_w'], p['rw_ln_b'])
    up = jnp.einsum('blnw,nwd->blnd', jnp.stack([y_a, y_b, y_c], axis=2), p['w_branch'])
    gates = jax.nn.sigmoid(gate_logits.reshape(bsz, seq_len, N_BRANCH, D_MODEL))
    x = x + jnp.sum(gates * up, axis=2) @ p['w_out']
    x = x + 0.5 * swiglu(rms_norm(x, p['ffn2_norm']), p['ffn2_w_gate'], p['ffn2_w_up'], p['ffn2_w_down'])
    return x


def encoder_trunk(x, weights):
    cos, sin = rope_tables(x.shape[1])
    for i in range(DEPTH):
        x = encoder_layer(x, {name: w[i] for name, w in weights.items()}, cos, sin)
    return x


def setup_inputs(seed: int = 0) -> dict:
    key = jax.random.key(seed)
    ks = jax.random.split(key, 40)
    f32 = jnp.float32

    def nrm(k, shape, scale):
        return scale * jax.random.normal(k, shape, f32)

    W = BRANCH_W
    n_idx = jnp.arange(S5_P, dtype=f32)
    return {
        'x_prompt': nrm(ks[0], (BATCH, SEQ, D_MODEL), 1.0),
        'x_sample': nrm(ks[1], (DEC_BATCH, DEC_SEQ, D_MODEL), 1.0),
        'ffn1_norm': 1.0 + nrm(ks[2], (DEPTH, D_MODEL), 0.02),
        'ffn1_w_gate': nrm(ks[3], (DEPTH, D_MODEL, D_FF), D_MODEL ** -0.5),
        'ffn1_w_up': nrm(ks[4], (DEPTH, D_MODEL, D_FF), D_MODEL ** -0.5),
        'ffn1_w_down': nrm(ks[5], (DEPTH, D_FF, D_MODEL), D_FF ** -0.5),
        'mix_norm': 1.0 + nrm(ks[6], (DEPTH, D_MODEL), 0.02),
        'w_in': nrm(ks[7], (DEPTH, D_MODEL, N_IN), D_MODEL ** -0.5),
        's5_lam_re': -0.5 + nrm(ks[8], (DEPTH, 2, S5_G, S5_P), 0.01),
        's5_lam_im': math.pi * n_idx + nrm(ks[9], (DEPTH, 2, S5_G, S5_P), 0.01),
        's5_log_step': jax.random.uniform(ks[10], (DEPTH, 2, S5_G), f32,
                                          math.log(S5_DT_MIN), math.log(S5_DT_MAX)),
        's5_b_re': nrm(ks[11], (DEPTH, 2, S5_G, S5_P, S5_H), (2 * S5_H) ** -0.5),
        's5_b_im': nrm(ks[12], (DEPTH, 2, S5_G, S5_P, S5_H), (2 * S5_H) ** -0.5),
        's5_c_re': nrm(ks[13], (DEPTH, 2, S5_G, S5_H, S5_P), S5_P ** -0.5),
        's5_c_im': nrm(ks[14], (DEPTH, 2, S5_G, S5_H, S5_P), S5_P ** -0.5),
        's5_d': nrm(ks[15], (DEPTH, S5_G, S5_H), 1.0),
        's5_w_glu': nrm(ks[16], (DEPTH, W, W), W ** -0.5),
        'q_norm': 1.0 + nrm(ks[17], (DEPTH, HEAD_DIM), 0.02),
        'k_norm': 1.0 + nrm(ks[18], (DEPTH, HEAD_DIM), 0.02),
        'attn_sink': nrm(ks[19], (DEPTH, ATT_HQ), 0.5),
        'rw_mu': jax.random.uniform(ks[20], (DEPTH, 2, N_RW_IN), f32, 0.0, 0.5),
        'rw_w0': jax.random.uniform(ks[21], (DEPTH, 2, W), f32, -5.0, -0.5),
        'rw_w2': nrm(ks[22], (DEPTH, 2, RW_RANK_W, W), 0.5 * RW_RANK_W ** -0.5),
        'rw_a0': nrm(ks[23], (DEPTH, 2, W), 0.5),
        'rw_a2': nrm(ks[24], (DEPTH, 2, RW_RANK_A, W), RW_RANK_A ** -0.5),
        'rw_g2': nrm(ks[25], (DEPTH, RW_RANK_G, W), RW_RANK_G ** -0.5),
        'rw_k_k': 0.85 + nrm(ks[26], (DEPTH, W), 0.02),
        'rw_k_a': 1.0 + nrm(ks[27], (DEPTH, W), 0.02),
        'rw_r_k': nrm(ks[28], (DEPTH, RW_H, RW_N), 0.1),
        'rw_ln_w': 1.0 + nrm(ks[29], (DEPTH, W), 0.02),
        'rw_ln_b': nrm(ks[30], (DEPTH, W), 0.02),
        'w_branch': nrm(ks[31], (DEPTH, N_BRANCH, W, D_MODEL), W ** -0.5),
        'w_out': nrm(ks[32], (DEPTH, D_MODEL, D_MODEL), D_MODEL ** -0.5),
        'ffn2_norm': 1.0 + nrm(ks[33], (DEPTH, D_MODEL), 0.02),
        'ffn2_w_gate': nrm(ks[34], (DEPTH, D_MODEL, D_FF), D_MODEL ** -0.5),
        'ffn2_w_up': nrm(ks[35], (DEPTH, D_MODEL, D_FF), D_MODEL ** -0.5),
        'ffn2_w_down': nrm(ks[36], (DEPTH, D_FF, D_MODEL), D_FF ** -0.5),
    }


def reference(x_prompt, x_sample, ffn1_norm, ffn1_w_gate, ffn1_w_up, ffn1_w_down, mix_norm, w_in,
              s5_lam_re, s5_lam_im, s5_log_step, s5_b_re, s5_b_im, s5_c_re, s5_c_im, s5_d, s5_w_glu,
              q_norm, k_norm, attn_sink, rw_mu, rw_w0, rw_w2, rw_a0, rw_a2, rw_g2, rw_k_k, rw_k_a,
              rw_r_k, rw_ln_w, rw_ln_b, w_branch, w_out, ffn2_norm, ffn2_w_gate, ffn2_w_up, ffn2_w_down):
    weights = {
        'ffn1_norm': ffn1_norm, 'ffn1_w_gate': ffn1_w_gate, 'ffn1_w_up': ffn1_w_up,
        'ffn1_w_down': ffn1_w_down, 'mix_norm': mix_norm, 'w_in': w_in,
        's5_lam_re': s5_lam_re, 's5_lam_im': s5_lam_im, 's5_log_step': s5_log_step,
        's5_b_re': s5_b_re, 's5_b_im': s5_b_im, 's5_c_re': s5_c_re, 's5_c_im': s5_c_im,
        's5_d': s5_d, 's5_w_glu': s5_w_glu, 'q_norm': q_norm, 'k_norm': k_norm,
        'attn_sink': attn_sink, 'rw_mu': rw_mu, 'rw_w0': rw_w0, 'rw_w2': rw_w2, 'rw_a0': rw_a0,
        'rw_a2': rw_a2, 'rw_g2': rw_g2, 'rw_k_k': rw_k_k, 'rw_k_a': rw_k_a, 'rw_r_k': rw_r_k,
        'rw_ln_w': rw_ln_w, 'rw_ln_b': rw_ln_b, 'w_branch': w_branch, 'w_out': w_out,
        'ffn2_norm': ffn2_norm, 'ffn2_w_gate': ffn2_w_gate, 'ffn2_w_up': ffn2_w_up,
        'ffn2_w_down': ffn2_w_down,
    }
    y_prompt = encoder_trunk(x_prompt, weights)
    y_sample = encoder_trunk(x_sample, weights)
    return (y_prompt, y_sample)
```

```python
import contextlib
import math
import numpy as np
import concourse.bass as bass
import concourse.mybir as mybir
from concourse.bass_utils import run_bass_kernel_spmd

F32 = mybir.dt.float32
BF16 = mybir.dt.bfloat16
I32 = mybir.dt.int32
AF = mybir.ActivationFunctionType
ALU = mybir.AluOpType
AX = mybir.AxisListType

ENGS = ("pe", "dve", "act", "pool", "sp")
N_DMA_SEM = 8
RW_WINDOW = 3
SELF_SYNC = True

D = 1024
DFF = 2816
NFF = DFF // 128
W = 512
NIN = 6272
DEPTH = 2
RMS_EPS = 1e-6
TT = 512
C_U, C_Q, C_K, C_V, C_RW, C_G = 0, 512, 1024, 1152, 1280, 3200
N_RW = 1920
ROPE_THETA = 500000.0


class V:
    __slots__ = ("buf", "ap")

    def __init__(self, buf, ap):
        self.buf = buf
        self.ap = ap

    def __getitem__(self, idx):
        return V(self.buf, self.ap[idx])

    def re(self, pat, **kw):
        return V(self.buf, self.ap.rearrange(pat, **kw))

    def bc(self, shape):
        return V(self.buf, self.ap.broadcast_to(list(shape)))

    def pbc(self, n):
        return V(self.buf, self.ap.partition_broadcast(n))


class Buf:
    __slots__ = ("h", "name", "last_w", "readers")

    def __init__(self, h, name):
        self.h = h
        self.name = name
        self.last_w = None
        self.readers = []

    def __getitem__(self, idx):
        return V(self, self.h.ap()[idx])

    def v(self):
        return V(self, self.h.ap())

    def sub(self, name=None):
        return Buf(self.h, name or self.name + "_sub")


class Fw:
    def __init__(self, nc):
        self.nc = nc
        self.sem = {e: nc.alloc_semaphore("s_" + e) for e in ENGS}
        self.cnt = {e: 0 for e in ENGS}
        self.seen = {e: {} for e in ENGS}
        self.ops = {e: [] for e in ENGS}
        self.semobj = {("e", e): self.sem[e] for e in ENGS}
        self.dcnt = {}
        for q in ENGS:
            for i in range(N_DMA_SEM):
                self.semobj[("d", q, i)] = nc.alloc_semaphore(f"d_{q}_{i}")
            self.dcnt[q] = 0
        self.nbuf = 0
        self.n_ops = 0
        self.stack = None

    def sbuf(self, shape, dt, name=None):
        self.nbuf += 1
        name = f"{name or 'sb'}_{self.nbuf}"
        if self.stack is not None:
            h = self.stack.enter_context(self.nc.sbuf_tensor(name, list(shape), dt))
        else:
            h = self.nc.alloc_sbuf_tensor(name, list(shape), dt)
        return Buf(h, name)

    def psum(self, shape, dt, name=None):
        self.nbuf += 1
        name = name or f"ps{self.nbuf}"
        return Buf(self.nc.alloc_psum_tensor(name, list(shape), dt), name)

    def dram(self, shape, dt, name, kind="Internal"):
        return Buf(self.nc.dram_tensor(name, list(shape), dt, kind=kind), name)

    @contextlib.contextmanager
    def phase(self):
        old = self.stack
        with contextlib.ExitStack() as st:
            self.stack = st
            yield
            self.barrier()
        self.stack = old

    def _waits(self, eng, reads, writes, skip_self=True):
        need = {}
        seen = self.seen[eng]

        def add(ev):
            if ev is None:
                return
            k, v = ev
            if skip_self and k == ("e", eng) and (eng == "pe" or not SELF_SYNC):
                return
            if seen.get(k, 0) >= v:
                return
            if need.get(k, 0) < v:
                need[k] = v
        for b in reads:
            add(b.last_w)
        for b in writes:
            add(b.last_w)
            for r in b.readers:
                add(r)
        for k, v in need.items():
            seen[k] = v
        return list(need.items())

    def _commit(self, ev, reads, writes):
        for b in reads:
            rd = b.readers
            rd.append(ev)
            if len(rd) > 48:
                d = {}
                for k, v in rd:
                    if d.get(k, 0) < v:
                        d[k] = v
                b.readers = list(d.items())
        for b in writes:
            b.last_w = ev
            b.readers = []

    def op(self, eng, fn, reads=(), writes=(), inc=True):
        waits = self._waits(eng, reads, writes)
        if inc:
            self.cnt[eng] += 1
            val = self.cnt[eng]
        else:
            val = self.cnt[eng] + 1
        ev = (("e", eng), val)
        self.ops[eng].append((waits, fn, ("e", eng) if inc else None))
        self._commit(ev, reads, writes)
        self.n_ops += 1
        return ev

    def dma(self, q, out, in_, extra_r=(), extra_w=(), **kw):
        i = self.dcnt[q]
        self.dcnt[q] += 1
        slot = i % N_DMA_SEM
        key = ("d", q, slot)
        val = 16 * (i // N_DMA_SEM + 1)
        reads, writes = [in_.buf] + list(extra_r), [out.buf] + list(extra_w)
        waits = self._waits(q, reads, writes, skip_self=False)
        prev = val - 16
        if prev > 0 and self.seen[q].get(key, 0) < prev:
            waits.append((key, prev))
            self.seen[q][key] = prev
        oap, iap = out.ap, in_.ap

        def fn(e):
            return e.dma_start(out=oap, in_=iap, **kw)
        self.ops[q].append((waits, fn, key))
        ev = (key, val)
        self._commit(ev, reads, writes)
        self.n_ops += 1
        return ev

    def barrier(self):
        evs = []
        for e in ENGS:
            if self.cnt[e] > 0:
                evs.append((("e", e), self.cnt[e]))
        for q in ENGS:
            n = self.dcnt[q]
            for slot in range(N_DMA_SEM):
                if n > slot:
                    cntslot = (n - 1 - slot) // N_DMA_SEM + 1
                    evs.append((("d", q, slot), 16 * cntslot))
        for e in ENGS:
            waits = []
            for k, v in evs:
                if k == ("e", e):
                    continue
                if self.seen[e].get(k, 0) < v:
                    waits.append((k, v))
                    self.seen[e][k] = v
            if waits:
                self.ops[e].append((waits, None, None))

    def emit(self):
        nc = self.nc
        with nc.Block() as block:
            def mk(e):
                def body(engh):
                    for waits, fn, inckey in self.ops[e]:
                        for k, v in waits:
                            engh.wait_ge(self.semobj[k], v)
                        if fn is None:
                            continue
                        ins = fn(engh)
                        if inckey is not None:
                            ins.then_inc(self.semobj[inckey], 1 if inckey[0] == "e" else 16)
                return body
            block.tensor(mk("pe"))
            block.vector(mk("dve"))
            block.scalar(mk("act"))
            block.gpsimd(mk("pool"))
            block.sync(mk("sp"))

    def matmul(self, out, lhsT, rhs, start=True, stop=True, inc=None):
        if inc is None:
            inc = stop
        o, l, r = out.ap, lhsT.ap, rhs.ap
        return self.op("pe", lambda e: e.matmul(o, lhsT=l, rhs=r, start=start, stop=stop),
                       reads=[lhsT.buf, rhs.buf], writes=[out.buf], inc=inc)

    def transpose(self, out, in_, ident, inc=True):
        o, i, d = out.ap, in_.ap, ident.ap
        return self.op("pe", lambda e: e.transpose(o, i, d), reads=[in_.buf, ident.buf],
                       writes=[out.buf], inc=inc)

    def act(self, out, in_, func, bias=None, scale=None, accum_out=None, eng="act"):
        o, i = out.ap, in_.ap
        reads = [in_.buf]
        writes = [out.buf]
        kw = {}
        if bias is not None:
            if isinstance(bias, V):
                kw["bias"] = bias.ap
                reads.append(bias.buf)
            else:
                kw["bias"] = bias
        if scale is not None:
            if isinstance(scale, V):
                kw["scale"] = scale.ap
                reads.append(scale.buf)
            else:
                kw["scale"] = scale
        if accum_out is not None:
            kw["accum_out"] = accum_out.ap
            writes.append(accum_out.buf)
        return self.op(eng, lambda e: e.activation(out=o, in_=i, func=func, **kw), reads=reads, writes=writes)

    def tt(self, out, in0, in1, op, eng="dve"):
        o, a, b = out.ap, in0.ap, in1.ap
        return self.op(eng, lambda e: e.tensor_tensor(out=o, in0=a, in1=b, op=op),
                       reads=[in0.buf, in1.buf], writes=[out.buf])

    def ts(self, out, in0, s1, s2=None, op0=ALU.mult, op1=None, eng="dve", accum_out=None):
        o, a = out.ap, in0.ap
        reads = [in0.buf]
        writes = [out.buf]
        if isinstance(s1, V):
            reads.append(s1.buf)
            s1 = s1.ap
        if isinstance(s2, V):
            reads.append(s2.buf)
            s2 = s2.ap
        kw = {}
        if op1 is not None:
            kw["op1"] = op1
        if accum_out is not None:
            kw["accum_out"] = accum_out.ap
            writes.append(accum_out.buf)
        return self.op(eng, lambda e: e.tensor_scalar(out=o, in0=a, scalar1=s1, scalar2=s2, op0=op0, **kw),
                       reads=reads, writes=writes)

    def stt(self, out, in0, scalar, in1, op0, op1, eng="dve"):
        o, a, b = out.ap, in0.ap, in1.ap
        reads = [in0.buf, in1.buf]
        if isinstance(scalar, V):
            reads.append(scalar.buf)
            scalar = scalar.ap
        return self.op(eng, lambda e: e.scalar_tensor_tensor(out=o, in0=a, scalar=scalar, in1=b, op0=op0, op1=op1),
                       reads=reads, writes=[out.buf])

    def copy(self, out, in_, eng="dve"):
        o, i = out.ap, in_.ap
        if eng == "act":
            return self.op(eng, lambda e: e.copy(out=o, in_=i), reads=[in_.buf], writes=[out.buf])
        return self.op(eng, lambda e: e.tensor_copy(out=o, in_=i), reads=[in_.buf], writes=[out.buf])

    def memset(self, out, val, eng="dve"):
        o = out.ap
        return self.op(eng, lambda e: e.memset(o, val), writes=[out.buf])

    def reduce(self, out, in_, op=ALU.add, axis=AX.X, eng="dve"):
        o, i = out.ap, in_.ap
        return self.op(eng, lambda e: e.tensor_reduce(out=o, in_=i, axis=axis, op=op), reads=[in_.buf], writes=[out.buf])

    def scan(self, out, data0, data1, initial=0.0, op0=ALU.mult, op1=ALU.add):
        o, a, b = out.ap, data0.ap, data1.ap
        return self.op("dve", lambda e: e.tensor_tensor_scan(out=o, data0=a, data1=b, initial=initial, op0=op0, op1=op1),
                       reads=[data0.buf, data1.buf], writes=[out.buf])

    def recip(self, out, in_):
        o, i = out.ap, in_.ap
        return self.op("dve", lambda e: e.reciprocal(out=o, in_=i), reads=[in_.buf], writes=[out.buf])


WSPEC = {
    'ffn1_norm': (D,), 'ffn1_w_gate': (D, DFF), 'ffn1_w_up': (D, DFF), 'ffn1_w_down': (DFF, D),
    'mix_norm': (D,), 'w_in': (D, NIN),
    's5_lam_re': (2, 32, 64), 's5_lam_im': (2, 32, 64), 's5_log_step': (2, 32),
    's5_b_re': (2, 32, 64, 16), 's5_b_im': (2, 32, 64, 16), 's5_c_re': (2, 32, 16, 64), 's5_c_im': (2, 32, 16, 64),
    's5_d': (32, 16), 's5_w_glu': (W, W), 'q_norm': (64,), 'k_norm': (64,), 'attn_sink': (8,),
    'rw_mu': (2, N_RW), 'rw_w0': (2, W), 'rw_w2': (2, 64, W), 'rw_a0': (2, W), 'rw_a2': (2, 64, W),
    'rw_g2': (128, W), 'rw_k_k': (W,), 'rw_k_a': (W,), 'rw_r_k': (8, 64), 'rw_ln_w': (W,), 'rw_ln_b': (W,),
    'w_branch': (3, W, D), 'w_out': (D, D),
    'ffn2_norm': (D,), 'ffn2_w_gate': (D, DFF), 'ffn2_w_up': (D, DFF), 'ffn2_w_down': (DFF, D),
}
BF_W = {
    'ffn1_w_gate': (D, DFF), 'ffn1_w_up': (D, DFF), 'ffn1_w_down': (DFF, D),
    'ffn2_w_gate': (D, DFF), 'ffn2_w_up': (D, DFF), 'ffn2_w_down': (DFF, D),
    'w_in': (D, NIN), 's5_w_glu': (W, W), 'w_branch': (3 * W, D), 'w_out': (D, D),
    'rw_w2': (128, W), 'rw_a2': (128, W), 'rw_g2': (128, W),
}


class Rot:
    def __init__(self, items):
        self.items = items
        self.i = 0

    def next(self):
        b = self.items[self.i % len(self.items)]
        self.i += 1
        return b


class SlotPool:
    cur = [0]

    def __init__(self, items):
        self.items = items

    def next(self):
        return self.items[SlotPool.cur[0] % len(self.items)]


class Prog:
    def __init__(self, segs, depth=DEPTH, debug=()):
        self.segs = list(segs)
        self.T = sum(segs)
        assert all(s % TT == 0 for s in segs)
        self.NT = self.T // TT
        self.depth = depth
        self.debug = set(debug)
        self.nc = bass.Bass("TRN2", target_bir_lowering=False)
        self.fw = Fw(self.nc)
        fw = self.fw
        T = self.T
        self.tile_pos = []
        for si, L in enumerate(self.segs):
            for k in range(L // TT):
                self.tile_pos.append((si, k * (TT // 128)))
        self.seg_start = np.cumsum([0] + self.segs).tolist()
        self.x_in = fw.dram([T, D], F32, "x", kind="ExternalInput")
        self.y_out = fw.dram([T, D], F32, "y", kind="ExternalOutput")
        self.w = {n: fw.dram([DEPTH] + list(s), F32, n, kind="ExternalInput") for n, s in WSPEC.items()}
        self.wb = {n: [fw.dram(list(s), BF16, f"bf_{n}_{l}") for l in range(depth)] for n, s in BF_W.items()
                   if not (n.startswith("ffn") or n == "w_in")}
        self.wt = {}
        for f_ in ("ffn1", "ffn2"):
            self.wt[f"{f_}_w_gate"] = [fw.dram([11, 128, 8, 256], BF16, f"bt_{f_}_g_{l}") for l in range(depth)]
            self.wt[f"{f_}_w_up"] = [fw.dram([11, 128, 8, 256], BF16, f"bt_{f_}_u_{l}") for l in range(depth)]
            self.wt[f"{f_}_w_down"] = [fw.dram([2, 2, 128, 11, 512], BF16, f"bt_{f_}_d_{l}") for l in range(depth)]
        self.wt["w_in_tm"] = [fw.dram([128, 8, 1280], BF16, f"bt_win_tm_{l}") for l in range(depth)]
        self.wt["w_in_fm"] = [fw.dram([13, 128, 8, 384], BF16, f"bt_win_fm_{l}") for l in range(depth)]

        def scratch(name, shape, dt):
            kind = "ExternalOutput" if name in self.debug else "Internal"
            return fw.dram(shape, dt, name, kind=kind)
        self.scratch = scratch
        NT = self.NT
        self.xres = scratch("xres", [T, D], F32)
        self.xcur = scratch("xcur", [T, D], F32)
        self.zu = scratch("zu", [T, W], BF16)
        self.qT = scratch("qT", [8, 64, T], BF16)
        self.kT = scratch("kT", [2, 64, T], BF16)
        self.vv = scratch("vv", [T, 128], BF16)
        self.zrwT = scratch("zrwT", [N_RW, T], BF16)
        self.sgT = scratch("sgT", [3 * D, T], BF16)
        self.ya = scratch("ya", [T, W], BF16)
        self.yb = scratch("yb", [T, W], BF16)
        self.yc = scratch("yc", [T, W], BF16)
        def regs(b):
            return [b.sub(f"{b.name}_t{i}") for i in range(NT)]
        self.r_xres, self.r_xcur, self.r_zu = regs(self.xres), regs(self.xcur), regs(self.zu)
        self.r_qT, self.r_kT, self.r_vv = regs(self.qT), regs(self.kT), regs(self.vv)
        self.r_zrwT, self.r_sgT = regs(self.zrwT), regs(self.sgT)
        self.r_ya, self.r_yb, self.r_yc = regs(self.ya), regs(self.yb), regs(self.yc)
        self.r_xin, self.r_yout = regs(self.x_in), regs(self.y_out)
        self.psA = Rot([fw.psum([128, 512], F32, f"psA{i}") for i in range(4)])
        self.psB = Rot([fw.psum([128, 512], F32, f"psB{i}") for i in range(2)])
        self.psT = Rot([fw.psum([128, 1024], BF16, f"psT{i}") for i in range(2)])
        self.dq = Rot(["sp"])
        self.ident = fw.sbuf([128, 128], BF16, "ident")
        self.identf = fw.sbuf([128, 128], F32, "identf")
        fw.memset(self.ident.v(), 1.0, eng="pool")
        idap = self.ident.v().ap
        fw.op("pool", lambda e: e.affine_select(out=idap, in_=idap, pattern=[[-1, 128]], compare_op=ALU.is_equal,
                                                fill=0.0, base=0, channel_multiplier=1),
              reads=[self.ident], writes=[self.ident])
        fw.copy(self.identf.v(), self.ident.v(), eng="pool")

    def wview(self, name, l):
        v = self.w[name][l]
        shp = WSPEC[name]
        if name == 'w_branch':
            return v.re("a b c -> (a b) c")
        if name in ('rw_w2', 'rw_a2'):
            return v.re("a b c -> (a b) c")
        return v

    def prep_weights(self, l, part="all"):
        fw = self.fw
        for name, (R, C) in BF_W.items():
            early = name.startswith("ffn1") or name == "w_in"
            if (part == "A" and not early) or (part == "B" and early):
                continue
            src = self.wview(name, l)
            if name.endswith("w_gate") or name.endswith("w_up"):
                dst = self.wt[name][l]
                for kc in range(8):
                    fw.dma("pool", V(dst, dst.h.ap()[:, :, kc, :].rearrange("j p f -> p j f")),
                           V(src.buf, src.ap[kc * 128:(kc + 1) * 128, :].rearrange("p (j f) -> p j f", f=256)))
                continue
            if name.endswith("w_down"):
                dst = self.wt[name][l]
                for j in range(NFF):
                    half, jl = j // 11, j % 11
                    fw.dma("pool", V(dst, dst.h.ap()[:, half, :, jl, :].rearrange("h p d -> p h d")),
                           V(src.buf, src.ap[j * 128:(j + 1) * 128, :].rearrange("p (h d) -> p h d", d=512)))
                continue
            if name == "w_in":
                dtm, dfm = self.wt["w_in_tm"][l], self.wt["w_in_fm"][l]
                for kc in range(8):
                    fw.dma("pool", V(dtm, dtm.h.ap()[:, kc, :]), src[kc * 128:(kc + 1) * 128, 0:1280])
                    fw.dma("pool", V(dfm, dfm.h.ap()[:, :, kc, :].rearrange("g p f -> p g f")),
                           V(src.buf, src.ap[kc * 128:(kc + 1) * 128, C_RW:C_RW + 13 * 384].rearrange("p (g f) -> p g f", f=384)))
                continue
            dst = self.wb[name][l]
            for r0 in range(0, R, 128):
                r1 = min(R, r0 + 128)
                for c0 in range(0, C, 2048):
                    c1 = min(C, c0 + 2048)
                    fw.dma("pool", dst[r0:r1, c0:c1], src[r0:r1, c0:c1])

    def sincos(self, ang, sin_out, cos_out, shape, tmp=None):
        fw = self.fw
        if tmp is None:
            ti = fw.sbuf(shape, I32)
            tf = fw.sbuf(shape, F32)
            tc = fw.sbuf(shape, F32)
        else:
            ti, tf, tc = tmp
        def vv(t):
            return t if isinstance(t, V) else t.v()
        ti, tf, tc = vv(ti), vv(tf), vv(tc)
        fw.ts(tf, ang, 1.0 / (2 * math.pi), None, op0=ALU.mult)
        fw.copy(ti, tf)
        fw.copy(tf, ti)
        fw.stt(ang, tf, -2.0 * math.pi, ang, op0=ALU.mult, op1=ALU.add)
        fw.ts(tc, ang, math.pi / 2, None, op0=ALU.add)
        fw.ts(tf, tc, math.pi, -2.0 * math.pi, op0=ALU.is_gt, op1=ALU.mult)
        fw.tt(tc, tc, tf, ALU.add)
        lim = 3.14159
        fw.ts(ang, ang, -lim, lim, op0=ALU.max, op1=ALU.min)
        fw.ts(tc, tc, -lim, lim, op0=ALU.max, op1=ALU.min)
        fw.act(sin_out, ang, AF.Sin)
        fw.act(cos_out, tc, AF.Sin)

    def load_bcast(self, dst, src_row, q="sp"):
        self.fw.dma(q, dst, src_row.pbc(128))

    def alloc_tok(self, wrot=3, nx=2):
        fw = self.fw
        B = {}
        B["xts"] = [fw.sbuf([128, 4, D], F32, f"xt{i}") for i in range(nx)]
        B["xt"] = B["xts"][0]
        B["gain"] = fw.sbuf([128, D], F32, "gain")
        B["gain2"] = fw.sbuf([128, D], F32, "gain2")
        B["h4"] = [fw.sbuf([128, D], BF16, f"h{i}") for i in range(4)]
        B["junk"] = fw.sbuf([128, D], BF16, "junk")
        B["ss"] = Rot([fw.sbuf([128, 2], F32, f"ss{i}") for i in range(4)])
        B["hT"] = fw.sbuf([128, 8, TT], BF16, "hT")
        B["aT"] = fw.sbuf([128, NFF, TT], BF16, "aT")
        B["sg"] = Rot([fw.sbuf([128, TT], F32, f"sg{i}") for i in range(2)])
        B["wg"] = Rot([fw.sbuf([128, 8, 256], BF16, f"wg{i}") for i in range(wrot)])
        B["wu"] = Rot([fw.sbuf([128, 8, 256], BF16, f"wu{i}") for i in range(wrot)])
        B["wd"] = Rot([fw.sbuf([128, 11, 512], BF16, f"wd{i}") for i in range(2)])
        return B

    def norm_T4(self, B, xt, gain, hT):
        fw = self.fw
        hs = []
        for s in range(4):
            ss = B["ss"].next()
            xsub = xt[:, s, :]
            fw.act(B["junk"].v(), xsub, AF.Square, accum_out=ss[:, 0:1])
            fw.act(ss[:, 1:2], ss[:, 0:1], AF.Sqrt, scale=1.0 / D, bias=self.eps_t[:, 0:1])
            fw.recip(ss[:, 1:2], ss[:, 1:2])
            h = B["h4"][s]
            fw.stt(h.v(), xsub, ss[:, 1:2], gain.v(), op0=ALU.mult, op1=ALU.mult)
            hs.append(h)
        for s in range(4):
            pt = self.psT.next()
            ptv = pt.v().re("p (k t) -> p k t", k=8)
            for kc in range(8):
                fw.transpose(ptv[:, kc, :], hs[s][:, kc * 128:(kc + 1) * 128], self.ident.v(), inc=(kc == 7))
            fw.copy(hT[:, :, s * 128:(s + 1) * 128], ptv, eng=("act" if s % 2 else "dve"))

    def ffn(self, B, l, which, gain, prefetch=None):
        fw = self.fw
        xt, hT, aT = B["xt"], B["hT"], B["aT"]
        wg_d = self.wt[f"{which}_w_gate"][l]
        wu_d = self.wt[f"{which}_w_up"][l]
        wd_d = self.wt[f"{which}_w_down"][l]
        self.norm_T4(B, xt, gain, hT)
        if prefetch is not None:
            prefetch()
        for jj in range(NFF // 2):
            wg, wu = B["wg"].next(), B["wu"].next()
            fw.dma("sp", wg.v(), wg_d[jj])
            fw.dma("sp", wu.v(), wu_d[jj])
            for jf in range(2):
                j = 2 * jj + jf
                gp, up = self.psA.next(), self.psA.next()
                for kc in range(8):
                    fw.matmul(gp.v(), wg[:, kc, jf * 128:(jf + 1) * 128], hT[:, kc, :], start=(kc == 0), stop=(kc == 7))
                for kc in range(8):
                    fw.matmul(up.v(), wu[:, kc, jf * 128:(jf + 1) * 128], hT[:, kc, :], start=(kc == 0), stop=(kc == 7))
                sg = B["sg"].next()
                fw.act(sg.v(), gp.v(), AF.Silu)
                fw.tt(aT[:, j, :], sg.v(), up.v(), ALU.mult)
        for dh in range(2):
            outs = [self.psA.next() for _ in range(4)]
            for half in range(2):
                wd = B["wd"].next()
                fw.dma("sp", wd.v(), wd_d[dh][half])
                for ts in range(4):
                    for jl in range(11):
                        j = half * 11 + jl
                        fw.matmul(outs[ts].v(), aT[:, j, ts * 128:(ts + 1) * 128], wd[:, jl, :],
                                  start=(j == 0), stop=(j == NFF - 1), inc=(jl == 10))
            for ts in range(4):
                xs = xt[:, ts, dh * 512:(dh + 1) * 512]
                fw.stt(xs, outs[ts].v(), 0.5, xs, op0=ALU.mult, op1=ALU.add)

    def load_x(self, B, src_regs, src, it):
        if it >= self.NT:
            return
        t0 = it * TT
        v = V(src_regs[it], src.h.ap()[t0:t0 + TT, :].rearrange("(s p) d -> p s d", p=128))
        self.fw.dma("sp", B["xts"][it % len(B["xts"])].v(), v)

    def store_x(self, B, dst_regs, dst, it, q="sp"):
        t0 = it * TT
        v = V(dst_regs[it], dst.h.ap()[t0:t0 + TT, :].rearrange("(s p) d -> p s d", p=128))
        self.fw.dma(q, v, B["xt"].v())

    def init_consts(self):
        fw = self.fw
        self.eps_t = fw.sbuf([128, 4], F32, "eps_t")
        fw.memset(self.eps_t[:, 0:1], RMS_EPS)
        fw.memset(self.eps_t[:, 1:2], 64e-5)
        fw.memset(self.eps_t[:, 2:3], 1.0)
        fw.memset(self.eps_t[:, 3:4], 0.0)
        NB = max(self.segs) // 128
        self.NBmax = NB
        self.cosT = fw.sbuf([128, NB, 8], F32, "cosT")
        self.sinT = fw.sbuf([128, NB, 8], F32, "sinT")
        with fw.phase():
            posi = fw.sbuf([128, NB], I32)
            posf = fw.sbuf([128, NB], F32)
            ang = fw.sbuf([128, NB, 8], F32)
            pap = posi.v().ap
            fw.op("pool", lambda e: e.iota(pap, pattern=[[128, NB]], base=0, channel_multiplier=1), writes=[posi])
            fw.copy(posf.v(), posi.v())
            for i in range(8):
                inv = ROPE_THETA ** (-(2.0 * i) / 16.0)
                fw.ts(ang[:, :, i], posf.v(), float(inv), None, op0=ALU.mult)
            self.sincos(ang.v(), self.sinT.v(), self.cosT.v(), [128, NB, 8])

    def qk_head_norm(self, B, ps_v, nh, gain_bc, blk, out_bf):
        fw = self.fw
        sq = B["qsq"]
        qn = B["qn"]
        st = B["qst"].next()
        fw.act(sq[:, 0:nh * 64], ps_v, AF.Square)
        fw.reduce(st[:, 0:nh], sq[:, 0:nh * 64].re("p (h d) -> p h d", d=64))
        fw.act(st[:, 8:8 + nh], st[:, 0:nh], AF.Sqrt, scale=1.0 / 64, bias=self.eps_t[:, 0:1])
        fw.recip(st[:, 8:8 + nh], st[:, 8:8 + nh])
        q3 = qn[:, 0:nh * 64].re("p (h d) -> p h d", d=64)
        fw.tt(q3, ps_v.re("p (h d) -> p h d", d=64), st[:, 8:8 + nh].re("p (h o) -> p h o", o=1).bc([128, nh, 64]), ALU.mult)
        fw.tt(q3, q3, gain_bc.v().re("p (o d) -> p o d", o=1).bc([128, nh, 64]), ALU.mult)
        c = self.cosT[:, blk:blk + 1, :].bc([128, nh, 8])
        s = self.sinT[:, blk:blk + 1, :].bc([128, nh, 8])
        t1, t2 = q3[:, :, 0:8], q3[:, :, 8:16]
        ra = B["rope"]
        a = ra[:, 0:nh * 8].re("p (h d) -> p h d", d=8)
        b = ra[:, 64:64 + nh * 8].re("p (h d) -> p h d", d=8)
        c2 = ra[:, 128:128 + nh * 8].re("p (h d) -> p h d", d=8)
        d2 = ra[:, 192:192 + nh * 8].re("p (h d) -> p h d", d=8)
        fw.tt(a, t1, c, ALU.mult)
        fw.tt(b, t2, s, ALU.mult)
        fw.tt(c2, t2, c, ALU.mult)
        fw.tt(d2, t1, s, ALU.mult)
        fw.copy(out_bf[:, :, 16:64], q3[:, :, 16:64], eng="pool")
        fw.tt(out_bf[:, :, 0:8], a, b, ALU.subtract)
        fw.tt(out_bf[:, :, 8:16], c2, d2, ALU.add)

    def phase_p1(self, l, src, src_regs):
        fw = self.fw
        with fw.phase():
            B = self.alloc_tok()
            B["wtm"] = fw.sbuf([128, 8, 1280], BF16, "wtm")
            B["wfm"] = Rot([fw.sbuf([128, 8, 384], BF16, f"wfm{i}") for i in range(3)])
            B["stg"] = Rot([fw.sbuf([128, 3, TT], BF16, f"stg{i}") for i in range(2)])
            B["zu"] = fw.sbuf([128, 4, W], BF16, "zu_sb")
            B["v"] = fw.sbuf([128, 4, 128], BF16, "v_sb")
            B["qsq"] = fw.sbuf([128, 512], F32, "qsq")
            B["qraw"] = Rot([fw.sbuf([128, 512], F32, f"qraw{i}") for i in range(2)])
            B["kraw"] = Rot([fw.sbuf([128, 128], F32, f"kraw{i}") for i in range(2)])
            B["qn"] = fw.sbuf([128, 512], F32, "qn")
            B["qst"] = Rot([fw.sbuf([128, 16], F32, f"qst{i}") for i in range(2)])
            B["rope"] = fw.sbuf([128, 256], F32, "rope")
            B["qb"] = Rot([fw.sbuf([128, 8, 64], BF16, f"qb{i}") for i in range(2)])
            B["kb"] = Rot([fw.sbuf([128, 2, 64], BF16, f"kb{i}") for i in range(2)])
            B["qT"] = fw.sbuf([64, 8, TT], BF16, "qT_sb")
            B["kT"] = fw.sbuf([64, 2, TT], BF16, "kT_sb")
            qg = fw.sbuf([128, 64], F32, "qg")
            kg = fw.sbuf([128, 64], F32, "kg")
            self.load_bcast(B["gain"].v(), self.w['ffn1_norm'][l])
            self.load_bcast(B["gain2"].v(), self.w['mix_norm'][l])
            self.load_bcast(qg.v(), self.w['q_norm'][l])
            self.load_bcast(kg.v(), self.w['k_norm'][l])
            fw.dma("sp", B["wtm"].v(), self.wt["w_in_tm"][l].v())
            hT = B["hT"]
            self.load_x(B, src_regs, src, 0)
            for it in range(self.NT):
                t0 = it * TT
                si, blk0 = self.tile_pos[it]
                B["xt"] = xt = B["xts"][it % 2]
                self.ffn(B, l, "ffn1", B["gain"], prefetch=lambda it=it: self.load_x(B, src_regs, src, it + 1))
                self.store_x(B, self.r_xres, self.xres, it)
                self.norm_T4(B, xt, B["gain2"], hT)
                for ts in range(4):
                    tsl = slice(ts * 128, (ts + 1) * 128)
                    pu, pq, pk = self.psA.next(), self.psA.next(), self.psA.next()
                    for (ps, c0, n) in ((pu, 0, 512), (pq, 512, 512), (pk, 1024, 256)):
                        for kc in range(8):
                            fw.matmul(ps[:, 0:n], hT[:, kc, tsl], B["wtm"][:, kc, c0:c0 + n], start=(kc == 0), stop=(kc == 7))
                    qraw, kraw = B["qraw"].next(), B["kraw"].next()
                    fw.copy(B["zu"][:, ts, :], pu.v(), eng="act")
                    fw.copy(qraw.v(), pq.v(), eng="act")
                    fw.copy(kraw.v(), pk[:, 0:128], eng="act")
                    fw.copy(B["v"][:, ts, :], pk[:, 128:256], eng="act")
                    qb, kb = B["qb"].next(), B["kb"].next()
                    self.qk_head_norm(B, qraw.v(), 8, qg, blk0 + ts, qb.v())
                    self.qk_head_norm(B, kraw.v(), 2, kg, blk0 + ts, kb.v())
                    pt = self.psT.next()
                    ptv = pt[0:64, :].re("p (h t) -> p h t", h=8)
                    for h in range(8):
                        fw.transpose(ptv[:, h, :], qb[:, h, :], self.ident.v(), inc=(h == 7))
                    fw.copy(B["qT"][:, :, tsl], ptv, eng="act")
                    pt = self.psT.next()
                    ptv = pt[0:64, 0:256].re("p (h t) -> p h t", h=2)
                    for h in range(2):
                        fw.transpose(ptv[:, h, :], kb[:, h, :], self.ident.v(), inc=(h == 1))
                    fw.copy(B["kT"][:, :, tsl], ptv, eng="act")
                fw.dma("pool", V(self.r_zu[it], self.zu.h.ap()[t0:t0 + TT, :].rearrange("(s p) c -> p s c", p=128)), B["zu"].v())
                fw.dma("pool", V(self.r_vv[it], self.vv.h.ap()[t0:t0 + TT, :].rearrange("(s p) c -> p s c", p=128)), B["v"].v())
                fw.dma("pool", V(self.r_qT[it], self.qT.h.ap()[:, :, t0:t0 + TT].rearrange("h d t -> d h t")), B["qT"].v())
                fw.dma("pool", V(self.r_kT[it], self.kT.h.ap()[:, :, t0:t0 + TT].rearrange("h d t -> d h t")), B["kT"].v())
                for g in range(13):
                    wfm = B["wfm"].next()
                    c0 = C_RW + g * 384
                    fw.dma("sp", wfm.v(), self.wt["w_in_fm"][l][g])
                    stg = B["stg"].next()
                    for ci in range(3):
                        ps = self.psA.next()
                        for kc in range(8):
                            fw.matmul(ps.v(), wfm[:, kc, ci * 128:(ci + 1) * 128], hT[:, kc, :], start=(kc == 0), stop=(kc == 7))
                        if g < 5:
                            fw.copy(stg[:, ci, :], ps.v(), eng=("act" if ci % 2 else "dve"))
                        else:
                            fw.act(stg[:, ci, :], ps.v(), AF.Sigmoid)
                    if g < 5:
                        dst = V(self.r_zrwT[it], self.zrwT.h.ap()[g * 384:(g + 1) * 384, t0:t0 + TT].rearrange("(c p) t -> p c t", p=128))
                    else:
                        r0 = (g - 5) * 384
                        dst = V(self.r_sgT[it], self.sgT.h.ap()[r0:r0 + 384, t0:t0 + TT].rearrange("(c p) t -> p c t", p=128))
                    fw.dma("pool", dst, stg.v())

    def phase_p3(self, l, dst, dst_regs):
        fw = self.fw
        with fw.phase():
            B = self.alloc_tok(wrot=3, nx=1)
            yin = Rot([fw.sbuf([128, 4, W], BF16, f"yin{i}") for i in range(2)])
            yT = [fw.sbuf([128, 4, TT], BF16, f"yT{i}") for i in range(3)]
            yaT = fw.sbuf([128, 4, TT], BF16, "yaT")
            sgl = Rot([fw.sbuf([128, 3, TT], BF16, f"sgl{i}") for i in range(2)])
            mT = fw.sbuf([128, 8, TT], BF16, "mT")
            wbr = fw.sbuf([128, 12, D], BF16, "wbr")
            wout = fw.sbuf([128, 8, D], BF16, "wout")
            wglu = fw.sbuf([128, 4, W], BF16, "wglu")
            tmpa = fw.sbuf([128, TT], F32, "tmpa")
            tmpb = fw.sbuf([128, TT], F32, "tmpb")
            self.load_bcast(B["gain"].v(), self.w['ffn2_norm'][l])
            fw.dma("sp", wbr.v(), self.wb['w_branch'][l].v().re("(k p) d -> p k d", p=128))
            fw.dma("sp", wout.v(), self.wb['w_out'][l].v().re("(k p) d -> p k d", p=128))
            fw.dma("sp", wglu.v(), self.wb['s5_w_glu'][l].v().re("(k p) d -> p k d", p=128))
            ysrc = [(self.ya, self.r_ya), (self.yb, self.r_yb), (self.yc, self.r_yc)]
            for it in range(self.NT):
                t0 = it * TT
                B["xt"] = xt = B["xts"][0]
                self.load_x(B, self.r_xres, self.xres, it)
                for i in range(3):
                    yb_, yr = ysrc[i]
                    yi = yin.next()
                    fw.dma("sp", yi.v(), V(yr[it], yb_.h.ap()[t0:t0 + TT, :].rearrange("(s p) c -> p s c", p=128)))
                    for s0 in (0, 2):
                        pt = self.psT.next()
                        ptv = pt.v().re("p (s k t) -> p s k t", s=2, k=4)
                        for s in range(2):
                            for kc in range(4):
                                fw.transpose(ptv[:, s, kc, :], yi[:, s0 + s, kc * 128:(kc + 1) * 128], self.ident.v(),
                                             inc=(s == 1 and kc == 3))
                        fw.copy(yT[i][:, :, s0 * 128:(s0 + 2) * 128].re("p k (s t) -> p s k t", s=2), ptv,
                                eng=("act" if s0 else "dve"))
                for f in range(4):
                    ps = self.psA.next()
                    for kc in range(4):
                        fw.matmul(ps.v(), wglu[:, kc, f * 128:(f + 1) * 128], yT[0][:, kc, :], start=(kc == 0), stop=(kc == 3))
                    sg = B["sg"].next()
                    fw.act(sg.v(), ps.v(), AF.Sigmoid)
                    fw.tt(yaT[:, f, :], sg.v(), yT[0][:, f, :], ALU.mult)
                rhs_i = [yaT, yT[1], yT[2]]
                for f in range(8):
                    sg3 = sgl.next()
                    src = V(self.r_sgT[it], self.sgT.h.ap()[:, t0:t0 + TT].rearrange("(i f p) t -> p i f t", i=3, p=128)[:, :, f, :])
                    fw.dma("sp", sg3.v(), src)
                    pss = []
                    for i in range(3):
                        ps = self.psA.next()
                        for kc in range(4):
                            fw.matmul(ps.v(), wbr[:, i * 4 + kc, f * 128:(f + 1) * 128], rhs_i[i][:, kc, :],
                                      start=(kc == 0), stop=(kc == 3))
                        pss.append(ps)
                    fw.tt(tmpa.v(), pss[0].v(), sg3[:, 0, :], ALU.mult)
                    fw.tt(tmpb.v(), pss[1].v(), sg3[:, 1, :], ALU.mult)
                    fw.tt(tmpa.v(), tmpa.v(), tmpb.v(), ALU.add, eng="pool")
                    fw.tt(tmpb.v(), pss[2].v(), sg3[:, 2, :], ALU.mult)
                    fw.tt(mT[:, f, :], tmpa.v(), tmpb.v(), ALU.add, eng="pool")
                for dh in range(2):
                    for ts in range(4):
                        ps = self.psA.next()
                        for kc in range(8):
                            fw.matmul(ps.v(), mT[:, kc, ts * 128:(ts + 1) * 128], wout[:, kc, dh * 512:(dh + 1) * 512],
                                      start=(kc == 0), stop=(kc == 7))
                        xs = xt[:, ts, dh * 512:(dh + 1) * 512]
                        fw.tt(xs, ps.v(), xs, ALU.add)
                self.ffn(B, l, "ffn2", B["gain"])
                self.store_x(B, dst_regs, dst, it, q="pool")

    def phase_attn(self, l):
        fw = self.fw
        Lmax = max(self.segs)
        nbmax = Lmax // 128
        with fw.phase():
            QT = fw.sbuf([64, 8, Lmax], BF16, "QT")
            KT = fw.sbuf([64, 2, Lmax], BF16, "KT")
            VA = fw.sbuf([128, nbmax, 2, 65], BF16, "VA")
            mprev = fw.sbuf([128, 128], BF16, "mprev")
            mnext = fw.sbuf([128, 128], BF16, "mnext")
            esink = fw.sbuf([128, 8], F32, "esink")
            pT = Rot([fw.sbuf([128, 512], BF16, f"pT{i}") for i in range(6)])
            den = Rot([fw.sbuf([128, 8], F32, f"den{i}") for i in range(2)])
            ystg = Rot([fw.sbuf([128, 4, W], BF16, f"ystg{i}") for i in range(2)])
            for m, cm, pat in ((mprev, 1, -1), (mnext, -1, 1)):
                fw.memset(m.v(), 1.0, eng="pool")
                map_ = m.v().ap
                fw.op("pool", lambda e, map_=map_, cm=cm, pat=pat: e.affine_select(
                    out=map_, in_=map_, pattern=[[pat, 128]], compare_op=ALU.is_ge, fill=0.0, base=0, channel_multiplier=cm),
                    reads=[m], writes=[m])
            self.load_bcast(esink.v(), self.w['attn_sink'][l])
            fw.act(esink.v(), esink.v(), AF.Exp)
            fw.memset(VA[:, :, :, 64:65], 1.0)
            for si, L in enumerate(self.segs):
                tok0 = self.seg_start[si]
                nb = L // 128
                it0 = tok0 // TT
                for k in range(L // TT):
                    it = it0 + k
                    a, b = k * TT, (k + 1) * TT
                    fw.dma("sp", QT[:, :, a:b], V(self.r_qT[it], self.qT.h.ap()[:, :, tok0 + a:tok0 + b].rearrange("h d t -> d h t")))
                    fw.dma("sp", KT[:, :, a:b], V(self.r_kT[it], self.kT.h.ap()[:, :, tok0 + a:tok0 + b].rearrange("h d t -> d h t")))
                    for hk in range(2):
                        fw.dma("sp", VA[:, k * 4:(k + 1) * 4, hk, 0:64],
                               V(self.r_vv[it], self.vv.h.ap()[tok0 + a:tok0 + b, hk * 64:(hk + 1) * 64].rearrange("(b p) d -> p b d", p=128)))
                for n in range(nb):
                    qs = slice(n * 128, (n + 1) * 128)
                    if n % 4 == 0:
                        yst = ystg.next()
                    kbs = [kb for kb in (n - 1, n, n + 1) if 0 <= kb < nb]
                    for hk in range(2):
                        pts = []
                        for kb in kbs:
                            ps = self.psA.next()
                            fw.matmul(ps.v(), KT[:, hk, kb * 128:(kb + 1) * 128], QT[:, hk * 4:(hk + 1) * 4, qs])
                            p = pT.next()
                            fw.act(p.v(), ps.v(), AF.Exp, scale=0.125)
                            if kb != n:
                                mk = mprev if kb < n else mnext
                                p3 = p.v().re("p (g q) -> p g q", g=4)
                                fw.tt(p3, p3, mk.v().re("p (o q) -> p o q", o=1).bc([128, 4, 128]), ALU.mult,
                                      eng=("pool" if kb < n else "dve"))
                            pts.append((kb, p))
                        po = self.psB.next()
                        pov = po[:, 0:260].re("p (g d) -> p g d", g=4)
                        for g in range(4):
                            for i, (kb, p) in enumerate(pts):
                                fw.matmul(pov[:, g, :], p[:, g * 128:(g + 1) * 128], VA[:, kb, hk, :],
                                          start=(i == 0), stop=(i == len(pts) - 1))
                        dn = den.next()
                        fw.tt(dn[:, 0:4], pov[:, :, 64], esink[:, hk * 4:(hk + 1) * 4], ALU.add)
                        fw.recip(dn[:, 4:8], dn[:, 0:4])
                        yo = yst[:, n % 4, hk * 256:(hk + 1) * 256].re("p (g d) -> p g d", g=4)
                        fw.tt(yo, pov[:, :, 0:64], dn[:, 4:8].re("p (g o) -> p g o", o=1).bc([128, 4, 64]), ALU.mult)
                    if n % 4 == 3:
                        it = it0 + n // 4
                        t0 = it * TT
                        fw.dma("pool", V(self.r_yb[it], self.yb.h.ap()[t0:t0 + TT, :].rearrange("(s p) c -> p s c", p=128)), yst.v())

    def s5_precompute(self, l):
        fw = self.fw
        P = {}
        P["Rr"] = fw.sbuf([128, 32, 128], BF16, "s5Rr")
        P["Ri"] = fw.sbuf([128, 32, 128], BF16, "s5Ri")
        P["T"] = fw.sbuf([128, 32, 128], BF16, "s5T")
        P["Or"] = fw.sbuf([128, 32, 128], BF16, "s5Or")
        P["Oni"] = fw.sbuf([128, 32, 128], BF16, "s5Oni")
        P["rho8"] = fw.sbuf([128, 32], F32, "s5rho8")
        P["th8"] = fw.sbuf([128, 32], F32, "s5th8")
        with fw.phase():
            def t32(name, shape=(128, 32)):
                return fw.sbuf(list(shape), F32, name)
            lamX = [t32("lamXr", (32, 128)), t32("lamXi", (32, 128))]
            lam = [t32("lam_r"), t32("lam_i")]
            for part, nm in enumerate(("s5_lam_re", "s5_lam_im")):
                for d in range(2):
                    fw.dma("sp", lamX[part][:, d * 64:(d + 1) * 64], self.w[nm][l][d])
                ps = self.psB.next()
                fw.transpose(ps[:, 0:32], lamX[part].v(), self.identf[0:32, 0:32])
                fw.copy(lam[part].v(), ps[:, 0:32])
            if getattr(self, 's5_stop', 99) <= 1:
                return P
            dt = t32("dt")
            for d in range(2):
                fw.dma("sp", dt[d * 64:(d + 1) * 64, :], self.w["s5_log_step"][l][d].pbc(64))
            fw.act(dt.v(), dt.v(), AF.Exp)
            lamr = lam[0]
            fw.ts(lamr.v(), lamr.v(), -1e-4, None, op0=ALU.min)
            lr, li = t32("lr"), t32("li")
            fw.tt(lr.v(), lamr.v(), dt.v(), ALU.mult)
            fw.tt(li.v(), lam[1].v(), dt.v(), ALU.mult)
            fw.act(P["rho8"].v(), lr.v(), AF.Exp, scale=8.0)
            fw.ts(P["th8"].v(), li.v(), 8.0, None, op0=ALU.mult)
            if getattr(self, 's5_stop', 99) <= 2:
                return P
            e1, sn, cs, ang = t32("e1"), t32("sn"), t32("cs"), t32("ang")
            fw.act(e1.v(), lr.v(), AF.Exp)
            fw.copy(ang.v(), li.v())
            self.sincos(ang.v(), sn.v(), cs.v(), [128, 32])
            am1, ai = t32("am1"), t32("ai")
            fw.tt(am1.v(), e1.v(), cs.v(), ALU.mult)
            fw.ts(am1.v(), am1.v(), -1.0, None, op0=ALU.add)
            fw.tt(ai.v(), e1.v(), sn.v(), ALU.mult)
            den, t1, t2, wr, wi = t32("den"), t32("t1"), t32("t2"), t32("wr"), t32("wi")
            fw.tt(den.v(), lamr.v(), lamr.v(), ALU.mult)
            fw.tt(t1.v(), lam[1].v(), lam[1].v(), ALU.mult)
            fw.tt(den.v(), den.v(), t1.v(), ALU.add)
            fw.recip(den.v(), den.v())
            fw.tt(t1.v(), am1.v(), lamr.v(), ALU.mult)
            fw.tt(t2.v(), ai.v(), lam[1].v(), ALU.mult)
            fw.tt(wr.v(), t1.v(), t2.v(), ALU.add)
            fw.tt(wr.v(), wr.v(), den.v(), ALU.mult)
            fw.tt(t1.v(), ai.v(), lamr.v(), ALU.mult)
            fw.tt(t2.v(), am1.v(), lam[1].v(), ALU.mult)
            fw.tt(wi.v(), t1.v(), t2.v(), ALU.subtract)
            fw.tt(wi.v(), wi.v(), den.v(), ALU.mult)
            if getattr(self, 's5_stop', 99) <= 3:
                return P
            Br, Bi = t32("Br", (128, 32, 16)), t32("Bi", (128, 32, 16))
            for d in range(2):
                fw.dma("sp", Br[d * 64:(d + 1) * 64], self.w["s5_b_re"][l][d].re("g p h -> p g h"))
                fw.dma("sp", Bi[d * 64:(d + 1) * 64], self.w["s5_b_im"][l][d].re("g p h -> p g h"))
            Bbr, Bbi = t32("Bbr", (128, 32, 16)), t32("Bbi", (128, 32, 16))
            pa, pb = t32("pa", (128, 32, 16)), t32("pb", (128, 32, 16))
            wrb = wr.v().re("p (g o) -> p g o", o=1).bc([128, 32, 16])
            wib = wi.v().re("p (g o) -> p g o", o=1).bc([128, 32, 16])
            fw.tt(pa.v(), Br.v(), wrb, ALU.mult)
            fw.tt(pb.v(), Bi.v(), wib, ALU.mult)
            fw.tt(Bbr.v(), pa.v(), pb.v(), ALU.subtract)
            fw.tt(pa.v(), Bi.v(), wrb, ALU.mult)
            fw.tt(pb.v(), Br.v(), wib, ALU.mult)
            fw.tt(Bbi.v(), pa.v(), pb.v(), ALU.add)
            if getattr(self, 's5_stop', 99) <= 4:
                return P
            Cr, Ci = t32("Cr", (128, 32, 16)), t32("Ci", (128, 32, 16))
            CX = Rot([t32(f"CX{i}", (128, 128)) for i in range(2)])
            for Ct, nm in ((Cr, "s5_c_re"), (Ci, "s5_c_im")):
                for gb in range(4):
                    X = CX.next()
                    for d in range(2):
                        fw.dma("sp", X[:, d * 64:(d + 1) * 64], self.w[nm][l][d][gb * 8:(gb + 1) * 8].re("g h p -> (g h) p"))
                    ps = self.psB.next()
                    fw.transpose(ps[:, 0:128], X.v(), self.identf.v())
                    fw.copy(Ct[:, gb * 8:(gb + 1) * 8, :], ps[:, 0:128].re("p (g h) -> p g h", h=16))
            if getattr(self, 's5_stop', 99) <= 5:
                return P
            ti = fw.sbuf([128, 8], I32, "ti")
            tf = t32("tf", (128, 8))
            tiap = ti.v().ap
            fw.op("pool", lambda e: e.iota(tiap, pattern=[[1, 8]], base=0, channel_multiplier=0), writes=[ti])
            fw.copy(tf.v(), ti.v())
            kR, kO2, kO = t32("kR", (128, 8)), t32("kO2", (128, 8)), t32("kO", (128, 8))
            fw.ts(kR[0:64], tf[0:64], -1.0, 7.0, op0=ALU.mult, op1=ALU.add)
            fw.copy(kR[64:128], tf[64:128])
            fw.ts(kO2[0:64], tf[0:64], -7.0, None, op0=ALU.add)
            fw.ts(kO2[64:128], tf[64:128], -1.0, None, op0=ALU.mult)
            fw.ts(kO[0:64], tf[0:64], 1.0, None, op0=ALU.add)
            fw.ts(kO[64:128], tf[64:128], -1.0, 8.0, op0=ALU.mult, op1=ALU.add)
            lrb = lr.v().re("p (g o) -> p g o", o=1).bc([128, 32, 8])
            lib = li.v().re("p (g o) -> p g o", o=1).bc([128, 32, 8])
            big = (128, 32, 8, 16)
            q1, q2 = t32("q1", big), t32("q2", big)
            RTb = [fw.sbuf([128, 32, 128], BF16, "RTbr"), fw.sbuf([128, 32, 128], BF16, "RTbi")]
            O2b = [fw.sbuf([128, 32, 128], BF16, "O2br"), fw.sbuf([128, 32, 128], BF16, "O2bni")]

            def powers(kt):
                mg, ph = t32("mg", (128, 32, 8)), t32("ph", (128, 32, 8))
                pr, pi = t32("pr", (128, 32, 8)), t32("pi", (128, 32, 8))
                kb = kt.v().re("p (o k) -> p o k", o=1).bc([128, 32, 8])
                fw.tt(mg.v(), lrb, kb, ALU.mult)
                fw.act(mg.v(), mg.v(), AF.Exp)
                fw.tt(ph.v(), lib, kb, ALU.mult)
                self.sincos(ph.v(), pi.v(), pr.v(), [128, 32, 8])
                fw.tt(pr.v(), pr.v(), mg.v(), ALU.mult)
                fw.tt(pi.v(), pi.v(), mg.v(), ALU.mult)
                return pr, pi

            def cprod(Xr, Xi, pr, pi, out_r, out_i, neg_i):
                xr = Xr.v().re("p g (o h) -> p g o h", o=1).bc(list(big))
                xi = Xi.v().re("p g (o h) -> p g o h", o=1).bc(list(big))
                prb = pr.v().re("p g (k o) -> p g k o", o=1).bc(list(big))
                pib = pi.v().re("p g (k o) -> p g k o", o=1).bc(list(big))
                o_r = out_r.v().re("p g (k h) -> p g k h", h=16)
                o_i = out_i.v().re("p g (k h) -> p g k h", h=16)
                fw.tt(q1.v(), xr, prb, ALU.mult)
                fw.tt(q2.v(), xi, pib, ALU.mult, eng="pool")
                fw.tt(o_r, q1.v(), q2.v(), ALU.subtract)
                fw.tt(q1.v(), xr, pib, ALU.mult)
                fw.tt(q2.v(), xi, prb, ALU.mult, eng="pool")
                if neg_i:
                    fw.stt(o_i, q1.v(), -1.0, q2.v(), op0=ALU.mult, op1=ALU.subtract)
                else:
                    fw.tt(o_i, q1.v(), q2.v(), ALU.add)

            if getattr(self, 's5_stop', 99) <= 6:
                return P
            pr, pi = powers(kR)
            cprod(Bbr, Bbi, pr, pi, RTb[0], RTb[1], False)
            pr, pi = powers(kO2)
            cprod(Cr, Ci, pr, pi, O2b[0], O2b[1], True)
            pr, pi = powers(kO)
            cprod(Cr, Ci, pr, pi, P["Or"], P["Oni"], True)
            if getattr(self, 's5_stop', 99) <= 7:
                return P
            for part, Rt in ((0, P["Rr"]), (1, P["Ri"])):
                for gb in range(4):
                    pt = self.psT.next()
                    ptv = pt.v().re("p (g q) -> p g q", g=8)
                    for gi in range(8):
                        fw.transpose(ptv[:, gi, :], RTb[part][:, gb * 8 + gi, :], self.ident.v(), inc=(gi == 7))
                    fw.copy(Rt[:, gb * 8:(gb + 1) * 8, :], ptv, eng="act")
            if getattr(self, 's5_stop', 99) <= 8:
                return P
            maskf, maskb = t32("maskf", (128, 8, 16)), t32("maskb", (128, 8, 16))
            for m, cm, st, base in ((maskf, -1, 16, 15), (maskb, 1, -16, 0)):
                fw.memset(m.v(), 1.0, eng="pool")
                map_ = m.v().ap
                fw.op("pool", lambda e, map_=map_, cm=cm, st=st, base=base: e.affine_select(
                    out=map_, in_=map_, pattern=[[st, 8], [0, 16]], compare_op=ALU.is_ge, fill=0.0, base=base,
                    channel_multiplier=cm), reads=[m], writes=[m])
            dcol = t32("dcol")
            for s8 in range(8):
                fw.dma("sp", dcol[s8 * 16:(s8 + 1) * 16, :], self.w["s5_d"][l].re("g h -> h g"), allow_slow_non_contiguous=True)
            if getattr(self, 's5_stop', 99) <= 9:
                return P
            ta, tb = Rot([t32(f"Tta{i}", (128, 128)) for i in range(2)]), Rot([t32(f"Ttb{i}", (128, 128)) for i in range(2)])
            mf2, mb2 = maskf.v().re("p t h -> p (t h)"), maskb.v().re("p t h -> p (t h)")
            O2m = {}
            for d in range(2):
                for part in range(2):
                    t = fw.sbuf([128, 32, 128], BF16, f"O2m{d}{part}")
                    ln, other = slice(d * 64, (d + 1) * 64), slice((1 - d) * 64, (2 - d) * 64)
                    fw.copy(t[ln], O2b[part][ln], eng="pool")
                    fw.memset(t[other], 0.0, eng="pool")
                    O2m[(d, part)] = t
            for g in range(32):
                pss = [self.psA.next(), self.psA.next()]
                for d in range(2):
                    o = pss[d][:, 0:128]
                    fw.matmul(o, RTb[0][:, g, :], O2m[(d, 0)][:, g, :], start=True, stop=False)
                    fw.matmul(o, RTb[1][:, g, :], O2m[(d, 1)][:, g, :], start=False, stop=True)
                a_, b_ = ta.next(), tb.next()
                fw.tt(a_.v(), pss[0][:, 0:128], mf2, ALU.mult)
                fw.tt(b_.v(), pss[1][:, 0:128], mb2, ALU.mult)
                fw.tt(a_.v(), a_.v(), b_.v(), ALU.add, eng="pool")
                fw.stt(P["T"][:, g, :], self.identf.v(), dcol[:, g:g + 1], a_.v(), op0=ALU.mult, op1=ALU.add)
        return P

    def gelu_tanh(self, out_bf, x, t1, shape2d):
        fw = self.fw
        fw.act(t1, x, AF.Square)
        fw.ts(t1, t1, 0.044715, 1.0, op0=ALU.mult, op1=ALU.add)
        fw.tt(t1, t1, x, ALU.mult)
        fw.act(t1, t1, AF.Sigmoid, scale=2.0 * math.sqrt(2.0 / math.pi))
        fw.tt(out_bf, x, t1, ALU.mult)

    def phase_s5(self, l):
        fw = self.fw
        G = 4
        nmax = max(self.segs) // 8
        with fw.phase():
            P = self.s5_precompute(l)
            A = [fw.sbuf([128, G * nmax], F32, f"s5A{i}") for i in range(6)]
            cosT = fw.sbuf([128, G * nmax], F32, "s5cos")
            sinT = fw.sbuf([128, G * nmax], F32, "s5sin")
            rho = fw.sbuf([128, G * nmax], F32, "s5rho")
            tint = fw.sbuf([128, G * nmax], I32, "s5ti")
            cidx_i = fw.sbuf([128, nmax], I32, "s5cidxi")
            cidx = fw.sbuf([128, nmax], F32, "s5cidx")
            U = fw.sbuf([128, G, nmax], BF16, "s5U")
            S = [fw.sbuf([128, G * (nmax + 2)], BF16, f"s5S{i}") for i in range(2)]
            X = fw.sbuf([128, G * nmax], F32, "s5X")
            G1 = fw.sbuf([128, G * nmax], F32, "s5G1")
            Ysb = fw.sbuf([128, G, nmax], BF16, "s5Ysb")
            Dt = Rot([fw.sbuf([128, 8, G * 16], BF16, f"s5Dt{i}") for i in range(2)])
            Yt = Rot([fw.sbuf([128, 8, G * 16], BF16, f"s5Yt{i}") for i in range(2)])
            Dg = Rot([fw.sbuf([128, G, 128], BF16, f"s5Dg{i}") for i in range(2)])
            cap = cidx_i.v().ap
            fw.op("pool", lambda e: e.iota(cap, pattern=[[1, nmax]], base=0, channel_multiplier=0), writes=[cidx_i])
            fw.copy(cidx.v(), cidx_i.v())
            for si, L in enumerate(self.segs):
                tok0 = self.seg_start[si]
                n = L // 8
                ncb = n // 128

                def v3(t, n=n):
                    return t.v()[:, 0:G * n].re("p (g c) -> p g c", c=n)

                def v2(t, n=n):
                    return t.v()[:, 0:G * n]
                Sv = [s_.v()[:, 0:G * (n + 2)].re("p (g c) -> p g c", c=n + 2) for s_ in S]
                for gb in range(32 // G):
                    g0 = gb * G
                    for cb in range(ncb):
                        dt_ = Dt.next()
                        t0 = tok0 + cb * 1024
                        its = sorted({(t0) // TT, (t0 + 512) // TT})
                        src = V(self.r_zu[its[0]], self.zu.h.ap()[t0:t0 + 1024, g0 * 16:(g0 + G) * 16].rearrange("(c s) f -> c s f", s=8))
                        fw.dma(self.dq.next(), dt_.v(), src, extra_r=[self.r_zu[i] for i in its[1:]])
                        dg = Dg.next()
                        fw.copy(dg.v().re("p g (s h) -> p g s h", h=16), dt_.v().re("p s (g h) -> p g s h", h=16), eng="pool")
                        pt = self.psT.next()
                        ptv = pt.v()[:, 0:G * 128].re("p (g q) -> p g q", g=G)
                        for gi in range(G):
                            fw.transpose(ptv[:, gi, :], dg[:, gi, :], self.ident.v(), inc=(gi == G - 1))
                        fw.copy(U[:, :, cb * 128:(cb + 1) * 128], ptv, eng="act")
                    for part, Rt in ((0, P["Rr"]), (1, P["Ri"])):
                        Ev = v3(A[part])
                        for gi in range(G):
                            ps = self.psA.next()
                            fw.matmul(ps[:, 0:n], Rt[:, g0 + gi, :], U[:, gi, 0:n])
                            fw.copy(Ev[0:64, gi, :], ps[0:64, 0:n], eng="act")
                            fw.copy(Ev[64:128, gi, :][:, ::-1], ps[64:128, 0:n], eng="act")
                    ph = v3(A[2])
                    fw.tt(ph, P["th8"][:, g0:g0 + G].re("p (g o) -> p g o", o=1).bc([128, G, n]),
                          cidx[:, 0:n].re("p (o c) -> p o c", o=1).bc([128, G, n]), ALU.mult)
                    self.sincos(v2(A[2]), v2(sinT), v2(cosT), None, tmp=(v2(tint), v2(A[3]), v2(A[4])))
                    fw.copy(v3(rho), P["rho8"][:, g0:g0 + G].re("p (g o) -> p g o", o=1).bc([128, G, n]), eng="act")
                    fw.memset(v3(rho)[:, :, 0:1], 0.0, eng="pool")
                    fw.tt(v2(A[3]), v2(A[1]), v2(sinT), ALU.mult, eng="pool")
                    fw.tt(v2(A[2]), v2(A[0]), v2(cosT), ALU.mult)
                    fw.tt(v2(A[4]), v2(A[2]), v2(A[3]), ALU.add)
                    fw.tt(v2(A[2]), v2(A[1]), v2(cosT), ALU.mult)
                    fw.tt(v2(A[3]), v2(A[0]), v2(sinT), ALU.mult)
                    fw.tt(v2(A[5]), v2(A[2]), v2(A[3]), ALU.subtract)
                    fw.scan(v2(A[0]), v2(rho), v2(A[4]))
                    fw.scan(v2(A[1]), v2(rho), v2(A[5]))
                    for x_ in Sv:
                        fw.memset(x_[0:64, :, 0:1], 0.0, eng="pool")
                        fw.memset(x_[64:128, :, n - 1:n], 0.0, eng="pool")
                    fw.tt(v2(A[3]), v2(A[1]), v2(sinT), ALU.mult, eng="pool")
                    fw.tt(v2(A[2]), v2(A[0]), v2(cosT), ALU.mult)
                    fw.tt(Sv[0][0:64, :, 1:n], v3(A[2])[0:64, :, 0:n - 1], v3(A[3])[0:64, :, 0:n - 1], ALU.subtract)
                    fw.tt(Sv[0][64:128, :, 0:n - 1][:, :, ::-1], v3(A[2])[64:128, :, 0:n - 1], v3(A[3])[64:128, :, 0:n - 1], ALU.subtract)
                    fw.tt(v2(A[5]), v2(A[1]), v2(cosT), ALU.mult, eng="pool")
                    fw.tt(v2(A[4]), v2(A[0]), v2(sinT), ALU.mult)
                    fw.tt(Sv[1][0:64, :, 1:n], v3(A[4])[0:64, :, 0:n - 1], v3(A[5])[0:64, :, 0:n - 1], ALU.add)
                    fw.tt(Sv[1][64:128, :, 0:n - 1][:, :, ::-1], v3(A[4])[64:128, :, 0:n - 1], v3(A[5])[64:128, :, 0:n - 1], ALU.add)
                    Xv = v3(X)
                    for gi in range(G):
                        g = g0 + gi
                        ps = self.psA.next()
                        o = ps[:, 0:n]
                        fw.matmul(o, P["T"][:, g, :], U[:, gi, 0:n], start=True, stop=False)
                        fw.matmul(o, P["Or"][:, g, :], Sv[0][:, gi, 0:n], start=False, stop=False)
                        fw.matmul(o, P["Oni"][:, g, :], Sv[1][:, gi, 0:n], start=False, stop=True)
                        fw.copy(Xv[:, gi, :], o, eng="act")
                    self.gelu_tanh(Ysb.v()[:, :, 0:n], Xv, v3(G1), None)
                    for cb in range(ncb):
                        pt = self.psT.next()
                        ptv = pt.v()[:, 0:G * 128].re("p (g q) -> p g q", g=G)
                        for gi in range(G):
                            fw.transpose(ptv[:, gi, :], Ysb[:, gi, cb * 128:(cb + 1) * 128], self.ident.v(), inc=(gi == G - 1))
                        yt = Yt.next()
                        fw.copy(yt.v().re("p t (g h) -> p g t h", g=G), ptv.re("p g (t h) -> p g t h", h=16), eng="act")
                        t0 = tok0 + cb * 1024
                        its = sorted({(t0) // TT, (t0 + 512) // TT})
                        dst = V(self.r_ya[its[0]], self.ya.h.ap()[t0:t0 + 1024, g0 * 16:(g0 + G) * 16].rearrange("(c t) f -> c t f", t=8))
                        fw.dma("pool", dst, yt.v(), extra_w=[self.r_ya[i] for i in its[1:]])

    RW_Q = ("r", "v", "nkk", "kd0", "kd1", "b0", "b1", "g", "bonus", "lw0", "lw1")

    def rw_alloc_scratch(self):
        if hasattr(self, "rws"):
            return
        T = self.T
        self.rws = {}
        self.rws_reg = {}
        for q in self.RW_Q:
            dt = F32 if q in ("lw0", "lw1", "bonus") else BF16
            b = self.scratch("rw_" + q, [T, W], dt)
            self.rws[q] = b
            self.rws_reg[q] = [b.sub(f"rw_{q}_c{i}") for i in range(T // 128)]
        b = self.scratch("rw_o", [T, W], F32)
        self.rws["o"] = b
        self.rws_reg["o"] = [b.sub(f"rw_o_c{i}") for i in range(T // 128)]

    def rw_stage1(self, l):
        fw = self.fw
        self.rw_alloc_scratch()
        with fw.phase():
            def cols(name, n):
                t = fw.sbuf([128, n], F32, name)
                return t
            mu0, mu1, c0 = cols("mu0", 15), cols("mu1", 15), cols("c0", 15)
            fw.dma("sp", mu0.v(), self.w["rw_mu"][l][0].re("(c p) -> p c", p=128), allow_slow_non_contiguous=True)
            fw.dma("sp", mu1.v(), self.w["rw_mu"][l][1].re("(c p) -> p c", p=128), allow_slow_non_contiguous=True)
            fw.tt(c0.v(), mu0.v(), mu1.v(), ALU.add)
            fw.ts(c0.v(), c0.v(), -1.0, 1.0, op0=ALU.mult, op1=ALU.add)

            def bc_row(name, src):
                t = fw.sbuf([128, W], F32, name)
                self.load_bcast(t.v(), src)
                return t
            w0 = [bc_row(f"w0_{d}", self.w["rw_w0"][l][d]) for d in range(2)]
            a0 = [bc_row(f"a0_{d}", self.w["rw_a0"][l][d]) for d in range(2)]
            kkw = bc_row("kkw", self.w["rw_k_k"][l])
            kaw = bc_row("kaw", self.w["rw_k_a"][l])
            rkw = bc_row("rkw", self.w["rw_r_k"][l].re("h n -> (h n)"))
            kam1 = fw.sbuf([128, W], F32, "kam1")
            fw.ts(kam1.v(), kaw.v(), -1.0, 1.0, op0=ALU.mult, op1=ALU.add)
            w2b = fw.sbuf([128, W], BF16, "w2b")
            a2b = fw.sbuf([128, W], BF16, "a2b")
            g2b = fw.sbuf([128, W], BF16, "g2b")
            fw.dma("sp", w2b.v(), self.wb["rw_w2"][l].v())
            fw.dma("sp", a2b.v(), self.wb["rw_a2"][l].v())
            fw.dma("sp", g2b.v(), self.wb["rw_g2"][l].v())
            zt = Rot([fw.sbuf([128, 15, TT + 2], BF16, f"zt{i}") for i in range(2)])
            zs = fw.sbuf([128, 15, TT], BF16, "zs")
            tmp = Rot([fw.sbuf([128, TT], F32, f"rwtmp{i}") for i in range(2)])
            tw = fw.sbuf([128, TT], BF16, "tw")
            sgl = fw.sbuf([128, TT], BF16, "sgl")
            rkv = Rot([fw.sbuf([128, 3 * W], BF16, f"rkv{i}") for i in range(2)])
            f32t = {n: Rot([fw.sbuf([128, W], F32, f"{n}{i}") for i in range(2)]) for n in
                    ("kk", "a0t", "a1t", "lw0t", "lw1t", "wk", "wk2", "bon")}
            bft = {n: Rot([fw.sbuf([128, W], BF16, f"{n}{i}") for i in range(2)]) for n in
                   ("nkk", "kd0", "kd1", "b0", "b1", "g")}
            st = Rot([fw.sbuf([128, 32], F32, f"rwst{i}") for i in range(2)])
            for it in range(self.NT):
                si, blk0 = self.tile_pos[it]
                tok0, L = self.seg_start[si], self.segs[si]
                t0 = it * TT
                z = zt.next()
                lo, hi = t0 - 1, t0 + TT + 1
                first, last = (t0 == tok0), (t0 + TT == tok0 + L)
                a_ = 1 if first else 0
                b_ = TT + 1 if last else TT + 2
                regs = [self.r_zrwT[it]]
                if not first:
                    regs.append(self.r_zrwT[it - 1])
                if not last:
                    regs.append(self.r_zrwT[it + 1])
                if first:
                    fw.memset(z[:, :, 0:1], 0.0)
                if last:
                    fw.memset(z[:, :, TT + 1:TT + 2], 0.0)
                src = V(regs[0], self.zrwT.h.ap()[:, lo + a_:lo + b_].rearrange("(c p) t -> p c t", p=128))
                fw.dma(self.dq.next(), z[:, :, a_:b_], src, extra_r=regs[1:])
                for c in range(15):
                    t_ = tmp.next()
                    fw.act(t_.v(), z[:, c, 1:TT + 1], AF.Copy, scale=c0[:, c:c + 1])
                    fw.stt(t_.v(), z[:, c, 0:TT], mu0[:, c:c + 1], t_.v(), op0=ALU.mult, op1=ALU.add)
                    fw.stt(zs[:, c, :], z[:, c, 2:TT + 2], mu1[:, c:c + 1], t_.v(), op0=ALU.mult, op1=ALU.add)
                fw.act(tw.v(), zs[:, 12, :], AF.Tanh)
                fw.act(sgl.v(), zs[:, 14, :], AF.Sigmoid)
                for ts in range(4):
                    tsl = slice(ts * 128, (ts + 1) * 128)
                    ci = (t0 + ts * 128) // 128
                    rk = rkv.next()
                    for half in range(2):
                        pt = self.psT.next()
                        ncs = 8 if half == 0 else 4
                        ptv = pt.v()[:, 0:ncs * 128].re("p (c f) -> p c f", c=ncs)
                        for cc in range(ncs):
                            fw.transpose(ptv[:, cc, :], zs[:, half * 8 + cc, tsl], self.ident.v(), inc=(cc == ncs - 1))
                        fw.copy(rk[:, half * 1024:half * 1024 + ncs * 128], pt.v()[:, 0:ncs * 128], eng="act")
                    r_, k_, v_ = rk[:, 0:W], rk[:, W:2 * W], rk[:, 2 * W:3 * W]

                    def store(q, src_v, ci=ci):
                        dst = V(self.rws_reg[q][ci], self.rws[q].h.ap()[ci * 128:(ci + 1) * 128, :])
                        fw.dma("sp", dst, src_v)
                    store("r", r_)
                    store("v", v_)
                    at, lwt = [], []
                    for d in range(2):
                        ln = slice(d * 64, (d + 1) * 64)
                        ps = self.psA.next()
                        fw.matmul(ps.v(), tw[ln, tsl], w2b[ln, :])
                        lw_ = f32t[f"lw{d}t"].next()
                        fw.tt(lw_.v(), ps.v(), w0[d].v(), ALU.add)
                        fw.act(lw_.v(), lw_.v(), AF.Sigmoid)
                        fw.act(lw_.v(), lw_.v(), AF.Copy, scale=-math.exp(-0.5))
                        store(f"lw{d}", lw_.v())
                        ps = self.psA.next()
                        fw.matmul(ps.v(), zs[ln, 13, tsl], a2b[ln, :])
                        a_t = f32t[f"a{d}t"].next()
                        fw.tt(a_t.v(), ps.v(), a0[d].v(), ALU.add)
                        fw.act(a_t.v(), a_t.v(), AF.Sigmoid)
                        at.append(a_t)
                    ps = self.psA.next()
                    fw.matmul(ps.v(), sgl[:, tsl], g2b.v())
                    g_ = bft["g"].next()
                    fw.copy(g_.v(), ps.v(), eng="act")
                    store("g", g_.v())
                    kk = f32t["kk"].next()
                    wk, wk2 = f32t["wk"].next(), f32t["wk2"].next()
                    s_ = st.next()
                    fw.tt(kk.v(), k_, kkw.v(), ALU.mult)
                    fw.act(wk.v(), kk.v(), AF.Square)
                    fw.reduce(s_[:, 0:8], wk.v().re("p (h n) -> p h n", n=64))
                    fw.act(s_[:, 8:16], s_[:, 0:8], AF.Sqrt)
                    fw.ts(s_[:, 8:16], s_[:, 8:16], 1e-12, None, op0=ALU.max)
                    fw.recip(s_[:, 8:16], s_[:, 8:16])
                    kk3 = kk.v().re("p (h n) -> p h n", n=64)
                    fw.tt(kk3, kk3, s_[:, 8:16].re("p (h o) -> p h o", o=1).bc([128, 8, 64]), ALU.mult)
                    nkk = bft["nkk"].next()
                    fw.act(nkk.v(), kk.v(), AF.Copy, scale=-1.0)
                    store("nkk", nkk.v())
                    for d in range(2):
                        fw.tt(wk.v(), at[d].v(), kaw.v(), ALU.mult)
                        fw.tt(wk.v(), wk.v(), kam1.v(), ALU.add)
                        kd = bft[f"kd{d}"].next()
                        fw.tt(kd.v(), wk.v(), k_, ALU.mult)
                        store(f"kd{d}", kd.v())
                        b_t = bft[f"b{d}"].next()
                        fw.tt(b_t.v(), kk.v(), at[d].v(), ALU.mult, eng="pool")
                        store(f"b{d}", b_t.v())
                    fw.tt(wk2.v(), r_, k_, ALU.mult, eng="pool")
                    fw.tt(wk2.v(), wk2.v(), rkw.v(), ALU.mult, eng="pool")
                    fw.reduce(s_[:, 16:24], wk2.v().re("p (h n) -> p h n", n=64))
                    bon = f32t["bon"].next()
                    fw.tt(bon.v().re("p (h n) -> p h n", n=64), v_.re("p (h n) -> p h n", n=64),
                          s_[:, 16:24].re("p (h o) -> p h o", o=1).bc([128, 8, 64]), ALU.mult)
                    store("bonus", bon.v())

    def rw_stage2(self, l):
        fw = self.fw
        with fw.phase():
            def f32(name, shape):
                return fw.sbuf(list(shape), F32, name)

            def bf(name, shape):
                return fw.sbuf(list(shape), BF16, name)
            masks = {}
            for nm, cm, st, base, cmp_ in (("LE", -1, 1, 0, ALU.is_ge), ("GE", 1, -1, 0, ALU.is_ge),
                                           ("LT", -1, 1, 0, ALU.is_gt), ("GT", 1, -1, 0, ALU.is_gt)):
                m = f32("mask" + nm, (128, 128))
                fw.memset(m.v(), 1.0, eng="pool")
                map_ = m.v().ap
                fw.op("pool", lambda e, map_=map_, cm=cm, st=st, base=base, cmp_=cmp_: e.affine_select(
                    out=map_, in_=map_, pattern=[[st, 128]], compare_op=cmp_, fill=0.0, base=base, channel_multiplier=cm),
                    reads=[m], writes=[m])
                masks[nm] = m
            ones2 = f32("ones2", (128, 2))
            fw.memset(ones2.v(), 1.0)
            lnw = f32("lnw", (128, W))
            lnb = f32("lnb", (128, W))
            self.load_bcast(lnw.v(), self.w["rw_ln_w"][l])
            self.load_bcast(lnb.v(), self.w["rw_ln_b"][l])
            inb = {q: SlotPool([bf(f"in_{q}{i}", (128, W)) for i in range(RW_WINDOW)]) for q in ("r", "v", "nkk", "kd", "b")}
            inlw = SlotPool([f32(f"in_lw{i}", (128, W)) for i in range(RW_WINDOW)])
            gam = SlotPool([f32(f"gam{i}", (128, W)) for i in range(RW_WINDOW)])
            ginv = SlotPool([f32(f"ginv{i}", (128, W)) for i in range(RW_WINDOW)])
            til = {q: SlotPool([bf(f"til_{q}{i}", (128, W)) for i in range(RW_WINDOW)]) for q in ("r", "k", "b", "a")}

            def padded(name, n):
                items = []
                for i in range(n):
                    ta_, tb_ = bf(f"{name}A{i}", (128, 4, 128)), bf(f"{name}B{i}", (128, 4, 128))
                    fw.memset(ta_[64:128], 0.0, eng="pool")
                    fw.memset(tb_[0:64], 0.0, eng="pool")
                    items.append((ta_, tb_))
                return SlotPool(items)
            tT = {q: padded(f"tT_{q}", RW_WINDOW) for q in ("r", "k", "b", "a")}
            WT = padded("WT", RW_WINDOW)
            gC = SlotPool([f32(f"gC{i}", (128, 8)) for i in range(RW_WINDOW)])
            sc = {q: SlotPool([bf(f"sc_{q}{i}", (128, 8, 128)) for i in range(RW_WINDOW)]) for q in ("Ak", "Arb", "Ark")}
            Mr_s = [Rot([bf(f"Mpow{w_}_{i}", (128, 8, 128)) for i in range(3)]) for w_ in range(RW_WINDOW)]
            Ar_s = [Rot([bf(f"Apow{w_}_{i}", (128, 8, 128)) for i in range(3)]) for w_ in range(RW_WINDOW)]
            Pb = SlotPool([bf(f"Pb{i}", (128, 8, 128)) for i in range(RW_WINDOW)])
            X0 = SlotPool([bf(f"X0{i}", (128, W)) for i in range(RW_WINDOW)])
            U0 = SlotPool([bf(f"U0{i}", (128, W)) for i in range(RW_WINDOW)])
            Ub = SlotPool([bf(f"Ub{i}", (128, W)) for i in range(RW_WINDOW)])
            ost = Rot([f32(f"ost{i}", (128, W)) for i in range(1)])
            oin = Rot([f32(f"oin{i}", (128, W)) for i in range(1)])
            bonin = Rot([f32(f"bonin{i}", (128, W)) for i in range(1)])
            gin = Rot([bf(f"gin{i}", (128, W)) for i in range(2)])
            fy = Rot([f32(f"fy{i}", (128, W)) for i in range(1)])
            fsq = Rot([f32(f"fsq{i}", (128, W)) for i in range(1)])
            fst = Rot([f32(f"fst{i}", (128, 32)) for i in range(2)])
            yout = Rot([bf(f"ycout{i}", (128, W)) for i in range(2)])
            tmpH = f32("tmpH", (128, 4, 128))
            chains = {}
            for si in range(len(self.segs)):
                for d in range(2):
                    Hf_ = f32(f"Hf_{si}_{d}", (128, 4, 128))
                    Hb_ = bf(f"Hb_{si}_{d}", (128, 4, 128))
                    fw.memset(Hf_.v(), 0.0)
                    fw.memset(Hb_.v(), 0.0)
                    chains[(si, d)] = (Hf_, Hb_)
            identf3 = self.identf.v().re("p (o t) -> p o t", o=1)
            def evac(dst, src):
                fw.copy(dst, src, eng="act")

            def chunk(si, d, c, first, slot):
                Mr, Ar = Mr_s[slot], Ar_s[slot]
                ci = self.seg_start[si] // 128 + c
                Hf_, Hb_ = chains[(si, d)]
                rows = slice(ci * 128, (ci + 1) * 128)

                def load(q, buf, qn=None):
                    qn = qn or q
                    fw.dma(self.dq.next(), buf.v(), V(self.rws_reg[qn][ci], self.rws[qn].h.ap()[rows, :]))
                r_, v_, nkk_, kd_, b_, lw_ = (inb["r"].next(), inb["v"].next(), inb["nkk"].next(), inb["kd"].next(),
                                             inb["b"].next(), inlw.next())
                load("r", r_)
                load("v", v_)
                load("nkk", nkk_)
                load("kd", kd_, f"kd{d}")
                load("b", b_, f"b{d}")
                load("lw", lw_, f"lw{d}")
                yield
                mLE, mLT, mGT = (masks["LE"], masks["LT"], masks["GT"]) if d == 0 else (masks["GE"], masks["GT"], masks["LT"])
                pc = self.psA.next()
                fw.matmul(pc.v(), mLE.v(), lw_.v())
                g_, gi_ = gam.next(), ginv.next()
                tr, tk, tb, ta = til["r"].next(), til["k"].next(), til["b"].next(), til["a"].next()
                fw.act(g_.v(), pc.v(), AF.Exp)
                fw.act(gi_.v(), lw_.v(), AF.Exp, scale=-1.0)
                fw.tt(tr.v(), r_.v(), g_.v(), ALU.mult)
                fw.tt(gi_.v(), gi_.v(), g_.v(), ALU.mult)
                fw.tt(ta.v(), nkk_.v(), gi_.v(), ALU.mult)
                fw.act(gi_.v(), pc.v(), AF.Exp, scale=-1.0)
                fw.tt(tb.v(), b_.v(), gi_.v(), ALU.mult)
                fw.tt(tk.v(), kd_.v(), gi_.v(), ALU.mult)
                pg = self.psB.next()
                for q in range(4):
                    fw.matmul(pg[:, 2 * q:2 * q + 2], lw_[:, q * 128:(q + 1) * 128], ones2.v())
                gc = gC.next()
                fw.act(gc.v(), pg[:, 0:8], AF.Exp)
                yield
                T_ = {}
                for q, src in (("a", ta), ("b", tb), ("k", tk), ("r", tr)):
                    pt = self.psT.next()
                    ptv = pt[:, 0:512].re("p (q t) -> p q t", q=4)
                    for pr_ in range(4):
                        fw.transpose(ptv[:, pr_, :], src[:, pr_ * 128:(pr_ + 1) * 128], self.ident.v(), inc=(pr_ == 3))
                    TA_, TB_ = tT[q].next()
                    fw.copy(TA_[0:64], ptv[0:64], eng="act")
                    fw.copy(TB_[64:128], ptv[64:128], eng="act")
                    T_[q] = (TA_, TB_)

                def Th(q, h):
                    return T_[q][h % 2][:, h // 2, :]
                yield
                M0, A0 = Mr.next(), Ar.next()
                AkT, ArbT, ArkT = sc["Ak"].next(), sc["Arb"].next(), sc["Ark"].next()
                for dst, lh, rh, mk in ((M0, "b", "a", mLT), (A0, "a", "b", mGT), (AkT, "k", "a", mLT),
                                        (ArbT, "b", "r", mLE), (ArkT, "k", "r", mLE)):
                    for hg in range(2):
                        ps = self.psA.next()
                        for hh in range(4):
                            h = hg * 4 + hh
                            fw.matmul(ps[:, hh * 128:(hh + 1) * 128], Th(lh, h), Th(rh, h))
                        fw.tt(dst[:, hg * 4:(hg + 1) * 4, :], ps.v().re("p (h t) -> p h t", h=4),
                              mk.v().re("p (o t) -> p o t", o=1).bc([128, 4, 128]), ALU.mult)
                yield
                pb = Pb.next()
                fw.tt(pb.v(), M0.v(), identf3.bc([128, 8, 128]), ALU.add)
                Mp, Ap = M0, A0
                for k in range(1, 7):
                    Mn = Mr.next() if k < 6 else None
                    An = Ar.next()
                    for hg in range(2):
                        hs = slice(hg * 4, (hg + 1) * 4)
                        if k < 6:
                            ps = self.psA.next()
                            for hh in range(4):
                                h = hg * 4 + hh
                                fw.matmul(ps[:, hh * 128:(hh + 1) * 128], Ap[:, h, :], Mp[:, h, :])
                            evac(Mn[:, hs, :], ps.v().re("p (h t) -> p h t", h=4))
                        ps = self.psA.next()
                        for hh in range(4):
                            h = hg * 4 + hh
                            fw.matmul(ps[:, hh * 128:(hh + 1) * 128], Mp[:, h, :], Ap[:, h, :])
                        evac(An[:, hs, :], ps.v().re("p (h t) -> p h t", h=4))
                    for hg in range(2):
                        hs = slice(hg * 4, (hg + 1) * 4)
                        ps = self.psA.next()
                        for hh in range(4):
                            h = hg * 4 + hh
                            fw.matmul(ps[:, hh * 128:(hh + 1) * 128], An[:, h, :], pb[:, h, :])
                        fw.tt(pb[:, hs, :], pb[:, hs, :], ps.v().re("p (h t) -> p h t", h=4), ALU.add)
                    Mp, Ap = Mn, An
                    yield
                WTA, WTB = WT.next()
                for par, wt_ in ((0, WTA), (1, WTB)):
                    ps = self.psA.next()
                    for q in range(4):
                        h = 2 * q + par
                        fw.matmul(ps[:, q * 128:(q + 1) * 128], ta[:, q * 128:(q + 1) * 128], pb[:, h, :])
                    ln = slice(par * 64, (par + 1) * 64)
                    evac(wt_[ln], ps[ln, :].re("p (q t) -> p q t", q=4))
                wts = (WTA, WTB)
                yield
                x0, u0 = X0.next(), U0.next()
                ps = self.psA.next()
                for h in range(8):
                    hs = slice(h * 64, (h + 1) * 64)
                    fw.matmul(ps[:, hs], AkT[:, h, :], v_[:, hs])
                evac(x0.v(), ps.v())
                ps = self.psA.next()
                for h in range(8):
                    hs = slice(h * 64, (h + 1) * 64)
                    fw.matmul(ps[:, hs], pb[:, h, :], x0[:, hs])
                evac(u0.v(), ps.v())
                yield
                pred = (si, d, c - 1) if d == 0 else (si, d, c + 1)
                if 0 <= pred[2] < nch[si]:
                    while pred not in done:
                        yield
                ub = Ub.next()
                ps = self.psA.next()
                for h in range(8):
                    hs = slice(h * 64, (h + 1) * 64)
                    q, par = h // 2, h % 2
                    fw.matmul(ps[:, hs], wts[par][:, q, :], Hb_[:, q, par * 64:(par + 1) * 64])
                fw.tt(ub.v(), ps.v(), u0.v(), ALU.add)
                if not first:
                    while (si, 1 - d, c) not in done:
                        yield
                else:
                    yield
                py = self.psA.next()
                for h in range(8):
                    hs = slice(h * 64, (h + 1) * 64)
                    q, par = h // 2, h % 2
                    fw.matmul(py[:, hs], Th("r", h), Hb_[:, q, par * 64:(par + 1) * 64], start=True, stop=False)
                    fw.matmul(py[:, hs], ArbT[:, h, :], ub[:, hs], start=False, stop=False)
                    fw.matmul(py[:, hs], ArkT[:, h, :], v_[:, hs], start=False, stop=True)
                ph = self.psB.next()
                for q in range(4):
                    qs = slice(q * 128, (q + 1) * 128)
                    fw.matmul(ph[:, qs], tb[:, qs], ub[:, qs], start=True, stop=False)
                    fw.matmul(ph[:, qs], tk[:, qs], v_[:, qs], start=False, stop=True)
                fw.tt(tmpH.v(), ph.v().re("p (q i) -> p q i", q=4), Hf_.v(), ALU.add)
                fw.tt(Hf_.v(), tmpH.v(), gc.v().re("p (q two) -> p q two", two=2)[:, :, 0:1].bc([128, 4, 128]), ALU.mult)
                fw.copy(Hb_.v(), Hf_.v(), eng="act")
                oreg = self.rws_reg["o"][ci]
                odr = V(oreg, self.rws["o"].h.ap()[rows, :])
                if first:
                    o_ = ost.next()
                    fw.copy(o_.v(), py.v(), eng="act")
                    fw.dma("pool", odr, o_.v())
                    done.add((si, d, c))
                    return
                oi, bi, gi2 = oin.next(), bonin.next(), gin.next()
                fw.dma(self.dq.next(), oi.v(), odr)
                fw.dma(self.dq.next(), bi.v(), V(self.rws_reg["bonus"][ci], self.rws["bonus"].h.ap()[rows, :]))
                fw.dma(self.dq.next(), gi2.v(), V(self.rws_reg["g"][ci], self.rws["g"].h.ap()[rows, :]))
                y, sq, st = fy.next(), fsq.next(), fst.next()
                fw.tt(y.v(), py.v(), oi.v(), ALU.add)
                y3 = y.v().re("p (h n) -> p h n", n=64)
                fw.reduce(st[:, 0:8], y3)
                fw.ts(st[:, 0:8], st[:, 0:8], 1.0 / 64, None, op0=ALU.mult)
                fw.tt(y3, y3, st[:, 0:8].re("p (h o) -> p h o", o=1).bc([128, 8, 64]), ALU.subtract)
                fw.act(sq.v(), y.v(), AF.Square)
                fw.reduce(st[:, 8:16], sq.v().re("p (h n) -> p h n", n=64))
                fw.act(st[:, 16:24], st[:, 8:16], AF.Sqrt, scale=1.0 / 64, bias=self.eps_t[:, 1:2])
                fw.recip(st[:, 16:24], st[:, 16:24])
                fw.tt(y3, y3, st[:, 16:24].re("p (h o) -> p h o", o=1).bc([128, 8, 64]), ALU.mult)
                fw.tt(y.v(), y.v(), lnw.v(), ALU.mult)
                fw.tt(y.v(), y.v(), lnb.v(), ALU.add, eng="pool")
                fw.tt(y.v(), y.v(), bi.v(), ALU.add, eng="pool")
                yo = yout.next()
                fw.tt(yo.v(), y.v(), gi2.v(), ALU.mult)
                it = (ci * 128) // TT
                fw.dma("pool", V(self.r_yc[it], self.yc.h.ap()[rows, :]), yo.v())
                done.add((si, d, c))

            done = set()
            nch = [L // 128 for L in self.segs]
            todo = []
            for step in range(max(nch)):
                for si, n in enumerate(nch):
                    if step >= n:
                        continue
                    first = step < n - 1 - step
                    todo.append((si, 0, step, first))
                    todo.append((si, 1, n - 1 - step, first))
            active = {}
            pos = 0
            while pos < len(todo) or active:
                for slot in range(RW_WINDOW):
                    if slot not in active and pos < len(todo):
                        active[slot] = chunk(*todo[pos], slot)
                        pos += 1
                for slot in sorted(active):
                    SlotPool.cur[0] = slot
                    try:
                        next(active[slot])
                    except StopIteration:
                        del active[slot]

    def phase_rw(self, l):
        self.rw_stage1(l)
        self.rw_stage2(l)

    def zero_fill(self, dst, regs):
        fw = self.fw
        with fw.phase():
            z = fw.sbuf([128, 4, W], BF16, "zfill")
            fw.memset(z.v(), 0.0)
            for it in range(self.NT):
                t0 = it * TT
                fw.dma("pool", V(regs[it], dst.h.ap()[t0:t0 + TT, :].rearrange("(s p) c -> p s c", p=128)), z.v())

    def build_all(self):
        self.prep_weights(0, "A")
        self.prep_weights(0, "B")
        for l in range(self.depth):
            last = (l == self.depth - 1)
            if l == 0:
                self.phase_p1(l, self.x_in, self.r_xin)
            else:
                self.phase_p1(l, self.xcur, self.r_xcur)
            if not last:
                self.prep_weights(l + 1)
            self.phase_attn(l)
            if hasattr(self, "phase_s5"):
                self.phase_s5(l)
            else:
                self.zero_fill(self.ya, self.r_ya)
            if hasattr(self, "phase_rw"):
                self.phase_rw(l)
            else:
                self.zero_fill(self.yc, self.r_yc)
            if last:
                self.phase_p3(l, self.y_out, self.r_yout)
            else:
                self.phase_p3(l, self.xcur, self.r_xcur)

    def finish(self):
        fw = self.fw
        fw.barrier()
        fw.emit()


def host_inputs_for_core(inputs, x_core):
    m = {"x": np.ascontiguousarray(x_core, dtype=np.float32)}
    for n in WSPEC:
        m[n] = np.ascontiguousarray(np.asarray(inputs[n]), dtype=np.float32)
    return m


N_CORES = 8
SEGS = [2048, 2048, 4096]
_PROG = None


def _get_prog():
    global _PROG
    if _PROG is None:
        P = Prog(SEGS, depth=DEPTH)
        P.init_consts()
        P.build_all()
        P.finish()
        _PROG = P
    return _PROG


def kernel(**inputs):
    xp = np.asarray(inputs["x_prompt"], dtype=np.float32)
    xs = np.asarray(inputs["x_sample"], dtype=np.float32)
    P = _get_prog()
    in_maps = []
    for c in range(N_CORES):
        xc = np.concatenate([xp[2 * c], xp[2 * c + 1], xs[c]], axis=0)
        in_maps.append(host_inputs_for_core(inputs, xc))
    res = run_bass_kernel_spmd(P.nc, in_maps, core_ids=list(range(N_CORES)))
    yp = np.empty_like(xp)
    ys = np.empty_like(xs)
    for c in range(N_CORES):
        y = np.asarray(res.results[c]["y"], dtype=np.float32)
        yp[2 * c] = y[0:2048]
        yp[2 * c + 1] = y[2048:4096]
        ys[c] = y[4096:8192]
    return (yp, ys)
```

```python
import contextlib
import math
import numpy as np
import concourse.bass as bass
import concourse.mybir as mybir
from concourse.bass_utils import run_bass_kernel_spmd

F32 = mybir.dt.float32
BF16 = mybir.dt.bfloat16
I32 = mybir.dt.int32
AF = mybir.ActivationFunctionType
ALU = mybir.AluOpType
AX = mybir.AxisListType

ENGS = ("pe", "dve", "act", "pool", "sp")
N_DMA_SEM = 8
RW_WINDOW = 3
SELF_SYNC = True

D = 1024
DFF = 2816
NFF = DFF // 128
W = 512
NIN = 6272
DEPTH = 2
RMS_EPS = 1e-6
TT = 512
C_U, C_Q, C_K, C_V, C_RW, C_G = 0, 512, 1024, 1152, 1280, 3200
N_RW = 1920
ROPE_THETA = 500000.0


class V:
    __slots__ = ("buf", "ap")

    def __init__(self, buf, ap):
        self.buf = buf
        self.ap = ap

    def __getitem__(self, idx):
        return V(self.buf, self.ap[idx])

    def re(self, pat, **kw):
        return V(self.buf, self.ap.rearrange(pat, **kw))

    def bc(self, shape):
        return V(self.buf, self.ap.broadcast_to(list(shape)))

    def pbc(self, n):
        return V(self.buf, self.ap.partition_broadcast(n))


class Buf:
    __slots__ = ("h", "name", "last_w", "readers")

    def __init__(self, h, name):
        self.h = h
        self.name = name
        self.last_w = None
        self.readers = []

    def __getitem__(self, idx):
        return V(self, self.h.ap()[idx])

    def v(self):
        return V(self, self.h.ap())

    def sub(self, name=None):
        return Buf(self.h, name or self.name + "_sub")


class Fw:
    def __init__(self, nc):
        self.nc = nc
        self.sem = {e: nc.alloc_semaphore("s_" + e) for e in ENGS}
        self.cnt = {e: 0 for e in ENGS}
        self.seen = {e: {} for e in ENGS}
        self.ops = {e: [] for e in ENGS}
        self.semobj = {("e", e): self.sem[e] for e in ENGS}
        self.dcnt = {}
        for q in ENGS:
            for i in range(N_DMA_SEM):
                self.semobj[("d", q, i)] = nc.alloc_semaphore(f"d_{q}_{i}")
            self.dcnt[q] = 0
        self.nbuf = 0
        self.n_ops = 0
        self.stack = None

    def sbuf(self, shape, dt, name=None):
        self.nbuf += 1
        name = f"{name or 'sb'}_{self.nbuf}"
        if self.stack is not None:
            h = self.stack.enter_context(self.nc.sbuf_tensor(name, list(shape), dt))
        else:
            h = self.nc.alloc_sbuf_tensor(name, list(shape), dt)
        return Buf(h, name)

    def psum(self, shape, dt, name=None):
        self.nbuf += 1
        name = name or f"ps{self.nbuf}"
        return Buf(self.nc.alloc_psum_tensor(name, list(shape), dt), name)

    def dram(self, shape, dt, name, kind="Internal"):
        return Buf(self.nc.dram_tensor(name, list(shape), dt, kind=kind), name)

    @contextlib.contextmanager
    def phase(self):
        old = self.stack
        with contextlib.ExitStack() as st:
            self.stack = st
            yield
            self.barrier()
        self.stack = old

    def _waits(self, eng, reads, writes, skip_self=True):
        need = {}
        seen = self.seen[eng]

        def add(ev):
            if ev is None:
                return
            k, v = ev
            if skip_self and k == ("e", eng) and (eng == "pe" or not SELF_SYNC):
                return
            if seen.get(k, 0) >= v:
                return
            if need.get(k, 0) < v:
                need[k] = v
        for b in reads:
            add(b.last_w)
        for b in writes:
            add(b.last_w)
            for r in b.readers:
                add(r)
        for k, v in need.items():
            seen[k] = v
        return list(need.items())

    def _commit(self, ev, reads, writes):
        for b in reads:
            rd = b.readers
            rd.append(ev)
            if len(rd) > 48:
                d = {}
                for k, v in rd:
                    if d.get(k, 0) < v:
                        d[k] = v
                b.readers = list(d.items())
        for b in writes:
            b.last_w = ev
            b.readers = []

    def op(self, eng, fn, reads=(), writes=(), inc=True):
        waits = self._waits(eng, reads, writes)
        if inc:
            self.cnt[eng] += 1
            val = self.cnt[eng]
        else:
            val = self.cnt[eng] + 1
        ev = (("e", eng), val)
        self.ops[eng].append((waits, fn, ("e", eng) if inc else None))
        self._commit(ev, reads, writes)
        self.n_ops += 1
        return ev

    def dma(self, q, out, in_, extra_r=(), extra_w=(), **kw):
        i = self.dcnt[q]
        self.dcnt[q] += 1
        slot = i % N_DMA_SEM
        key = ("d", q, slot)
        val = 16 * (i // N_DMA_SEM + 1)
        reads, writes = [in_.buf] + list(extra_r), [out.buf] + list(extra_w)
        waits = self._waits(q, reads, writes, skip_self=False)
        prev = val - 16
        if prev > 0 and self.seen[q].get(key, 0) < prev:
            waits.append((key, prev))
            self.seen[q][key] = prev
        oap, iap = out.ap, in_.ap

        def fn(e):
            return e.dma_start(out=oap, in_=iap, **kw)
        self.ops[q].append((waits, fn, key))
        ev = (key, val)
        self._commit(ev, reads, writes)
        self.n_ops += 1
        return ev

    def barrier(self):
        evs = []
        for e in ENGS:
            if self.cnt[e] > 0:
                evs.append((("e", e), self.cnt[e]))
        for q in ENGS:
            n = self.dcnt[q]
            for slot in range(N_DMA_SEM):
                if n > slot:
                    cntslot = (n - 1 - slot) // N_DMA_SEM + 1
                    evs.append((("d", q, slot), 16 * cntslot))
        for e in ENGS:
            waits = []
            for k, v in evs:
                if k == ("e", e):
                    continue
                if self.seen[e].get(k, 0) < v:
                    waits.append((k, v))
                    self.seen[e][k] = v
            if waits:
                self.ops[e].append((waits, None, None))

    def emit(self):
        nc = self.nc
        with nc.Block() as block:
            def mk(e):
                def body(engh):
                    for waits, fn, inckey in self.ops[e]:
                        for k, v in waits:
                            engh.wait_ge(self.semobj[k], v)
                        if fn is None:
                            continue
                        ins = fn(engh)
                        if inckey is not None:
                            ins.then_inc(self.semobj[inckey], 1 if inckey[0] == "e" else 16)
                return body
            block.tensor(mk("pe"))
            block.vector(mk("dve"))
            block.scalar(mk("act"))
            block.gpsimd(mk("pool"))
            block.sync(mk("sp"))

    def matmul(self, out, lhsT, rhs, start=True, stop=True, inc=None):
        if inc is None:
            inc = stop
        o, l, r = out.ap, lhsT.ap, rhs.ap
        return self.op("pe", lambda e: e.matmul(o, lhsT=l, rhs=r, start=start, stop=stop),
                       reads=[lhsT.buf, rhs.buf], writes=[out.buf], inc=inc)

    def transpose(self, out, in_, ident, inc=True):
        o, i, d = out.ap, in_.ap, ident.ap
        return self.op("pe", lambda e: e.transpose(o, i, d), reads=[in_.buf, ident.buf],
                       writes=[out.buf], inc=inc)

    def act(self, out, in_, func, bias=None, scale=None, accum_out=None, eng="act"):
        o, i = out.ap, in_.ap
        reads = [in_.buf]
        writes = [out.buf]
        kw = {}
        if bias is not None:
            if isinstance(bias, V):
                kw["bias"] = bias.ap
                reads.append(bias.buf)
            else:
                kw["bias"] = bias
        if scale is not None:
            if isinstance(scale, V):
                kw["scale"] = scale.ap
                reads.append(scale.buf)
            else:
                kw["scale"] = scale
        if accum_out is not None:
            kw["accum_out"] = accum_out.ap
            writes.append(accum_out.buf)
        return self.op(eng, lambda e: e.activation(out=o, in_=i, func=func, **kw), reads=reads, writes=writes)

    def tt(self, out, in0, in1, op, eng="dve"):
        o, a, b = out.ap, in0.ap, in1.ap
        return self.op(eng, lambda e: e.tensor_tensor(out=o, in0=a, in1=b, op=op),
                       reads=[in0.buf, in1.buf], writes=[out.buf])

    def ts(self, out, in0, s1, s2=None, op0=ALU.mult, op1=None, eng="dve", accum_out=None):
        o, a = out.ap, in0.ap
        reads = [in0.buf]
        writes = [out.buf]
        if isinstance(s1, V):
            reads.append(s1.buf)
            s1 = s1.ap
        if isinstance(s2, V):
            reads.append(s2.buf)
            s2 = s2.ap
        kw = {}
        if op1 is not None:
            kw["op1"] = op1
        if accum_out is not None:
            kw["accum_out"] = accum_out.ap
            writes.append(accum_out.buf)
        return self.op(eng, lambda e: e.tensor_scalar(out=o, in0=a, scalar1=s1, scalar2=s2, op0=op0, **kw),
                       reads=reads, writes=writes)

    def stt(self, out, in0, scalar, in1, op0, op1, eng="dve"):
        o, a, b = out.ap, in0.ap, in1.ap
        reads = [in0.buf, in1.buf]
        if isinstance(scalar, V):
            reads.append(scalar.buf)
            scalar = scalar.ap
        return self.op(eng, lambda e: e.scalar_tensor_tensor(out=o, in0=a, scalar=scalar, in1=b, op0=op0, op1=op1),
                       reads=reads, writes=[out.buf])

    def copy(self, out, in_, eng="dve"):
        o, i = out.ap, in_.ap
        if eng == "act":
            return self.op(eng, lambda e: e.copy(out=o, in_=i), reads=[in_.buf], writes=[out.buf])
        return self.op(eng, lambda e: e.tensor_copy(out=o, in_=i), reads=[in_.buf], writes=[out.buf])

    def memset(self, out, val, eng="dve"):
        o = out.ap
        return self.op(eng, lambda e: e.memset(o, val), writes=[out.buf])

    def reduce(self, out, in_, op=ALU.add, axis=AX.X, eng="dve"):
        o, i = out.ap, in_.ap
        return self.op(eng, lambda e: e.tensor_reduce(out=o, in_=i, axis=axis, op=op), reads=[in_.buf], writes=[out.buf])

    def scan(self, out, data0, data1, initial=0.0, op0=ALU.mult, op1=ALU.add):
        o, a, b = out.ap, data0.ap, data1.ap
        return self.op("dve", lambda e: e.tensor_tensor_scan(out=o, data0=a, data1=b, initial=initial, op0=op0, op1=op1),
                       reads=[data0.buf, data1.buf], writes=[out.buf])

    def recip(self, out, in_):
        o, i = out.ap, in_.ap
        return self.op("dve", lambda e: e.reciprocal(out=o, in_=i), reads=[in_.buf], writes=[out.buf])


WSPEC = {
    'ffn1_norm': (D,), 'ffn1_w_gate': (D, DFF), 'ffn1_w_up': (D, DFF), 'ffn1_w_down': (DFF, D),
    'mix_norm': (D,), 'w_in': (D, NIN),
    's5_lam_re': (2, 32, 64), 's5_lam_im': (2, 32, 64), 's5_log_step': (2, 32),
    's5_b_re': (2, 32, 64, 16), 's5_b_im': (2, 32, 64, 16), 's5_c_re': (2, 32, 16, 64), 's5_c_im': (2, 32, 16, 64),
    's5_d': (32, 16), 's5_w_glu': (W, W), 'q_norm': (64,), 'k_norm': (64,), 'attn_sink': (8,),
    'rw_mu': (2, N_RW), 'rw_w0': (2, W), 'rw_w2': (2, 64, W), 'rw_a0': (2, W), 'rw_a2': (2, 64, W),
    'rw_g2': (128, W), 'rw_k_k': (W,), 'rw_k_a': (W,), 'rw_r_k': (8, 64), 'rw_ln_w': (W,), 'rw_ln_b': (W,),
    'w_branch': (3, W, D), 'w_out': (D, D),
    'ffn2_norm': (D,), 'ffn2_w_gate': (D, DFF), 'ffn2_w_up': (D, DFF), 'ffn2_w_down': (DFF, D),
}
BF_W = {
    'ffn1_w_gate': (D, DFF), 'ffn1_w_up': (D, DFF), 'ffn1_w_down': (DFF, D),
    'ffn2_w_gate': (D, DFF), 'ffn2_w_up': (D, DFF), 'ffn2_w_down': (DFF, D),
    'w_in': (D, NIN), 's5_w_glu': (W, W), 'w_branch': (3 * W, D), 'w_out': (D, D),
    'rw_w2': (128, W), 'rw_a2': (128, W), 'rw_g2': (128, W),
}


class Rot:
    def __init__(self, items):
        self.items = items
        self.i = 0

    def next(self):
        b = self.items[self.i % len(self.items)]
        self.i += 1
        return b


class SlotPool:
    cur = [0]

    def __init__(self, items):
        self.items = items

    def next(self):
        return self.items[SlotPool.cur[0] % len(self.items)]


class Prog:
    def __init__(self, segs, depth=DEPTH, debug=()):
        self.segs = list(segs)
        self.T = sum(segs)
        assert all(s % TT == 0 for s in segs)
        self.NT = self.T // TT
        self.depth = depth
        self.debug = set(debug)
        self.nc = bass.Bass("TRN2", target_bir_lowering=False)
        self.fw = Fw(self.nc)
        fw = self.fw
        T = self.T
        self.tile_pos = []
        for si, L in enumerate(self.segs):
            for k in range(L // TT):
                self.tile_pos.append((si, k * (TT // 128)))
        self.seg_start = np.cumsum([0] + self.segs).tolist()
        self.x_in = fw.dram([T, D], F32, "x", kind="ExternalInput")
        self.y_out = fw.dram([T, D], F32, "y", kind="ExternalOutput")
        self.w = {n: fw.dram([DEPTH] + list(s), F32, n, kind="ExternalInput") for n, s in WSPEC.items()}
        self.wb = {n: [fw.dram(list(s), BF16, f"bf_{n}_{l}") for l in range(depth)] for n, s in BF_W.items()
                   if not (n.startswith("ffn") or n == "w_in")}
        self.wt = {}
        for f_ in ("ffn1", "ffn2"):
            self.wt[f"{f_}_w_gate"] = [fw.dram([11, 128, 8, 256], BF16, f"bt_{f_}_g_{l}") for l in range(depth)]
            self.wt[f"{f_}_w_up"] = [fw.dram([11, 128, 8, 256], BF16, f"bt_{f_}_u_{l}") for l in range(depth)]
            self.wt[f"{f_}_w_down"] = [fw.dram([2, 2, 128, 11, 512], BF16, f"bt_{f_}_d_{l}") for l in range(depth)]
        self.wt["w_in_tm"] = [fw.dram([128, 8, 1280], BF16, f"bt_win_tm_{l}") for l in range(depth)]
        self.wt["w_in_fm"] = [fw.dram([13, 128, 8, 384], BF16, f"bt_win_fm_{l}") for l in range(depth)]

        def scratch(name, shape, dt):
            kind = "ExternalOutput" if name in self.debug else "Internal"
            return fw.dram(shape, dt, name, kind=kind)
        self.scratch = scratch
        NT = self.NT
        self.xres = scratch("xres", [T, D], F32)
        self.xcur = scratch("xcur", [T, D], F32)
        self.zu = scratch("zu", [T, W], BF16)
        self.qT = scratch("qT", [8, 64, T], BF16)
        self.kT = scratch("kT", [2, 64, T], BF16)
        self.vv = scratch("vv", [T, 128], BF16)
        self.zrwT = scratch("zrwT", [N_RW, T], BF16)
        self.sgT = scratch("sgT", [3 * D, T], BF16)
        self.ya = scratch("ya", [T, W], BF16)
        self.yb = scratch("yb", [T, W], BF16)
        self.yc = scratch("yc", [T, W], BF16)
        def regs(b):
            return [b.sub(f"{b.name}_t{i}") for i in range(NT)]
        self.r_xres, self.r_xcur, self.r_zu = regs(self.xres), regs(self.xcur), regs(self.zu)
        self.r_qT, self.r_kT, self.r_vv = regs(self.qT), regs(self.kT), regs(self.vv)
        self.r_zrwT, self.r_sgT = regs(self.zrwT), regs(self.sgT)
        self.r_ya, self.r_yb, self.r_yc = regs(self.ya), regs(self.yb), regs(self.yc)
        self.r_xin, self.r_yout = regs(self.x_in), regs(self.y_out)
        self.psA = Rot([fw.psum([128, 512], F32, f"psA{i}") for i in range(4)])
        self.psB = Rot([fw.psum([128, 512], F32, f"psB{i}") for i in range(2)])
        self.psT = Rot([fw.psum([128, 1024], BF16, f"psT{i}") for i in range(2)])
        self.dq = Rot(["sp"])
        self.ident = fw.sbuf([128, 128], BF16, "ident")
        self.identf = fw.sbuf([128, 128], F32, "identf")
        fw.memset(self.ident.v(), 1.0, eng="pool")
        idap = self.ident.v().ap
        fw.op("pool", lambda e: e.affine_select(out=idap, in_=idap, pattern=[[-1, 128]], compare_op=ALU.is_equal,
                                                fill=0.0, base=0, channel_multiplier=1),
              reads=[self.ident], writes=[self.ident])
        fw.copy(self.identf.v(), self.ident.v(), eng="pool")

    def wview(self, name, l):
        v = self.w[name][l]
        shp = WSPEC[name]
        if name == 'w_branch':
            return v.re("a b c -> (a b) c")
        if name in ('rw_w2', 'rw_a2'):
            return v.re("a b c -> (a b) c")
        return v

    def prep_weights(self, l, part="all"):
        fw = self.fw
        for name, (R, C) in BF_W.items():
            early = name.startswith("ffn1") or name == "w_in"
            if (part == "A" and not early) or (part == "B" and early):
                continue
            src = self.wview(name, l)
            if name.endswith("w_gate") or name.endswith("w_up"):
                dst = self.wt[name][l]
                for kc in range(8):
                    fw.dma("pool", V(dst, dst.h.ap()[:, :, kc, :].rearrange("j p f -> p j f")),
                           V(src.buf, src.ap[kc * 128:(kc + 1) * 128, :].rearrange("p (j f) -> p j f", f=256)))
                continue
            if name.endswith("w_down"):
                dst = self.wt[name][l]
                for j in range(NFF):
                    half, jl = j // 11, j % 11
                    fw.dma("pool", V(dst, dst.h.ap()[:, half, :, jl, :].rearrange("h p d -> p h d")),
                           V(src.buf, src.ap[j * 128:(j + 1) * 128, :].rearrange("p (h d) -> p h d", d=512)))
                continue
            if name == "w_in":
                dtm, dfm = self.wt["w_in_tm"][l], self.wt["w_in_fm"][l]
                for kc in range(8):
                    fw.dma("pool", V(dtm, dtm.h.ap()[:, kc, :]), src[kc * 128:(kc + 1) * 128, 0:1280])
                    fw.dma("pool", V(dfm, dfm.h.ap()[:, :, kc, :].rearrange("g p f -> p g f")),
                           V(src.buf, src.ap[kc * 128:(kc + 1) * 128, C_RW:C_RW + 13 * 384].rearrange("p (g f) -> p g f", f=384)))
                continue
            dst = self.wb[name][l]
            for r0 in range(0, R, 128):
                r1 = min(R, r0 + 128)
                for c0 in range(0, C, 2048):
                    c1 = min(C, c0 + 2048)
                    fw.dma("pool", dst[r0:r1, c0:c1], src[r0:r1, c0:c1])

    def sincos(self, ang, sin_out, cos_out, shape, tmp=None):
        fw = self.fw
        if tmp is None:
            ti = fw.sbuf(shape, I32)
            tf = fw.sbuf(shape, F32)
            tc = fw.sbuf(shape, F32)
        else:
            ti, tf, tc = tmp
        def vv(t):
            return t if isinstance(t, V) else t.v()
        ti, tf, tc = vv(ti), vv(tf), vv(tc)
        fw.ts(tf, ang, 1.0 / (2 * math.pi), None, op0=ALU.mult)
        fw.copy(ti, tf)
        fw.copy(tf, ti)
        fw.stt(ang, tf, -2.0 * math.pi, ang, op0=ALU.mult, op1=ALU.add)
        fw.ts(tc, ang, math.pi / 2, None, op0=ALU.add)
        fw.ts(tf, tc, math.pi, -2.0 * math.pi, op0=ALU.is_gt, op1=ALU.mult)
        fw.tt(tc, tc, tf, ALU.add)
        lim = 3.14159
        fw.ts(ang, ang, -lim, lim, op0=ALU.max, op1=ALU.min)
        fw.ts(tc, tc, -lim, lim, op0=ALU.max, op1=ALU.min)
        fw.act(sin_out, ang, AF.Sin)
        fw.act(cos_out, tc, AF.Sin)

    def load_bcast(self, dst, src_row, q="sp"):
        self.fw.dma(q, dst, src_row.pbc(128))

    def alloc_tok(self, wrot=3, nx=2):
        fw = self.fw
        B = {}
        B["xts"] = [fw.sbuf([128, 4, D], F32, f"xt{i}") for i in range(nx)]
        B["xt"] = B["xts"][0]
        B["gain"] = fw.sbuf([128, D], F32, "gain")
        B["gain2"] = fw.sbuf([128, D], F32, "gain2")
        B["h4"] = [fw.sbuf([128, D], BF16, f"h{i}") for i in range(4)]
        B["junk"] = fw.sbuf([128, D], BF16, "junk")
        B["ss"] = Rot([fw.sbuf([128, 2], F32, f"ss{i}") for i in range(4)])
        B["hT"] = fw.sbuf([128, 8, TT], BF16, "hT")
        B["aT"] = fw.sbuf([128, NFF, TT], BF16, "aT")
        B["sg"] = Rot([fw.sbuf([128, TT], F32, f"sg{i}") for i in range(2)])
        B["wg"] = Rot([fw.sbuf([128, 8, 256], BF16, f"wg{i}") for i in range(wrot)])
        B["wu"] = Rot([fw.sbuf([128, 8, 256], BF16, f"wu{i}") for i in range(wrot)])
        B["wd"] = Rot([fw.sbuf([128, 11, 512], BF16, f"wd{i}") for i in range(2)])
        return B

    def norm_T4(self, B, xt, gain, hT):
        fw = self.fw
        hs = []
        for s in range(4):
            ss = B["ss"].next()
            xsub = xt[:, s, :]
            fw.act(B["junk"].v(), xsub, AF.Square, accum_out=ss[:, 0:1])
            fw.act(ss[:, 1:2], ss[:, 0:1], AF.Sqrt, scale=1.0 / D, bias=self.eps_t[:, 0:1])
            fw.recip(ss[:, 1:2], ss[:, 1:2])
            h = B["h4"][s]
            fw.stt(h.v(), xsub, ss[:, 1:2], gain.v(), op0=ALU.mult, op1=ALU.mult)
            hs.append(h)
        for s in range(4):
            pt = self.psT.next()
            ptv = pt.v().re("p (k t) -> p k t", k=8)
            for kc in range(8):
                fw.transpose(ptv[:, kc, :], hs[s][:, kc * 128:(kc + 1) * 128], self.ident.v(), inc=(kc == 7))
            fw.copy(hT[:, :, s * 128:(s + 1) * 128], ptv, eng=("act" if s % 2 else "dve"))

    def ffn(self, B, l, which, gain, prefetch=None):
        fw = self.fw
        xt, hT, aT = B["xt"], B["hT"], B["aT"]
        wg_d = self.wt[f"{which}_w_gate"][l]
        wu_d = self.wt[f"{which}_w_up"][l]
        wd_d = self.wt[f"{which}_w_down"][l]
        self.norm_T4(B, xt, gain, hT)
        if prefetch is not None:
            prefetch()
        for jj in range(NFF // 2):
            wg, wu = B["wg"].next(), B["wu"].next()
            fw.dma("sp", wg.v(), wg_d[jj])
            fw.dma("sp", wu.v(), wu_d[jj])
            for jf in range(2):
                j = 2 * jj + jf
                gp, up = self.psA.next(), self.psA.next()
                for kc in range(8):
                    fw.matmul(gp.v(), wg[:, kc, jf * 128:(jf + 1) * 128], hT[:, kc, :], start=(kc == 0), stop=(kc == 7))
                for kc in range(8):
                    fw.matmul(up.v(), wu[:, kc, jf * 128:(jf + 1) * 128], hT[:, kc, :], start=(kc == 0), stop=(kc == 7))
                sg = B["sg"].next()
                fw.act(sg.v(), gp.v(), AF.Silu)
                fw.tt(aT[:, j, :], sg.v(), up.v(), ALU.mult)
        for dh in range(2):
            outs = [self.psA.next() for _ in range(4)]
            for half in range(2):
                wd = B["wd"].next()
                fw.dma("sp", wd.v(), wd_d[dh][half])
                for ts in range(4):
                    for jl in range(11):
                        j = half * 11 + jl
                        fw.matmul(outs[ts].v(), aT[:, j, ts * 128:(ts + 1) * 128], wd[:, jl, :],
                                  start=(j == 0), stop=(j == NFF - 1), inc=(jl == 10))
            for ts in range(4):
                xs = xt[:, ts, dh * 512:(dh + 1) * 512]
                fw.stt(xs, outs[ts].v(), 0.5, xs, op0=ALU.mult, op1=ALU.add)

    def load_x(self, B, src_regs, src, it):
        if it >= self.NT:
            return
        t0 = it * TT
        v = V(src_regs[it], src.h.ap()[t0:t0 + TT, :].rearrange("(s p) d -> p s d", p=128))
        self.fw.dma("sp", B["xts"][it % len(B["xts"])].v(), v)

    def store_x(self, B, dst_regs, dst, it, q="sp"):
        t0 = it * TT
        v = V(dst_regs[it], dst.h.ap()[t0:t0 + TT, :].rearrange("(s p) d -> p s d", p=128))
        self.fw.dma(q, v, B["xt"].v())

    def init_consts(self):
        fw = self.fw
        self.eps_t = fw.sbuf([128, 4], F32, "eps_t")
        fw.memset(self.eps_t[:, 0:1], RMS_EPS)
        fw.memset(self.eps_t[:, 1:2], 64e-5)
        fw.memset(self.eps_t[:, 2:3], 1.0)
        fw.memset(self.eps_t[:, 3:4], 0.0)
        NB = max(self.segs) // 128
        self.NBmax = NB
        self.cosT = fw.sbuf([128, NB, 8], F32, "cosT")
        self.sinT = fw.sbuf([128, NB, 8], F32, "sinT")
        with fw.phase():
            posi = fw.sbuf([128, NB], I32)
            posf = fw.sbuf([128, NB], F32)
            ang = fw.sbuf([128, NB, 8], F32)
            pap = posi.v().ap
            fw.op("pool", lambda e: e.iota(pap, pattern=[[128, NB]], base=0, channel_multiplier=1), writes=[posi])
            fw.copy(posf.v(), posi.v())
            for i in range(8):
                inv = ROPE_THETA ** (-(2.0 * i) / 16.0)
                fw.ts(ang[:, :, i], posf.v(), float(inv), None, op0=ALU.mult)
            self.sincos(ang.v(), self.sinT.v(), self.cosT.v(), [128, NB, 8])

    def qk_head_norm(self, B, ps_v, nh, gain_bc, blk, out_bf):
        fw = self.fw
        sq = B["qsq"]
        qn = B["qn"]
        st = B["qst"].next()
        fw.act(sq[:, 0:nh * 64], ps_v, AF.Square)
        fw.reduce(st[:, 0:nh], sq[:, 0:nh * 64].re("p (h d) -> p h d", d=64))
        fw.act(st[:, 8:8 + nh], st[:, 0:nh], AF.Sqrt, scale=1.0 / 64, bias=self.eps_t[:, 0:1])
        fw.recip(st[:, 8:8 + nh], st[:, 8:8 + nh])
        q3 = qn[:, 0:nh * 64].re("p (h d) -> p h d", d=64)
        fw.tt(q3, ps_v.re("p (h d) -> p h d", d=64), st[:, 8:8 + nh].re("p (h o) -> p h o", o=1).bc([128, nh, 64]), ALU.mult)
        fw.tt(q3, q3, gain_bc.v().re("p (o d) -> p o d", o=1).bc([128, nh, 64]), ALU.mult)
        c = self.cosT[:, blk:blk + 1, :].bc([128, nh, 8])
        s = self.sinT[:, blk:blk + 1, :].bc([128, nh, 8])
        t1, t2 = q3[:, :, 0:8], q3[:, :, 8:16]
        ra = B["rope"]
        a = ra[:, 0:nh * 8].re("p (h d) -> p h d", d=8)
        b = ra[:, 64:64 + nh * 8].re("p (h d) -> p h d", d=8)
        c2 = ra[:, 128:128 + nh * 8].re("p (h d) -> p h d", d=8)
        d2 = ra[:, 192:192 + nh * 8].re("p (h d) -> p h d", d=8)
        fw.tt(a, t1, c, ALU.mult)
        fw.tt(b, t2, s, ALU.mult)
        fw.tt(c2, t2, c, ALU.mult)
        fw.tt(d2, t1, s, ALU.mult)
        fw.copy(out_bf[:, :, 16:64], q3[:, :, 16:64], eng="pool")
        fw.tt(out_bf[:, :, 0:8], a, b, ALU.subtract)
        fw.tt(out_bf[:, :, 8:16], c2, d2, ALU.add)

    def phase_p1(self, l, src, src_regs):
        fw = self.fw
        with fw.phase():
            B = self.alloc_tok()
            B["wtm"] = fw.sbuf([128, 8, 1280], BF16, "wtm")
            B["wfm"] = Rot([fw.sbuf([128, 8, 384], BF16, f"wfm{i}") for i in range(3)])
            B["stg"] = Rot([fw.sbuf([128, 3, TT], BF16, f"stg{i}") for i in range(2)])
            B["zu"] = fw.sbuf([128, 4, W], BF16, "zu_sb")
            B["v"] = fw.sbuf([128, 4, 128], BF16, "v_sb")
            B["qsq"] = fw.sbuf([128, 512], F32, "qsq")
            B["qraw"] = Rot([fw.sbuf([128, 512], F32, f"qraw{i}") for i in range(2)])
            B["kraw"] = Rot([fw.sbuf([128, 128], F32, f"kraw{i}") for i in range(2)])
            B["qn"] = fw.sbuf([128, 512], F32, "qn")
            B["qst"] = Rot([fw.sbuf([128, 16], F32, f"qst{i}") for i in range(2)])
            B["rope"] = fw.sbuf([128, 256], F32, "rope")
            B["qb"] = Rot([fw.sbuf([128, 8, 64], BF16, f"qb{i}") for i in range(2)])
            B["kb"] = Rot([fw.sbuf([128, 2, 64], BF16, f"kb{i}") for i in range(2)])
            B["qT"] = fw.sbuf([64, 8, TT], BF16, "qT_sb")
            B["kT"] = fw.sbuf([64, 2, TT], BF16, "kT_sb")
            qg = fw.sbuf([128, 64], F32, "qg")
            kg = fw.sbuf([128, 64], F32, "kg")
            self.load_bcast(B["gain"].v(), self.w['ffn1_norm'][l])
            self.load_bcast(B["gain2"].v(), self.w['mix_norm'][l])
            self.load_bcast(qg.v(), self.w['q_norm'][l])
            self.load_bcast(kg.v(), self.w['k_norm'][l])
            fw.dma("sp", B["wtm"].v(), self.wt["w_in_tm"][l].v())
            hT = B["hT"]
            self.load_x(B, src_regs, src, 0)
            for it in range(self.NT):
                t0 = it * TT
                si, blk0 = self.tile_pos[it]
                B["xt"] = xt = B["xts"][it % 2]
                self.ffn(B, l, "ffn1", B["gain"], prefetch=lambda it=it: self.load_x(B, src_regs, src, it + 1))
                self.store_x(B, self.r_xres, self.xres, it)
                self.norm_T4(B, xt, B["gain2"], hT)
                for ts in range(4):
                    tsl = slice(ts * 128, (ts + 1) * 128)
                    pu, pq, pk = self.psA.next(), self.psA.next(), self.psA.next()
                    for (ps, c0, n) in ((pu, 0, 512), (pq, 512, 512), (pk, 1024, 256)):
                        for kc in range(8):
                            fw.matmul(ps[:, 0:n], hT[:, kc, tsl], B["wtm"][:, kc, c0:c0 + n], start=(kc == 0), stop=(kc == 7))
                    qraw, kraw = B["qraw"].next(), B["kraw"].next()
                    fw.copy(B["zu"][:, ts, :], pu.v(), eng="act")
                    fw.copy(qraw.v(), pq.v(), eng="act")
                    fw.copy(kraw.v(), pk[:, 0:128], eng="act")
                    fw.copy(B["v"][:, ts, :], pk[:, 128:256], eng="act")
                    qb, kb = B["qb"].next(), B["kb"].next()
                    self.qk_head_norm(B, qraw.v(), 8, qg, blk0 + ts, qb.v())
                    self.qk_head_norm(B, kraw.v(), 2, kg, blk0 + ts, kb.v())
                    pt = self.psT.next()
                    ptv = pt[0:64, :].re("p (h t) -> p h t", h=8)
                    for h in range(8):
                        fw.transpose(ptv[:, h, :], qb[:, h, :], self.ident.v(), inc=(h == 7))
                    fw.copy(B["qT"][:, :, tsl], ptv, eng="act")
                    pt = self.psT.next()
                    ptv = pt[0:64, 0:256].re("p (h t) -> p h t", h=2)
                    for h in range(2):
                        fw.transpose(ptv[:, h, :], kb[:, h, :], self.ident.v(), inc=(h == 1))
                    fw.copy(B["kT"][:, :, tsl], ptv, eng="act")
                fw.dma("pool", V(self.r_zu[it], self.zu.h.ap()[t0:t0 + TT, :].rearrange("(s p) c -> p s c", p=128)), B["zu"].v())
                fw.dma("pool", V(self.r_vv[it], self.vv.h.ap()[t0:t0 + TT, :].rearrange("(s p) c -> p s c", p=128)), B["v"].v())
                fw.dma("pool", V(self.r_qT[it], self.qT.h.ap()[:, :, t0:t0 + TT].rearrange("h d t -> d h t")), B["qT"].v())
                fw.dma("pool", V(self.r_kT[it], self.kT.h.ap()[:, :, t0:t0 + TT].rearrange("h d t -> d h t")), B["kT"].v())
                for g in range(13):
                    wfm = B["wfm"].next()
                    c0 = C_RW + g * 384
                    fw.dma("sp", wfm.v(), self.wt["w_in_fm"][l][g])
                    stg = B["stg"].next()
                    for ci in range(3):
                        ps = self.psA.next()
                        for kc in range(8):
                            fw.matmul(ps.v(), wfm[:, kc, ci * 128:(ci + 1) * 128], hT[:, kc, :], start=(kc == 0), stop=(kc == 7))
                        if g < 5:
                            fw.copy(stg[:, ci, :], ps.v(), eng=("act" if ci % 2 else "dve"))
                        else:
                            fw.act(stg[:, ci, :], ps.v(), AF.Sigmoid)
                    if g < 5:
                        dst = V(self.r_zrwT[it], self.zrwT.h.ap()[g * 384:(g + 1) * 384, t0:t0 + TT].rearrange("(c p) t -> p c t", p=128))
                    else:
                        r0 = (g - 5) * 384
                        dst = V(self.r_sgT[it], self.sgT.h.ap()[r0:r0 + 384, t0:t0 + TT].rearrange("(c p) t -> p c t", p=128))
                    fw.dma("pool", dst, stg.v())

    def phase_p3(self, l, dst, dst_regs):
        fw = self.fw
        with fw.phase():
            B = self.alloc_tok(wrot=3, nx=1)
            yin = Rot([fw.sbuf([128, 4, W], BF16, f"yin{i}") for i in range(2)])
            yT = [fw.sbuf([128, 4, TT], BF16, f"yT{i}") for i in range(3)]
            yaT = fw.sbuf([128, 4, TT], BF16, "yaT")
            sgl = Rot([fw.sbuf([128, 3, TT], BF16, f"sgl{i}") for i in range(2)])
            mT = fw.sbuf([128, 8, TT], BF16, "mT")
            wbr = fw.sbuf([128, 12, D], BF16, "wbr")
            wout = fw.sbuf([128, 8, D], BF16, "wout")
            wglu = fw.sbuf([128, 4, W], BF16, "wglu")
            gbf = Rot([fw.sbuf([128, TT], BF16, f"gbf{i}") for i in range(4)])
            self.load_bcast(B["gain"].v(), self.w['ffn2_norm'][l])
            fw.dma("sp", wbr.v(), self.wb['w_branch'][l].v().re("(k p) d -> p k d", p=128))
            fw.dma("sp", wout.v(), self.wb['w_out'][l].v().re("(k p) d -> p k d", p=128))
            fw.dma("sp", wglu.v(), self.wb['s5_w_glu'][l].v().re("(k p) d -> p k d", p=128))
            ysrc = [(self.ya, self.r_ya), (self.yb, self.r_yb), (self.yc, self.r_yc)]
            for it in range(self.NT):
                t0 = it * TT
                B["xt"] = xt = B["xts"][0]
                self.load_x(B, self.r_xres, self.xres, it)
                for i in range(3):
                    yb_, yr = ysrc[i]
                    yi = yin.next()
                    fw.dma("sp", yi.v(), V(yr[it], yb_.h.ap()[t0:t0 + TT, :].rearrange("(s p) c -> p s c", p=128)))
                    for s0 in (0, 2):
                        pt = self.psT.next()
                        ptv = pt.v().re("p (s k t) -> p s k t", s=2, k=4)
                        for s in range(2):
                            for kc in range(4):
                                fw.transpose(ptv[:, s, kc, :], yi[:, s0 + s, kc * 128:(kc + 1) * 128], self.ident.v(),
                                             inc=(s == 1 and kc == 3))
                        fw.copy(yT[i][:, :, s0 * 128:(s0 + 2) * 128].re("p k (s t) -> p s k t", s=2), ptv,
                                eng=("act" if s0 else "dve"))
                for f in range(4):
                    ps = self.psA.next()
                    for kc in range(4):
                        fw.matmul(ps.v(), wglu[:, kc, f * 128:(f + 1) * 128], yT[0][:, kc, :], start=(kc == 0), stop=(kc == 3))
                    sg = B["sg"].next()
                    fw.act(sg.v(), ps.v(), AF.Sigmoid)
                    fw.tt(yaT[:, f, :], sg.v(), yT[0][:, f, :], ALU.mult)
                rhs_i = [yaT, yT[1], yT[2]]
                for f in range(8):
                    sg3 = sgl.next()
                    src = V(self.r_sgT[it], self.sgT.h.ap()[:, t0:t0 + TT].rearrange("(i f p) t -> p i f t", i=3, p=128)[:, :, f, :])
                    fw.dma("sp", sg3.v(), src)
                    pss = []
                    for i in range(3):
                        ps = self.psA.next()
                        for kc in range(4):
                            fw.matmul(ps.v(), wbr[:, i * 4 + kc, f * 128:(f + 1) * 128], rhs_i[i][:, kc, :],
                                      start=(kc == 0), stop=(kc == 3))
                        pss.append(ps)
                    gs = []
                    for i in range(3):
                        g_ = gbf.next()
                        fw.tt(g_.v(), pss[i].v(), sg3[:, i, :], ALU.mult)
                        gs.append(g_)
                    pm = self.psB.next()
                    for i in range(3):
                        fw.matmul(pm.v(), self.ident.v(), gs[i].v(), start=(i == 0), stop=(i == 2))
                    fw.copy(mT[:, f, :], pm.v(), eng="act")
                for dh in range(2):
                    for ts in range(4):
                        ps = self.psA.next()
                        for kc in range(8):
                            fw.matmul(ps.v(), mT[:, kc, ts * 128:(ts + 1) * 128], wout[:, kc, dh * 512:(dh + 1) * 512],
                                      start=(kc == 0), stop=(kc == 7))
                        xs = xt[:, ts, dh * 512:(dh + 1) * 512]
                        fw.tt(xs, ps.v(), xs, ALU.add)
                self.ffn(B, l, "ffn2", B["gain"])
                self.store_x(B, dst_regs, dst, it, q="pool")

    def phase_attn(self, l):
        fw = self.fw
        Lmax = max(self.segs)
        nbmax = Lmax // 128
        with fw.phase():
            QT = fw.sbuf([64, 8, Lmax], BF16, "QT")
            KT = fw.sbuf([64, 2, Lmax], BF16, "KT")
            VA = fw.sbuf([128, nbmax, 2, 65], BF16, "VA")
            mprev = fw.sbuf([128, 128], BF16, "mprev")
            mnext = fw.sbuf([128, 128], BF16, "mnext")
            esink = fw.sbuf([128, 8], F32, "esink")
            pT = Rot([fw.sbuf([128, 512], BF16, f"pT{i}") for i in range(6)])
            den = Rot([fw.sbuf([128, 8], F32, f"den{i}") for i in range(2)])
            ystg = Rot([fw.sbuf([128, 4, W], BF16, f"ystg{i}") for i in range(2)])
            for m, cm, pat in ((mprev, 1, -1), (mnext, -1, 1)):
                fw.memset(m.v(), 1.0, eng="pool")
                map_ = m.v().ap
                fw.op("pool", lambda e, map_=map_, cm=cm, pat=pat: e.affine_select(
                    out=map_, in_=map_, pattern=[[pat, 128]], compare_op=ALU.is_ge, fill=0.0, base=0, channel_multiplier=cm),
                    reads=[m], writes=[m])
            self.load_bcast(esink.v(), self.w['attn_sink'][l])
            fw.act(esink.v(), esink.v(), AF.Exp)
            fw.memset(VA[:, :, :, 64:65], 1.0)
            for si, L in enumerate(self.segs):
                tok0 = self.seg_start[si]
                nb = L // 128
                it0 = tok0 // TT
                for k in range(L // TT):
                    it = it0 + k
                    a, b = k * TT, (k + 1) * TT
                    fw.dma("sp", QT[:, :, a:b], V(self.r_qT[it], self.qT.h.ap()[:, :, tok0 + a:tok0 + b].rearrange("h d t -> d h t")))
                    fw.dma("sp", KT[:, :, a:b], V(self.r_kT[it], self.kT.h.ap()[:, :, tok0 + a:tok0 + b].rearrange("h d t -> d h t")))
                    for hk in range(2):
                        fw.dma("sp", VA[:, k * 4:(k + 1) * 4, hk, 0:64],
                               V(self.r_vv[it], self.vv.h.ap()[tok0 + a:tok0 + b, hk * 64:(hk + 1) * 64].rearrange("(b p) d -> p b d", p=128)))
                for n in range(nb):
                    qs = slice(n * 128, (n + 1) * 128)
                    if n % 4 == 0:
                        yst = ystg.next()
                    kbs = [kb for kb in (n - 1, n, n + 1) if 0 <= kb < nb]
                    for hk in range(2):
                        pts = []
                        for kb in kbs:
                            ps = self.psA.next()
                            fw.matmul(ps.v(), KT[:, hk, kb * 128:(kb + 1) * 128], QT[:, hk * 4:(hk + 1) * 4, qs])
                            p = pT.next()
                            fw.act(p.v(), ps.v(), AF.Exp, scale=0.125)
                            if kb != n:
                                mk = mprev if kb < n else mnext
                                p3 = p.v().re("p (g q) -> p g q", g=4)
                                fw.tt(p3, p3, mk.v().re("p (o q) -> p o q", o=1).bc([128, 4, 128]), ALU.mult,
                                      eng=("pool" if kb < n else "dve"))
                            pts.append((kb, p))
                        po = self.psB.next()
                        pov = po[:, 0:260].re("p (g d) -> p g d", g=4)
                        for g in range(4):
                            for i, (kb, p) in enumerate(pts):
                                fw.matmul(pov[:, g, :], p[:, g * 128:(g + 1) * 128], VA[:, kb, hk, :],
                                          start=(i == 0), stop=(i == len(pts) - 1))
                        dn = den.next()
                        fw.tt(dn[:, 0:4], pov[:, :, 64], esink[:, hk * 4:(hk + 1) * 4], ALU.add)
                        fw.recip(dn[:, 4:8], dn[:, 0:4])
                        yo = yst[:, n % 4, hk * 256:(hk + 1) * 256].re("p (g d) -> p g d", g=4)
                        fw.tt(yo, pov[:, :, 0:64], dn[:, 4:8].re("p (g o) -> p g o", o=1).bc([128, 4, 64]), ALU.mult)
                    if n % 4 == 3:
                        it = it0 + n // 4
                        t0 = it * TT
                        fw.dma("pool", V(self.r_yb[it], self.yb.h.ap()[t0:t0 + TT, :].rearrange("(s p) c -> p s c", p=128)), yst.v())

    def s5_precompute(self, l):
        fw = self.fw
        P = {}
        P["Rr"] = fw.sbuf([128, 32, 128], BF16, "s5Rr")
        P["Ri"] = fw.sbuf([128, 32, 128], BF16, "s5Ri")
        P["T"] = fw.sbuf([128, 32, 128], BF16, "s5T")
        P["Or"] = fw.sbuf([128, 32, 128], BF16, "s5Or")
        P["Oni"] = fw.sbuf([128, 32, 128], BF16, "s5Oni")
        P["rho8"] = fw.sbuf([128, 32], F32, "s5rho8")
        P["th8"] = fw.sbuf([128, 32], F32, "s5th8")
        with fw.phase():
            def t32(name, shape=(128, 32)):
                return fw.sbuf(list(shape), F32, name)
            lamX = [t32("lamXr", (32, 128)), t32("lamXi", (32, 128))]
            lam = [t32("lam_r"), t32("lam_i")]
            for part, nm in enumerate(("s5_lam_re", "s5_lam_im")):
                for d in range(2):
                    fw.dma("sp", lamX[part][:, d * 64:(d + 1) * 64], self.w[nm][l][d])
                ps = self.psB.next()
                fw.transpose(ps[:, 0:32], lamX[part].v(), self.identf[0:32, 0:32])
                fw.copy(lam[part].v(), ps[:, 0:32])
            if getattr(self, 's5_stop', 99) <= 1:
                return P
            dt = t32("dt")
            for d in range(2):
                fw.dma("sp", dt[d * 64:(d + 1) * 64, :], self.w["s5_log_step"][l][d].pbc(64))
            fw.act(dt.v(), dt.v(), AF.Exp)
            lamr = lam[0]
            fw.ts(lamr.v(), lamr.v(), -1e-4, None, op0=ALU.min)
            lr, li = t32("lr"), t32("li")
            fw.tt(lr.v(), lamr.v(), dt.v(), ALU.mult)
            fw.tt(li.v(), lam[1].v(), dt.v(), ALU.mult)
            fw.act(P["rho8"].v(), lr.v(), AF.Exp, scale=8.0)
            fw.ts(P["th8"].v(), li.v(), 8.0, None, op0=ALU.mult)
            if getattr(self, 's5_stop', 99) <= 2:
                return P
            e1, sn, cs, ang = t32("e1"), t32("sn"), t32("cs"), t32("ang")
            fw.act(e1.v(), lr.v(), AF.Exp)
            fw.copy(ang.v(), li.v())
            self.sincos(ang.v(), sn.v(), cs.v(), [128, 32])
            am1, ai = t32("am1"), t32("ai")
            fw.tt(am1.v(), e1.v(), cs.v(), ALU.mult)
            fw.ts(am1.v(), am1.v(), -1.0, None, op0=ALU.add)
            fw.tt(ai.v(), e1.v(), sn.v(), ALU.mult)
            den, t1, t2, wr, wi = t32("den"), t32("t1"), t32("t2"), t32("wr"), t32("wi")
            fw.tt(den.v(), lamr.v(), lamr.v(), ALU.mult)
            fw.tt(t1.v(), lam[1].v(), lam[1].v(), ALU.mult)
            fw.tt(den.v(), den.v(), t1.v(), ALU.add)
            fw.recip(den.v(), den.v())
            fw.tt(t1.v(), am1.v(), lamr.v(), ALU.mult)
            fw.tt(t2.v(), ai.v(), lam[1].v(), ALU.mult)
            fw.tt(wr.v(), t1.v(), t2.v(), ALU.add)
            fw.tt(wr.v(), wr.v(), den.v(), ALU.mult)
            fw.tt(t1.v(), ai.v(), lamr.v(), ALU.mult)
            fw.tt(t2.v(), am1.v(), lam[1].v(), ALU.mult)
            fw.tt(wi.v(), t1.v(), t2.v(), ALU.subtract)
            fw.tt(wi.v(), wi.v(), den.v(), ALU.mult)
            if getattr(self, 's5_stop', 99) <= 3:
                return P
            Br, Bi = t32("Br", (128, 32, 16)), t32("Bi", (128, 32, 16))
            for d in range(2):
                fw.dma("sp", Br[d * 64:(d + 1) * 64], self.w["s5_b_re"][l][d].re("g p h -> p g h"))
                fw.dma("sp", Bi[d * 64:(d + 1) * 64], self.w["s5_b_im"][l][d].re("g p h -> p g h"))
            Bbr, Bbi = t32("Bbr", (128, 32, 16)), t32("Bbi", (128, 32, 16))
            pa, pb = t32("pa", (128, 32, 16)), t32("pb", (128, 32, 16))
            wrb = wr.v().re("p (g o) -> p g o", o=1).bc([128, 32, 16])
            wib = wi.v().re("p (g o) -> p g o", o=1).bc([128, 32, 16])
            fw.tt(pa.v(), Br.v(), wrb, ALU.mult)
            fw.tt(pb.v(), Bi.v(), wib, ALU.mult)
            fw.tt(Bbr.v(), pa.v(), pb.v(), ALU.subtract)
            fw.tt(pa.v(), Bi.v(), wrb, ALU.mult)
            fw.tt(pb.v(), Br.v(), wib, ALU.mult)
            fw.tt(Bbi.v(), pa.v(), pb.v(), ALU.add)
            if getattr(self, 's5_stop', 99) <= 4:
                return P
            Cr, Ci = t32("Cr", (128, 32, 16)), t32("Ci", (128, 32, 16))
            CX = Rot([t32(f"CX{i}", (128, 128)) for i in range(2)])
            for Ct, nm in ((Cr, "s5_c_re"), (Ci, "s5_c_im")):
                for gb in range(4):
                    X = CX.next()
                    for d in range(2):
                        fw.dma("sp", X[:, d * 64:(d + 1) * 64], self.w[nm][l][d][gb * 8:(gb + 1) * 8].re("g h p -> (g h) p"))
                    ps = self.psB.next()
                    fw.transpose(ps[:, 0:128], X.v(), self.identf.v())
                    fw.copy(Ct[:, gb * 8:(gb + 1) * 8, :], ps[:, 0:128].re("p (g h) -> p g h", h=16))
            if getattr(self, 's5_stop', 99) <= 5:
                return P
            ti = fw.sbuf([128, 8], I32, "ti")
            tf = t32("tf", (128, 8))
            tiap = ti.v().ap
            fw.op("pool", lambda e: e.iota(tiap, pattern=[[1, 8]], base=0, channel_multiplier=0), writes=[ti])
            fw.copy(tf.v(), ti.v())
            kR, kO2, kO = t32("kR", (128, 8)), t32("kO2", (128, 8)), t32("kO", (128, 8))
            fw.ts(kR[0:64], tf[0:64], -1.0, 7.0, op0=ALU.mult, op1=ALU.add)
            fw.copy(kR[64:128], tf[64:128])
            fw.ts(kO2[0:64], tf[0:64], -7.0, None, op0=ALU.add)
            fw.ts(kO2[64:128], tf[64:128], -1.0, None, op0=ALU.mult)
            fw.ts(kO[0:64], tf[0:64], 1.0, None, op0=ALU.add)
            fw.ts(kO[64:128], tf[64:128], -1.0, 8.0, op0=ALU.mult, op1=ALU.add)
            lrb = lr.v().re("p (g o) -> p g o", o=1).bc([128, 32, 8])
            lib = li.v().re("p (g o) -> p g o", o=1).bc([128, 32, 8])
            big = (128, 32, 8, 16)
            q1, q2 = t32("q1", big), t32("q2", big)
            RTb = [fw.sbuf([128, 32, 128], BF16, "RTbr"), fw.sbuf([128, 32, 128], BF16, "RTbi")]
            O2b = [fw.sbuf([128, 32, 128], BF16, "O2br"), fw.sbuf([128, 32, 128], BF16, "O2bni")]

            def powers(kt):
                mg, ph = t32("mg", (128, 32, 8)), t32("ph", (128, 32, 8))
                pr, pi = t32("pr", (128, 32, 8)), t32("pi", (128, 32, 8))
                kb = kt.v().re("p (o k) -> p o k", o=1).bc([128, 32, 8])
                fw.tt(mg.v(), lrb, kb, ALU.mult)
                fw.act(mg.v(), mg.v(), AF.Exp)
                fw.tt(ph.v(), lib, kb, ALU.mult)
                self.sincos(ph.v(), pi.v(), pr.v(), [128, 32, 8])
                fw.tt(pr.v(), pr.v(), mg.v(), ALU.mult)
                fw.tt(pi.v(), pi.v(), mg.v(), ALU.mult)
                return pr, pi

            def cprod(Xr, Xi, pr, pi, out_r, out_i, neg_i):
                xr = Xr.v().re("p g (o h) -> p g o h", o=1).bc(list(big))
                xi = Xi.v().re("p g (o h) -> p g o h", o=1).bc(list(big))
                prb = pr.v().re("p g (k o) -> p g k o", o=1).bc(list(big))
                pib = pi.v().re("p g (k o) -> p g k o", o=1).bc(list(big))
                o_r = out_r.v().re("p g (k h) -> p g k h", h=16)
                o_i = out_i.v().re("p g (k h) -> p g k h", h=16)
                fw.tt(q1.v(), xr, prb, ALU.mult)
                fw.tt(q2.v(), xi, pib, ALU.mult, eng="pool")
                fw.tt(o_r, q1.v(), q2.v(), ALU.subtract)
                fw.tt(q1.v(), xr, pib, ALU.mult)
                fw.tt(q2.v(), xi, prb, ALU.mult, eng="pool")
                if neg_i:
                    fw.stt(o_i, q1.v(), -1.0, q2.v(), op0=ALU.mult, op1=ALU.subtract)
                else:
                    fw.tt(o_i, q1.v(), q2.v(), ALU.add)

            if getattr(self, 's5_stop', 99) <= 6:
                return P
            pr, pi = powers(kR)
            cprod(Bbr, Bbi, pr, pi, RTb[0], RTb[1], False)
            pr, pi = powers(kO2)
            cprod(Cr, Ci, pr, pi, O2b[0], O2b[1], True)
            pr, pi = powers(kO)
            cprod(Cr, Ci, pr, pi, P["Or"], P["Oni"], True)
            if getattr(self, 's5_stop', 99) <= 7:
                return P
            for part, Rt in ((0, P["Rr"]), (1, P["Ri"])):
                for gb in range(4):
                    pt = self.psT.next()
                    ptv = pt.v().re("p (g q) -> p g q", g=8)
                    for gi in range(8):
                        fw.transpose(ptv[:, gi, :], RTb[part][:, gb * 8 + gi, :], self.ident.v(), inc=(gi == 7))
                    fw.copy(Rt[:, gb * 8:(gb + 1) * 8, :], ptv, eng="act")
            if getattr(self, 's5_stop', 99) <= 8:
                return P
            maskf, maskb = t32("maskf", (128, 8, 16)), t32("maskb", (128, 8, 16))
            for m, cm, st, base in ((maskf, -1, 16, 15), (maskb, 1, -16, 0)):
                fw.memset(m.v(), 1.0, eng="pool")
                map_ = m.v().ap
                fw.op("pool", lambda e, map_=map_, cm=cm, st=st, base=base: e.affine_select(
                    out=map_, in_=map_, pattern=[[st, 8], [0, 16]], compare_op=ALU.is_ge, fill=0.0, base=base,
                    channel_multiplier=cm), reads=[m], writes=[m])
            dcol = t32("dcol")
            for s8 in range(8):
                fw.dma("sp", dcol[s8 * 16:(s8 + 1) * 16, :], self.w["s5_d"][l].re("g h -> h g"), allow_slow_non_contiguous=True)
            if getattr(self, 's5_stop', 99) <= 9:
                return P
            ta, tb = Rot([t32(f"Tta{i}", (128, 128)) for i in range(2)]), Rot([t32(f"Ttb{i}", (128, 128)) for i in range(2)])
            mf2, mb2 = maskf.v().re("p t h -> p (t h)"), maskb.v().re("p t h -> p (t h)")
            O2m = {}
            for d in range(2):
                for part in range(2):
                    t = fw.sbuf([128, 32, 128], BF16, f"O2m{d}{part}")
                    ln, other = slice(d * 64, (d + 1) * 64), slice((1 - d) * 64, (2 - d) * 64)
                    fw.copy(t[ln], O2b[part][ln], eng="pool")
                    fw.memset(t[other], 0.0, eng="pool")
                    O2m[(d, part)] = t
            for g in range(32):
                pss = [self.psA.next(), self.psA.next()]
                for d in range(2):
                    o = pss[d][:, 0:128]
                    fw.matmul(o, RTb[0][:, g, :], O2m[(d, 0)][:, g, :], start=True, stop=False)
                    fw.matmul(o, RTb[1][:, g, :], O2m[(d, 1)][:, g, :], start=False, stop=True)
                a_, b_ = ta.next(), tb.next()
                fw.tt(a_.v(), pss[0][:, 0:128], mf2, ALU.mult)
                fw.tt(b_.v(), pss[1][:, 0:128], mb2, ALU.mult)
                fw.tt(a_.v(), a_.v(), b_.v(), ALU.add, eng="pool")
                fw.stt(P["T"][:, g, :], self.identf.v(), dcol[:, g:g + 1], a_.v(), op0=ALU.mult, op1=ALU.add)
        return P

    def gelu_tanh(self, out_bf, x, t1, shape2d):
        fw = self.fw
        fw.act(t1, x, AF.Square)
        fw.ts(t1, t1, 0.044715, 1.0, op0=ALU.mult, op1=ALU.add)
        fw.tt(t1, t1, x, ALU.mult)
        fw.act(t1, t1, AF.Sigmoid, scale=2.0 * math.sqrt(2.0 / math.pi))
        fw.tt(out_bf, x, t1, ALU.mult)

    def phase_s5(self, l):
        fw = self.fw
        G = 4
        nmax = max(self.segs) // 8
        with fw.phase():
            P = self.s5_precompute(l)
            A = [fw.sbuf([128, G * nmax], F32, f"s5A{i}") for i in range(6)]
            cosT = fw.sbuf([128, G * nmax], F32, "s5cos")
            sinT = fw.sbuf([128, G * nmax], F32, "s5sin")
            rho = fw.sbuf([128, G * nmax], F32, "s5rho")
            tint = fw.sbuf([128, G * nmax], I32, "s5ti")
            cidx_i = fw.sbuf([128, nmax], I32, "s5cidxi")
            cidx = fw.sbuf([128, nmax], F32, "s5cidx")
            U = fw.sbuf([128, G, nmax], BF16, "s5U")
            S = [fw.sbuf([128, G * (nmax + 2)], BF16, f"s5S{i}") for i in range(2)]
            X = fw.sbuf([128, G * nmax], F32, "s5X")
            G1 = fw.sbuf([128, G * nmax], F32, "s5G1")
            Ysb = fw.sbuf([128, G, nmax], BF16, "s5Ysb")
            Dt = Rot([fw.sbuf([128, 8, G * 16], BF16, f"s5Dt{i}") for i in range(2)])
            Yt = Rot([fw.sbuf([128, 8, G * 16], BF16, f"s5Yt{i}") for i in range(2)])
            Dg = Rot([fw.sbuf([128, G, 128], BF16, f"s5Dg{i}") for i in range(2)])
            cap = cidx_i.v().ap
            fw.op("pool", lambda e: e.iota(cap, pattern=[[1, nmax]], base=0, channel_multiplier=0), writes=[cidx_i])
            fw.copy(cidx.v(), cidx_i.v())
            for si, L in enumerate(self.segs):
                tok0 = self.seg_start[si]
                n = L // 8
                ncb = n // 128

                def v3(t, n=n):
                    return t.v()[:, 0:G * n].re("p (g c) -> p g c", c=n)

                def v2(t, n=n):
                    return t.v()[:, 0:G * n]
                Sv = [s_.v()[:, 0:G * (n + 2)].re("p (g c) -> p g c", c=n + 2) for s_ in S]
                for gb in range(32 // G):
                    g0 = gb * G
                    for cb in range(ncb):
                        dt_ = Dt.next()
                        t0 = tok0 + cb * 1024
                        its = sorted({(t0) // TT, (t0 + 512) // TT})
                        src = V(self.r_zu[its[0]], self.zu.h.ap()[t0:t0 + 1024, g0 * 16:(g0 + G) * 16].rearrange("(c s) f -> c s f", s=8))
                        fw.dma(self.dq.next(), dt_.v(), src, extra_r=[self.r_zu[i] for i in its[1:]])
                        dg = Dg.next()
                        fw.copy(dg.v().re("p g (s h) -> p g s h", h=16), dt_.v().re("p s (g h) -> p g s h", h=16), eng="pool")
                        pt = self.psT.next()
                        ptv = pt.v()[:, 0:G * 128].re("p (g q) -> p g q", g=G)
                        for gi in range(G):
                            fw.transpose(ptv[:, gi, :], dg[:, gi, :], self.ident.v(), inc=(gi == G - 1))
                        fw.copy(U[:, :, cb * 128:(cb + 1) * 128], ptv, eng="act")
                    for part, Rt in ((0, P["Rr"]), (1, P["Ri"])):
                        Ev = v3(A[part])
                        for gi in range(G):
                            ps = self.psA.next()
                            fw.matmul(ps[:, 0:n], Rt[:, g0 + gi, :], U[:, gi, 0:n])
                            fw.copy(Ev[0:64, gi, :], ps[0:64, 0:n], eng="act")
                            fw.copy(Ev[64:128, gi, :][:, ::-1], ps[64:128, 0:n], eng="act")
                    ph = v3(A[2])
                    fw.tt(ph, P["th8"][:, g0:g0 + G].re("p (g o) -> p g o", o=1).bc([128, G, n]),
                          cidx[:, 0:n].re("p (o c) -> p o c", o=1).bc([128, G, n]), ALU.mult)
                    self.sincos(v2(A[2]), v2(sinT), v2(cosT), None, tmp=(v2(tint), v2(A[3]), v2(A[4])))
                    fw.copy(v3(rho), P["rho8"][:, g0:g0 + G].re("p (g o) -> p g o", o=1).bc([128, G, n]), eng="act")
                    fw.memset(v3(rho)[:, :, 0:1], 0.0, eng="pool")
                    fw.tt(v2(A[3]), v2(A[1]), v2(sinT), ALU.mult, eng="pool")
                    fw.tt(v2(A[2]), v2(A[0]), v2(cosT), ALU.mult)
                    fw.tt(v2(A[4]), v2(A[2]), v2(A[3]), ALU.add)
                    fw.tt(v2(A[2]), v2(A[1]), v2(cosT), ALU.mult)
                    fw.tt(v2(A[3]), v2(A[0]), v2(sinT), ALU.mult)
                    fw.tt(v2(A[5]), v2(A[2]), v2(A[3]), ALU.subtract)
                    fw.scan(v2(A[0]), v2(rho), v2(A[4]))
                    fw.scan(v2(A[1]), v2(rho), v2(A[5]))
                    for x_ in Sv:
                        fw.memset(x_[0:64, :, 0:1], 0.0, eng="pool")
                        fw.memset(x_[64:128, :, n - 1:n], 0.0, eng="pool")
                    fw.tt(v2(A[3]), v2(A[1]), v2(sinT), ALU.mult, eng="pool")
                    fw.tt(v2(A[2]), v2(A[0]), v2(cosT), ALU.mult)
                    fw.tt(Sv[0][0:64, :, 1:n], v3(A[2])[0:64, :, 0:n - 1], v3(A[3])[0:64, :, 0:n - 1], ALU.subtract)
                    fw.tt(Sv[0][64:128, :, 0:n - 1][:, :, ::-1], v3(A[2])[64:128, :, 0:n - 1], v3(A[3])[64:128, :, 0:n - 1], ALU.subtract)
                    fw.tt(v2(A[5]), v2(A[1]), v2(cosT), ALU.mult, eng="pool")
                    fw.tt(v2(A[4]), v2(A[0]), v2(sinT), ALU.mult)
                    fw.tt(Sv[1][0:64, :, 1:n], v3(A[4])[0:64, :, 0:n - 1], v3(A[5])[0:64, :, 0:n - 1], ALU.add)
                    fw.tt(Sv[1][64:128, :, 0:n - 1][:, :, ::-1], v3(A[4])[64:128, :, 0:n - 1], v3(A[5])[64:128, :, 0:n - 1], ALU.add)
                    Xv = v3(X)
                    for gi in range(G):
                        g = g0 + gi
                        ps = self.psA.next()
                        o = ps[:, 0:n]
                        fw.matmul(o, P["T"][:, g, :], U[:, gi, 0:n], start=True, stop=False)
                        fw.matmul(o, P["Or"][:, g, :], Sv[0][:, gi, 0:n], start=False, stop=False)
                        fw.matmul(o, P["Oni"][:, g, :], Sv[1][:, gi, 0:n], start=False, stop=True)
                        fw.copy(Xv[:, gi, :], o, eng="act")
                    self.gelu_tanh(Ysb.v()[:, :, 0:n], Xv, v3(G1), None)
                    for cb in range(ncb):
                        pt = self.psT.next()
                        ptv = pt.v()[:, 0:G * 128].re("p (g q) -> p g q", g=G)
                        for gi in range(G):
                            fw.transpose(ptv[:, gi, :], Ysb[:, gi, cb * 128:(cb + 1) * 128], self.ident.v(), inc=(gi == G - 1))
                        yt = Yt.next()
                        fw.copy(yt.v().re("p t (g h) -> p g t h", g=G), ptv.re("p g (t h) -> p g t h", h=16), eng="act")
                        t0 = tok0 + cb * 1024
                        its = sorted({(t0) // TT, (t0 + 512) // TT})
                        dst = V(self.r_ya[its[0]], self.ya.h.ap()[t0:t0 + 1024, g0 * 16:(g0 + G) * 16].rearrange("(c t) f -> c t f", t=8))
                        fw.dma("pool", dst, yt.v(), extra_w=[self.r_ya[i] for i in its[1:]])

    RW_Q = ("r", "v", "nkk", "kd0", "kd1", "b0", "b1", "g", "bonus", "lw0", "lw1")

    def rw_alloc_scratch(self):
        if hasattr(self, "rws"):
            return
        T = self.T
        self.rws = {}
        self.rws_reg = {}
        for q in self.RW_Q:
            dt = F32 if q in ("lw0", "lw1", "bonus") else BF16
            b = self.scratch("rw_" + q, [T, W], dt)
            self.rws[q] = b
            self.rws_reg[q] = [b.sub(f"rw_{q}_c{i}") for i in range(T // 128)]
        b = self.scratch("rw_o", [T, W], F32)
        self.rws["o"] = b
        self.rws_reg["o"] = [b.sub(f"rw_o_c{i}") for i in range(T // 128)]

    def rw_stage1(self, l):
        fw = self.fw
        self.rw_alloc_scratch()
        with fw.phase():
            def cols(name, n):
                t = fw.sbuf([128, n], F32, name)
                return t
            mu0, mu1, c0 = cols("mu0", 15), cols("mu1", 15), cols("c0", 15)
            fw.dma("sp", mu0.v(), self.w["rw_mu"][l][0].re("(c p) -> p c", p=128), allow_slow_non_contiguous=True)
            fw.dma("sp", mu1.v(), self.w["rw_mu"][l][1].re("(c p) -> p c", p=128), allow_slow_non_contiguous=True)
            fw.tt(c0.v(), mu0.v(), mu1.v(), ALU.add)
            fw.ts(c0.v(), c0.v(), -1.0, 1.0, op0=ALU.mult, op1=ALU.add)

            def bc_row(name, src):
                t = fw.sbuf([128, W], F32, name)
                self.load_bcast(t.v(), src)
                return t
            w0 = [bc_row(f"w0_{d}", self.w["rw_w0"][l][d]) for d in range(2)]
            a0 = [bc_row(f"a0_{d}", self.w["rw_a0"][l][d]) for d in range(2)]
            kkw = bc_row("kkw", self.w["rw_k_k"][l])
            kaw = bc_row("kaw", self.w["rw_k_a"][l])
            rkw = bc_row("rkw", self.w["rw_r_k"][l].re("h n -> (h n)"))
            kam1 = fw.sbuf([128, W], F32, "kam1")
            fw.ts(kam1.v(), kaw.v(), -1.0, 1.0, op0=ALU.mult, op1=ALU.add)
            w2b = fw.sbuf([128, W], BF16, "w2b")
            a2b = fw.sbuf([128, W], BF16, "a2b")
            g2b = fw.sbuf([128, W], BF16, "g2b")
            fw.dma("sp", w2b.v(), self.wb["rw_w2"][l].v())
            fw.dma("sp", a2b.v(), self.wb["rw_a2"][l].v())
            fw.dma("sp", g2b.v(), self.wb["rw_g2"][l].v())
            zt = Rot([fw.sbuf([128, 15, TT + 2], BF16, f"zt{i}") for i in range(2)])
            zs = fw.sbuf([128, 15, TT], BF16, "zs")
            tmp = Rot([fw.sbuf([128, TT], F32, f"rwtmp{i}") for i in range(2)])
            tw = fw.sbuf([128, TT], BF16, "tw")
            sgl = fw.sbuf([128, TT], BF16, "sgl")
            rkv = Rot([fw.sbuf([128, 3 * W], BF16, f"rkv{i}") for i in range(2)])
            f32t = {n: Rot([fw.sbuf([128, W], F32, f"{n}{i}") for i in range(2)]) for n in
                    ("kk", "a0t", "a1t", "lw0t", "lw1t", "wk", "wk2", "bon")}
            bft = {n: Rot([fw.sbuf([128, W], BF16, f"{n}{i}") for i in range(2)]) for n in
                   ("nkk", "kd0", "kd1", "b0", "b1", "g")}
            st = Rot([fw.sbuf([128, 32], F32, f"rwst{i}") for i in range(2)])
            for it in range(self.NT):
                si, blk0 = self.tile_pos[it]
                tok0, L = self.seg_start[si], self.segs[si]
                t0 = it * TT
                z = zt.next()
                lo, hi = t0 - 1, t0 + TT + 1
                first, last = (t0 == tok0), (t0 + TT == tok0 + L)
                a_ = 1 if first else 0
                b_ = TT + 1 if last else TT + 2
                regs = [self.r_zrwT[it]]
                if not first:
                    regs.append(self.r_zrwT[it - 1])
                if not last:
                    regs.append(self.r_zrwT[it + 1])
                if first:
                    fw.memset(z[:, :, 0:1], 0.0)
                if last:
                    fw.memset(z[:, :, TT + 1:TT + 2], 0.0)
                src = V(regs[0], self.zrwT.h.ap()[:, lo + a_:lo + b_].rearrange("(c p) t -> p c t", p=128))
                fw.dma(self.dq.next(), z[:, :, a_:b_], src, extra_r=regs[1:])
                for c in range(15):
                    t_ = tmp.next()
                    fw.act(t_.v(), z[:, c, 1:TT + 1], AF.Copy, scale=c0[:, c:c + 1])
                    fw.stt(t_.v(), z[:, c, 0:TT], mu0[:, c:c + 1], t_.v(), op0=ALU.mult, op1=ALU.add)
                    fw.stt(zs[:, c, :], z[:, c, 2:TT + 2], mu1[:, c:c + 1], t_.v(), op0=ALU.mult, op1=ALU.add)
                fw.act(tw.v(), zs[:, 12, :], AF.Tanh)
                fw.act(sgl.v(), zs[:, 14, :], AF.Sigmoid)
                for ts in range(4):
                    tsl = slice(ts * 128, (ts + 1) * 128)
                    ci = (t0 + ts * 128) // 128
                    rk = rkv.next()
                    for half in range(2):
                        pt = self.psT.next()
                        ncs = 8 if half == 0 else 4
                        ptv = pt.v()[:, 0:ncs * 128].re("p (c f) -> p c f", c=ncs)
                        for cc in range(ncs):
                            fw.transpose(ptv[:, cc, :], zs[:, half * 8 + cc, tsl], self.ident.v(), inc=(cc == ncs - 1))
                        fw.copy(rk[:, half * 1024:half * 1024 + ncs * 128], pt.v()[:, 0:ncs * 128], eng="act")
                    r_, k_, v_ = rk[:, 0:W], rk[:, W:2 * W], rk[:, 2 * W:3 * W]

                    def store(q, src_v, ci=ci):
                        dst = V(self.rws_reg[q][ci], self.rws[q].h.ap()[ci * 128:(ci + 1) * 128, :])
                        fw.dma("sp", dst, src_v)
                    store("r", r_)
                    store("v", v_)
                    at, lwt = [], []
                    for d in range(2):
                        ln = slice(d * 64, (d + 1) * 64)
                        ps = self.psA.next()
                        fw.matmul(ps.v(), tw[ln, tsl], w2b[ln, :])
                        lw_ = f32t[f"lw{d}t"].next()
                        fw.tt(lw_.v(), ps.v(), w0[d].v(), ALU.add)
                        fw.act(lw_.v(), lw_.v(), AF.Sigmoid)
                        fw.act(lw_.v(), lw_.v(), AF.Copy, scale=-math.exp(-0.5))
                        store(f"lw{d}", lw_.v())
                        ps = self.psA.next()
                        fw.matmul(ps.v(), zs[ln, 13, tsl], a2b[ln, :])
                        a_t = f32t[f"a{d}t"].next()
                        fw.tt(a_t.v(), ps.v(), a0[d].v(), ALU.add)
                        fw.act(a_t.v(), a_t.v(), AF.Sigmoid)
                        at.append(a_t)
                    ps = self.psA.next()
                    fw.matmul(ps.v(), sgl[:, tsl], g2b.v())
                    g_ = bft["g"].next()
                    fw.copy(g_.v(), ps.v(), eng="act")
                    store("g", g_.v())
                    kk = f32t["kk"].next()
                    wk, wk2 = f32t["wk"].next(), f32t["wk2"].next()
                    s_ = st.next()
                    fw.tt(kk.v(), k_, kkw.v(), ALU.mult)
                    fw.act(wk.v(), kk.v(), AF.Square)
                    fw.reduce(s_[:, 0:8], wk.v().re("p (h n) -> p h n", n=64))
                    fw.act(s_[:, 8:16], s_[:, 0:8], AF.Sqrt)
                    fw.ts(s_[:, 8:16], s_[:, 8:16], 1e-12, None, op0=ALU.max)
                    fw.recip(s_[:, 8:16], s_[:, 8:16])
                    kk3 = kk.v().re("p (h n) -> p h n", n=64)
                    fw.tt(kk3, kk3, s_[:, 8:16].re("p (h o) -> p h o", o=1).bc([128, 8, 64]), ALU.mult)
                    nkk = bft["nkk"].next()
                    fw.act(nkk.v(), kk.v(), AF.Copy, scale=-1.0)
                    store("nkk", nkk.v())
                    for d in range(2):
                        fw.tt(wk.v(), at[d].v(), kaw.v(), ALU.mult)
                        fw.tt(wk.v(), wk.v(), kam1.v(), ALU.add)
                        kd = bft[f"kd{d}"].next()
                        fw.tt(kd.v(), wk.v(), k_, ALU.mult)
                        store(f"kd{d}", kd.v())
                        b_t = bft[f"b{d}"].next()
                        fw.tt(b_t.v(), kk.v(), at[d].v(), ALU.mult, eng="pool")
                        store(f"b{d}", b_t.v())
                    fw.tt(wk2.v(), r_, k_, ALU.mult, eng="pool")
                    fw.tt(wk2.v(), wk2.v(), rkw.v(), ALU.mult, eng="pool")
                    fw.reduce(s_[:, 16:24], wk2.v().re("p (h n) -> p h n", n=64))
                    bon = f32t["bon"].next()
                    fw.tt(bon.v().re("p (h n) -> p h n", n=64), v_.re("p (h n) -> p h n", n=64),
                          s_[:, 16:24].re("p (h o) -> p h o", o=1).bc([128, 8, 64]), ALU.mult)
                    store("bonus", bon.v())

    def rw_stage2(self, l):
        fw = self.fw
        with fw.phase():
            def f32(name, shape):
                return fw.sbuf(list(shape), F32, name)

            def bf(name, shape):
                return fw.sbuf(list(shape), BF16, name)
            masks = {}
            for nm, cm, st, base, cmp_ in (("LE", -1, 1, 0, ALU.is_ge), ("GE", 1, -1, 0, ALU.is_ge),
                                           ("LT", -1, 1, 0, ALU.is_gt), ("GT", 1, -1, 0, ALU.is_gt)):
                m = f32("mask" + nm, (128, 128))
                fw.memset(m.v(), 1.0, eng="pool")
                map_ = m.v().ap
                fw.op("pool", lambda e, map_=map_, cm=cm, st=st, base=base, cmp_=cmp_: e.affine_select(
                    out=map_, in_=map_, pattern=[[st, 128]], compare_op=cmp_, fill=0.0, base=base, channel_multiplier=cm),
                    reads=[m], writes=[m])
                masks[nm] = m
            ones2 = f32("ones2", (128, 2))
            fw.memset(ones2.v(), 1.0)
            lnw = f32("lnw", (128, W))
            lnb = f32("lnb", (128, W))
            self.load_bcast(lnw.v(), self.w["rw_ln_w"][l])
            self.load_bcast(lnb.v(), self.w["rw_ln_b"][l])
            inb = {q: SlotPool([bf(f"in_{q}{i}", (128, W)) for i in range(RW_WINDOW)]) for q in ("r", "v", "nkk", "kd", "b")}
            inlw = SlotPool([f32(f"in_lw{i}", (128, W)) for i in range(RW_WINDOW)])
            gam = SlotPool([f32(f"gam{i}", (128, W)) for i in range(RW_WINDOW)])
            ginv = SlotPool([f32(f"ginv{i}", (128, W)) for i in range(RW_WINDOW)])
            til = {q: SlotPool([bf(f"til_{q}{i}", (128, W)) for i in range(RW_WINDOW)]) for q in ("r", "k", "b", "a")}

            def padded(name, n):
                items = []
                for i in range(n):
                    ta_, tb_ = bf(f"{name}A{i}", (128, 4, 128)), bf(f"{name}B{i}", (128, 4, 128))
                    fw.memset(ta_[64:128], 0.0, eng="pool")
                    fw.memset(tb_[0:64], 0.0, eng="pool")
                    items.append((ta_, tb_))
                return SlotPool(items)
            tT = {q: padded(f"tT_{q}", RW_WINDOW) for q in ("r", "k", "b", "a")}
            WT = padded("WT", RW_WINDOW)
            gC = SlotPool([f32(f"gC{i}", (128, 8)) for i in range(RW_WINDOW)])
            sc = {q: SlotPool([bf(f"sc_{q}{i}", (128, 8, 128)) for i in range(RW_WINDOW)]) for q in ("Ak", "Arb", "Ark")}
            Mr_s = [Rot([bf(f"Mpow{w_}_{i}", (128, 8, 128)) for i in range(3)]) for w_ in range(RW_WINDOW)]
            Ar_s = [Rot([bf(f"Apow{w_}_{i}", (128, 8, 128)) for i in range(3)]) for w_ in range(RW_WINDOW)]
            Pb = SlotPool([bf(f"Pb{i}", (128, 8, 128)) for i in range(RW_WINDOW)])
            X0 = SlotPool([bf(f"X0{i}", (128, W)) for i in range(RW_WINDOW)])
            U0 = SlotPool([bf(f"U0{i}", (128, W)) for i in range(RW_WINDOW)])
            Ub = SlotPool([bf(f"Ub{i}", (128, W)) for i in range(RW_WINDOW)])
            ost = Rot([f32(f"ost{i}", (128, W)) for i in range(1)])
            oin = Rot([f32(f"oin{i}", (128, W)) for i in range(1)])
            bonin = Rot([f32(f"bonin{i}", (128, W)) for i in range(1)])
            gin = Rot([bf(f"gin{i}", (128, W)) for i in range(2)])
            fy = Rot([f32(f"fy{i}", (128, W)) for i in range(1)])
            fsq = Rot([f32(f"fsq{i}", (128, W)) for i in range(1)])
            fst = Rot([f32(f"fst{i}", (128, 32)) for i in range(2)])
            yout = Rot([bf(f"ycout{i}", (128, W)) for i in range(2)])
            tmpH = f32("tmpH", (128, 4, 128))
            chains = {}
            for si in range(len(self.segs)):
                for d in range(2):
                    Hf_ = f32(f"Hf_{si}_{d}", (128, 4, 128))
                    Hb_ = bf(f"Hb_{si}_{d}", (128, 4, 128))
                    fw.memset(Hf_.v(), 0.0)
                    fw.memset(Hb_.v(), 0.0)
                    chains[(si, d)] = (Hf_, Hb_)
            identf3 = self.identf.v().re("p (o t) -> p o t", o=1)
            def evac(dst, src):
                fw.copy(dst, src, eng="act")

            def chunk(si, d, c, first, slot):
                Mr, Ar = Mr_s[slot], Ar_s[slot]
                ci = self.seg_start[si] // 128 + c
                Hf_, Hb_ = chains[(si, d)]
                rows = slice(ci * 128, (ci + 1) * 128)

                def load(q, buf, qn=None):
                    qn = qn or q
                    fw.dma(self.dq.next(), buf.v(), V(self.rws_reg[qn][ci], self.rws[qn].h.ap()[rows, :]))
                r_, v_, nkk_, kd_, b_, lw_ = (inb["r"].next(), inb["v"].next(), inb["nkk"].next(), inb["kd"].next(),
                                             inb["b"].next(), inlw.next())
                load("r", r_)
                load("v", v_)
                load("nkk", nkk_)
                load("kd", kd_, f"kd{d}")
                load("b", b_, f"b{d}")
                load("lw", lw_, f"lw{d}")
                yield
                mLE, mLT, mGT = (masks["LE"], masks["LT"], masks["GT"]) if d == 0 else (masks["GE"], masks["GT"], masks["LT"])
                pc = self.psA.next()
                fw.matmul(pc.v(), mLE.v(), lw_.v())
                g_, gi_ = gam.next(), ginv.next()
                tr, tk, tb, ta = til["r"].next(), til["k"].next(), til["b"].next(), til["a"].next()
                fw.act(g_.v(), pc.v(), AF.Exp)
                fw.act(gi_.v(), lw_.v(), AF.Exp, scale=-1.0)
                fw.tt(tr.v(), r_.v(), g_.v(), ALU.mult)
                fw.tt(gi_.v(), gi_.v(), g_.v(), ALU.mult)
                fw.tt(ta.v(), nkk_.v(), gi_.v(), ALU.mult)
                fw.act(gi_.v(), pc.v(), AF.Exp, scale=-1.0)
                fw.tt(tb.v(), b_.v(), gi_.v(), ALU.mult)
                fw.tt(tk.v(), kd_.v(), gi_.v(), ALU.mult)
                pg = self.psB.next()
                for q in range(4):
                    fw.matmul(pg[:, 2 * q:2 * q + 2], lw_[:, q * 128:(q + 1) * 128], ones2.v())
                gc = gC.next()
                fw.act(gc.v(), pg[:, 0:8], AF.Exp)
                yield
                T_ = {}
                for q, src in (("a", ta), ("b", tb), ("k", tk), ("r", tr)):
                    pt = self.psT.next()
                    ptv = pt[:, 0:512].re("p (q t) -> p q t", q=4)
                    for pr_ in range(4):
                        fw.transpose(ptv[:, pr_, :], src[:, pr_ * 128:(pr_ + 1) * 128], self.ident.v(), inc=(pr_ == 3))
                    TA_, TB_ = tT[q].next()
                    fw.copy(TA_[0:64], ptv[0:64], eng="act")
                    fw.copy(TB_[64:128], ptv[64:128], eng="act")
                    T_[q] = (TA_, TB_)

                def Th(q, h):
                    return T_[q][h % 2][:, h // 2, :]
                yield
                M0, A0 = Mr.next(), Ar.next()
                AkT, ArbT, ArkT = sc["Ak"].next(), sc["Arb"].next(), sc["Ark"].next()
                for dst, lh, rh, mk in ((M0, "b", "a", mLT), (A0, "a", "b", mGT), (AkT, "k", "a", mLT),
                                        (ArbT, "b", "r", mLE), (ArkT, "k", "r", mLE)):
                    for hg in range(2):
                        ps = self.psA.next()
                        for hh in range(4):
                            h = hg * 4 + hh
                            fw.matmul(ps[:, hh * 128:(hh + 1) * 128], Th(lh, h), Th(rh, h))
                        fw.tt(dst[:, hg * 4:(hg + 1) * 4, :], ps.v().re("p (h t) -> p h t", h=4),
                              mk.v().re("p (o t) -> p o t", o=1).bc([128, 4, 128]), ALU.mult)
                yield
                pb = Pb.next()
                fw.tt(pb.v(), M0.v(), identf3.bc([128, 8, 128]), ALU.add)
                Mp, Ap = M0, A0
                for k in range(1, 7):
                    Mn = Mr.next() if k < 6 else None
                    An = Ar.next()
                    for hg in range(2):
                        hs = slice(hg * 4, (hg + 1) * 4)
                        if k < 6:
                            ps = self.psA.next()
                            for hh in range(4):
                                h = hg * 4 + hh
                                fw.matmul(ps[:, hh * 128:(hh + 1) * 128], Ap[:, h, :], Mp[:, h, :])
                            evac(Mn[:, hs, :], ps.v().re("p (h t) -> p h t", h=4))
                        ps = self.psA.next()
                        for hh in range(4):
                            h = hg * 4 + hh
                            fw.matmul(ps[:, hh * 128:(hh + 1) * 128], Mp[:, h, :], Ap[:, h, :])
                        evac(An[:, hs, :], ps.v().re("p (h t) -> p h t", h=4))
                    for hg in range(2):
                        hs = slice(hg * 4, (hg + 1) * 4)
                        ps = self.psA.next()
                        for hh in range(4):
                            h = hg * 4 + hh
                            fw.matmul(ps[:, hh * 128:(hh + 1) * 128], An[:, h, :], pb[:, h, :])
                        fw.tt(pb[:, hs, :], pb[:, hs, :], ps.v().re("p (h t) -> p h t", h=4), ALU.add)
                    Mp, Ap = Mn, An
                    yield
                WTA, WTB = WT.next()
                for par, wt_ in ((0, WTA), (1, WTB)):
                    ps = self.psA.next()
                    for q in range(4):
                        h = 2 * q + par
                        fw.matmul(ps[:, q * 128:(q + 1) * 128], ta[:, q * 128:(q + 1) * 128], pb[:, h, :])
                    ln = slice(par * 64, (par + 1) * 64)
                    evac(wt_[ln], ps[ln, :].re("p (q t) -> p q t", q=4))
                wts = (WTA, WTB)
                yield
                x0, u0 = X0.next(), U0.next()
                ps = self.psA.next()
                for h in range(8):
                    hs = slice(h * 64, (h + 1) * 64)
                    fw.matmul(ps[:, hs], AkT[:, h, :], v_[:, hs])
                evac(x0.v(), ps.v())
                ps = self.psA.next()
                for h in range(8):
                    hs = slice(h * 64, (h + 1) * 64)
                    fw.matmul(ps[:, hs], pb[:, h, :], x0[:, hs])
                evac(u0.v(), ps.v())
                yield
                pred = (si, d, c - 1) if d == 0 else (si, d, c + 1)
                if 0 <= pred[2] < nch[si]:
                    while pred not in done:
                        yield
                ub = Ub.next()
                ps = self.psA.next()
                for h in range(8):
                    hs = slice(h * 64, (h + 1) * 64)
                    q, par = h // 2, h % 2
                    fw.matmul(ps[:, hs], wts[par][:, q, :], Hb_[:, q, par * 64:(par + 1) * 64])
                fw.tt(ub.v(), ps.v(), u0.v(), ALU.add)
                if not first:
                    while (si, 1 - d, c) not in done:
                        yield
                else:
                    yield
                py = self.psA.next()
                for h in range(8):
                    hs = slice(h * 64, (h + 1) * 64)
                    q, par = h // 2, h % 2
                    fw.matmul(py[:, hs], Th("r", h), Hb_[:, q, par * 64:(par + 1) * 64], start=True, stop=False)
                    fw.matmul(py[:, hs], ArbT[:, h, :], ub[:, hs], start=False, stop=False)
                    fw.matmul(py[:, hs], ArkT[:, h, :], v_[:, hs], start=False, stop=True)
                ph = self.psB.next()
                for q in range(4):
                    qs = slice(q * 128, (q + 1) * 128)
                    fw.matmul(ph[:, qs], tb[:, qs], ub[:, qs], start=True, stop=False)
                    fw.matmul(ph[:, qs], tk[:, qs], v_[:, qs], start=False, stop=True)
                fw.tt(tmpH.v(), ph.v().re("p (q i) -> p q i", q=4), Hf_.v(), ALU.add)
                fw.tt(Hf_.v(), tmpH.v(), gc.v().re("p (q two) -> p q two", two=2)[:, :, 0:1].bc([128, 4, 128]), ALU.mult)
                fw.copy(Hb_.v(), Hf_.v(), eng="act")
                oreg = self.rws_reg["o"][ci]
                odr = V(oreg, self.rws["o"].h.ap()[rows, :])
                if first:
                    o_ = ost.next()
                    fw.copy(o_.v(), py.v(), eng="act")
                    fw.dma("pool", odr, o_.v())
                    done.add((si, d, c))
                    return
                oi, bi, gi2 = oin.next(), bonin.next(), gin.next()
                fw.dma(self.dq.next(), oi.v(), odr)
                fw.dma(self.dq.next(), bi.v(), V(self.rws_reg["bonus"][ci], self.rws["bonus"].h.ap()[rows, :]))
                fw.dma(self.dq.next(), gi2.v(), V(self.rws_reg["g"][ci], self.rws["g"].h.ap()[rows, :]))
                y, sq, st = fy.next(), fsq.next(), fst.next()
                fw.tt(y.v(), py.v(), oi.v(), ALU.add)
                y3 = y.v().re("p (h n) -> p h n", n=64)
                fw.reduce(st[:, 0:8], y3)
                fw.ts(st[:, 0:8], st[:, 0:8], 1.0 / 64, None, op0=ALU.mult)
                fw.tt(y3, y3, st[:, 0:8].re("p (h o) -> p h o", o=1).bc([128, 8, 64]), ALU.subtract)
                fw.act(sq.v(), y.v(), AF.Square)
                fw.reduce(st[:, 8:16], sq.v().re("p (h n) -> p h n", n=64))
                fw.act(st[:, 16:24], st[:, 8:16], AF.Sqrt, scale=1.0 / 64, bias=self.eps_t[:, 1:2])
                fw.recip(st[:, 16:24], st[:, 16:24])
                fw.tt(y3, y3, st[:, 16:24].re("p (h o) -> p h o", o=1).bc([128, 8, 64]), ALU.mult)
                fw.tt(y.v(), y.v(), lnw.v(), ALU.mult)
                fw.tt(y.v(), y.v(), lnb.v(), ALU.add, eng="pool")
                fw.tt(y.v(), y.v(), bi.v(), ALU.add, eng="pool")
                yo = yout.next()
                fw.tt(yo.v(), y.v(), gi2.v(), ALU.mult)
                it = (ci * 128) // TT
                fw.dma("pool", V(self.r_yc[it], self.yc.h.ap()[rows, :]), yo.v())
                done.add((si, d, c))

            done = set()
            nch = [L // 128 for L in self.segs]
            todo = []
            for step in range(max(nch)):
                for si, n in enumerate(nch):
                    if step >= n:
                        continue
                    first = step < n - 1 - step
                    todo.append((si, 0, step, first))
                    todo.append((si, 1, n - 1 - step, first))
            active = {}
            pos = 0
            while pos < len(todo) or active:
                for slot in range(RW_WINDOW):
                    if slot not in active and pos < len(todo):
                        active[slot] = chunk(*todo[pos], slot)
                        pos += 1
                for slot in sorted(active):
                    SlotPool.cur[0] = slot
                    try:
                        next(active[slot])
                    except StopIteration:
                        del active[slot]

    def phase_rw(self, l):
        self.rw_stage1(l)
        self.rw_stage2(l)

    def zero_fill(self, dst, regs):
        fw = self.fw
        with fw.phase():
            z = fw.sbuf([128, 4, W], BF16, "zfill")
            fw.memset(z.v(), 0.0)
            for it in range(self.NT):
                t0 = it * TT
                fw.dma("pool", V(regs[it], dst.h.ap()[t0:t0 + TT, :].rearrange("(s p) c -> p s c", p=128)), z.v())

    def build_all(self):
        self.prep_weights(0, "A")
        self.prep_weights(0, "B")
        for l in range(self.depth):
            last = (l == self.depth - 1)
            if l == 0:
                self.phase_p1(l, self.x_in, self.r_xin)
            else:
                self.phase_p1(l, self.xcur, self.r_xcur)
            if not last:
                self.prep_weights(l + 1)
            self.phase_attn(l)
            if hasattr(self, "phase_s5"):
                self.phase_s5(l)
            else:
                self.zero_fill(self.ya, self.r_ya)
            if hasattr(self, "phase_rw"):
                self.phase_rw(l)
            else:
                self.zero_fill(self.yc, self.r_yc)
            if last:
                self.phase_p3(l, self.y_out, self.r_yout)
            else:
                self.phase_p3(l, self.xcur, self.r_xcur)

    def finish(self):
        fw = self.fw
        fw.barrier()
        fw.emit()


def host_inputs_for_core(inputs, x_core):
    m = {"x": np.ascontiguousarray(x_core, dtype=np.float32)}
    for n in WSPEC:
        m[n] = np.ascontiguousarray(np.asarray(inputs[n]), dtype=np.float32)
    return m


N_CORES = 8
SEGS = [2048, 2048, 4096]
_PROG = None


def _get_prog():
    global _PROG
    if _PROG is None:
        P = Prog(SEGS, depth=DEPTH)
        P.init_consts()
        P.build_all()
        P.finish()
        _PROG = P
    return _PROG


def kernel(**inputs):
    xp = np.asarray(inputs["x_prompt"], dtype=np.float32)
    xs = np.asarray(inputs["x_sample"], dtype=np.float32)
    P = _get_prog()
    in_maps = []
    for c in range(N_CORES):
        xc = np.concatenate([xp[2 * c], xp[2 * c + 1], xs[c]], axis=0)
        in_maps.append(host_inputs_for_core(inputs, xc))
    res = run_bass_kernel_spmd(P.nc, in_maps, core_ids=list(range(N_CORES)))
    yp = np.empty_like(xp)
    ys = np.empty_like(xs)
    for c in range(N_CORES):
        y = np.asarray(res.results[c]["y"], dtype=np.float32)
        yp[2 * c] = y[0:2048]
        yp[2 * c + 1] = y[2048:4096]
        ys[c] = y[4096:8192]
    return (yp, ys)
```
